# Optimizing a Trainium2 kernel written in Bass

```python
import jax, jax.numpy as jnp
from jax import lax
import numpy as np

D_MODEL = 1024
BATCH = 4
SEQ = 4096
DEPTH = 2

PLE_DIM = 256
R_HEADS = 4
R_DK = 64
R_DV = 128
R_CHUNK = 128
ROPE_BASE = 10000.0
M_HEADS = 4
M_DK = 64
M_DV = 128
M_CHUNK = 64
M_CONV = 4
AB_MIX = R_HEADS * R_DV + M_HEADS * M_DV
AB_SIZES = (R_HEADS * R_DK, R_HEADS * R_DK, R_HEADS * R_DV, R_HEADS * R_DV,
            M_HEADS * M_DK, M_HEADS * M_DK, M_HEADS * M_DV, M_HEADS * M_DV,
            M_HEADS, M_HEADS)
AB_COLS = sum(AB_SIZES)
N_HEADS = 16
N_KV_GROUPS = 2
HEAD_DIM = 64
CMP_BLOCK = 32
CMP_STRIDE = 16
CMP_HIDDEN = 256
SLC_BLOCK = 64
N_SELECT = 16
WINDOW = 512
Q_BLOCK = 128
NSA_SIZES = (N_HEADS * HEAD_DIM,) + (N_KV_GROUPS * HEAD_DIM,) * 6 + (N_HEADS * 3,)
NSA_COLS = sum(NSA_SIZES)
D_FF = 2816
FFN_CONV = 3

NEG = -1e30
BIG = 1e30
EPS = 1e-6

kernel_name = 'hybrid_retnet_mlstm_nsa_convffn'


def rms_norm(x, g):
    xf = x.astype(jnp.float32)
    y = xf * lax.rsqrt(jnp.mean(xf * xf, axis=-1, keepdims=True) + EPS)
    return (y * g.astype(jnp.float32)).astype(x.dtype)


def split_cols(z, sizes):
    return jnp.split(z, np.cumsum(sizes)[:-1].tolist(), axis=-1)


def causal_dwconv(x, w, b):
    K = w.shape[0]
    T = x.shape[1]
    xp = jnp.pad(x, ((0, 0), (K - 1, 0), (0, 0)))
    y = b
    for j in range(K):
        y = y + xp[:, j:j + T] * w[j]
    return y


def to_chunks(x, L):
    B, T = x.shape[:2]
    x = x.reshape((B, T // L, L) + x.shape[2:])
    return jnp.moveaxis(x, (1, 2), (0, 3))


def from_chunks(y):
    y = jnp.moveaxis(y, (0, 3), (1, 2))
    return y.reshape((y.shape[0], y.shape[1] * y.shape[2]) + y.shape[3:])


def rotary(x, pos):
    half = x.shape[-1] // 2
    inv = ROPE_BASE ** (-jnp.arange(half, dtype=jnp.float32) / half)
    ang = pos[:, None] * inv[None, :]
    cos = jnp.cos(ang)[:, None, :]
    sin = jnp.sin(ang)[:, None, :]
    x1, x2 = x[..., :half], x[..., half:]
    return jnp.concatenate([x1 * cos - x2 * sin, x1 * sin + x2 * cos], axis=-1)


def retention_chunkwise(q, k, v):
    B, T, H, DK = q.shape
    DV = v.shape[-1]
    L = R_CHUNK
    log_g = jnp.log1p(-jnp.exp2(-5.0 - jnp.arange(H, dtype=jnp.float32)))
    idx = jnp.arange(L, dtype=jnp.float32)
    diff = idx[:, None] - idx[None, :]
    causal = diff >= 0
    dmask = jnp.where(causal, jnp.exp(jnp.where(causal, diff, 0.0)[None] * log_g[:, None, None]), 0.0)
    q_dec = jnp.exp((idx + 1.0)[None, :] * log_g[:, None])
    k_dec = jnp.exp((L - 1.0 - idx)[None, :] * log_g[:, None])
    c_dec = jnp.exp(L * log_g)

    def step(R, xs):
        qc, kc, vc = xs
        s = jnp.einsum('bhld,bhmd->bhlm', qc, kc) * dmask
        o = (jnp.einsum('bhlm,bhme->bhle', s, vc)
             + jnp.einsum('bhld,bhde->bhle', qc, R) * q_dec[..., None])
        R = c_dec[:, None, None] * R + jnp.einsum('bhmd,bhme->bhde', kc * k_dec[..., None], vc)
        return R, o

    R0 = jnp.zeros((B, H, DK, DV), jnp.float32)
    _, o = lax.scan(step, R0, (to_chunks(q, L), to_chunks(k, L), to_chunks(v, L)))
    return from_chunks(o)


def mlstm_chunkwise(q, k, v, ig, lf):
    B, T, H, DK = q.shape
    DV = v.shape[-1]
    L = M_CHUNK
    causal = jnp.tril(jnp.ones((L, L), dtype=bool))

    def step(carry, xs):
        C, n, m = carry
        qc, kc, vc, ic, fc = xs
        b = jnp.cumsum(fc, axis=-1)
        dlog = jnp.where(causal, b[..., :, None] - b[..., None, :] + ic[..., None, :], NEG)
        inter = b + m[..., None]
        m_t = jnp.maximum(inter, jnp.max(dlog, axis=-1))
        s = jnp.einsum('bhld,bhsd->bhls', qc, kc) * jnp.exp(dlog - m_t[..., None])
        w_inter = jnp.exp(inter - m_t)
        num = (jnp.einsum('bhls,bhse->bhle', s, vc)
               + w_inter[..., None] * jnp.einsum('bhld,bhde->bhle', qc, C))
        den = jnp.sum(s, axis=-1) + w_inter * jnp.einsum('bhld,bhd->bhl', qc, n)
        h = num / jnp.maximum(jnp.abs(den), jnp.exp(-m_t))[..., None]
        b_last = b[..., -1]
        wlog = b_last[..., None] - b + ic
        m_new = jnp.maximum(b_last + m, jnp.max(wlog, axis=-1))
        decay = jnp.exp(b_last + m - m_new)
        wk = kc * jnp.exp(wlog - m_new[..., None])[..., None]
        C = decay[..., None, None] * C + jnp.einsum('bhsd,bhse->bhde', wk, vc)
        n = decay[..., None] * n + jnp.sum(wk, axis=2)
        return (C, n, m_new), h

    init = (jnp.zeros((B, H, DK, DV), jnp.float32), jnp.zeros((B, H, DK), jnp.float32),
            jnp.zeros((B, H), jnp.float32))
    xs = (to_chunks(q, L), to_chunks(k, L), to_chunks(v, L), to_chunks(ig, L), to_chunks(lf, L))
    _, h = lax.scan(step, init, xs)
    return from_chunks(h)


def ab_mixer(h, w_in, conv_w, conv_b, ret_g, ig_b, fg_b, m_g, w_out):
    B, T, _ = h.shape
    z = (h @ w_in).astype(jnp.float32)
    rq, rk, rv, rg, mq, mk, mv, mo, mi, mf = split_cols(z, AB_SIZES)
    pos = jnp.arange(T, dtype=jnp.float32)
    rq = rotary(rq.reshape(B, T, R_HEADS, R_DK), pos)
    rk = rotary(rk.reshape(B, T, R_HEADS, R_DK), pos) * (R_DK ** -0.5)
    ret = retention_chunkwise(rq, rk, rv.reshape(B, T, R_HEADS, R_DV))
    ret = rms_norm(ret, ret_g.reshape(R_HEADS, R_DV)).reshape(B, T, -1) * jax.nn.silu(rg)
    mqk = jax.nn.silu(causal_dwconv(jnp.concatenate([mq, mk], axis=-1), conv_w, conv_b))
    mq, mk = jnp.split(mqk, 2, axis=-1)
    hm = mlstm_chunkwise(mq.reshape(B, T, M_HEADS, M_DK) * (M_DK ** -0.5),
                         mk.reshape(B, T, M_HEADS, M_DK),
                         mv.reshape(B, T, M_HEADS, M_DV),
                         mi + ig_b,
                         jax.nn.log_sigmoid(mf + fg_b))
    mlstm = jax.nn.sigmoid(mo) * rms_norm(hm, m_g.reshape(M_HEADS, M_DV)).reshape(B, T, -1)
    y = jnp.concatenate([ret, mlstm], axis=-1) @ w_out
    return y.astype(h.dtype)


def nsa_mixer(h, w_in, q_g, k_g, pos_k, pos_v, w1k, w2k, w1v, w2v, gate_b, w_out):
    B, T, _ = h.shape
    G, HG, HD = N_KV_GROUPS, N_HEADS // N_KV_GROUPS, HEAD_DIM
    z = (h @ w_in).astype(jnp.float32)
    q, kc, vc, ks, vs, kw, vw, gt = split_cols(z, NSA_SIZES)
    q = rms_norm(q.reshape(B, T, G, HG, HD), q_g).transpose(0, 2, 3, 1, 4)

    def kv(t):
        return t.reshape(B, T, G, HD)

    nc = (T - CMP_BLOCK) // CMP_STRIDE + 1
    c_start = jnp.arange(nc) * CMP_STRIDE
    cidx = c_start[:, None] + jnp.arange(CMP_BLOCK)[None, :]

    def compress(t, pos, w1, w2):
        blk = kv(t)[:, cidx] + pos[:, None, :]
        blk = blk.transpose(0, 3, 1, 2, 4).reshape(B, G, nc, CMP_BLOCK * HD)
        return jax.nn.gelu(blk @ w1) @ w2

    kc = rms_norm(compress(kc, pos_k, w1k, w2k), k_g[0])
    vc = compress(vc, pos_v, w1v, w2v)
    ks = rms_norm(kv(ks), k_g[1]).transpose(0, 2, 1, 3)
    vs = kv(vs).transpose(0, 2, 1, 3)
    kw = rms_norm(kv(kw), k_g[2]).transpose(0, 2, 1, 3)
    vw = kv(vw).transpose(0, 2, 1, 3)
    ns = T // SLC_BLOCK
    n_sel = min(N_SELECT, ns)
    ks_blk = ks.reshape(B, G, ns, SLC_BLOCK, HD)
    vs_blk = vs.reshape(B, G, ns, SLC_BLOCK, HD)
    pad = ((0, 0), (0, 0), (WINDOW, 0), (0, 0))
    kw_pad = jnp.pad(kw, pad)
    vw_pad = jnp.pad(vw, pad)
    gates = jax.nn.sigmoid(gt.reshape(B, T, G, HG, 3) + gate_b.reshape(G, HG, 3)).transpose(0, 2, 3, 1, 4)
    cmp_end = c_start + CMP_BLOCK - 1
    sj = jnp.arange(ns)
    overlap = ((c_start[:, None] < (sj[None, :] + 1) * SLC_BLOCK)
               & (c_start[:, None] + CMP_BLOCK > sj[None, :] * SLC_BLOCK)).astype(jnp.float32)
    scale = HD ** -0.5
    bi = jnp.arange(B)[:, None, None, None]
    gi = jnp.arange(G)[None, :, None, None]

    def query_block(qb):
        t0 = qb * Q_BLOCK
        tpos = t0 + jnp.arange(Q_BLOCK)
        qq = lax.dynamic_slice_in_dim(q, t0, Q_BLOCK, axis=3)
        s = jnp.einsum('bghqd,bgnd->bghqn', qq, kc) * scale
        cvalid = cmp_end[None, :] <= tpos[:, None]
        p_c = jnp.where(cvalid, jax.nn.softmax(jnp.where(cvalid, s, NEG), axis=-1), 0.0)
        o_cmp = jnp.einsum('bghqn,bgnd->bghqd', p_c, vc)
        imp = jnp.einsum('bghqn,ns->bgqs', p_c, overlap)
        cur = tpos // SLC_BLOCK
        forced = (sj[None, :] == 0) | (sj[None, :] == cur[:, None]) | (sj[None, :] == cur[:, None] - 1)
        bvalid = sj[None, :] <= cur[:, None]
        score = jnp.where(forced, BIG, jnp.where(bvalid, imp, NEG))
        _, sel = lax.top_k(score, n_sel)
        kg = ks_blk[bi, gi, sel]
        vg = vs_blk[bi, gi, sel]
        kpos = sel[..., None] * SLC_BLOCK + jnp.arange(SLC_BLOCK)
        smask = (kpos <= tpos[:, None, None])[:, :, None]
        s = jnp.einsum('bghqd,bgqnkd->bghqnk', qq, kg) * scale
        s = jnp.where(smask, s, NEG).reshape(B, G, HG, Q_BLOCK, n_sel * SLC_BLOCK)
        p_s = jax.nn.softmax(s, axis=-1).reshape(B, G, HG, Q_BLOCK, n_sel, SLC_BLOCK)
        o_slc = jnp.einsum('bghqnk,bgqnkd->bghqd', p_s, vg)
        kwb = lax.dynamic_slice_in_dim(kw_pad, t0, WINDOW + Q_BLOCK, axis=2)
        vwb = lax.dynamic_slice_in_dim(vw_pad, t0, WINDOW + Q_BLOCK, axis=2)
        wpos = t0 - WINDOW + jnp.arange(WINDOW + Q_BLOCK)
        wmask = ((wpos[None, :] <= tpos[:, None]) & (wpos[None, :] > tpos[:, None] - WINDOW)
                 & (wpos[None, :] >= 0))
        s = jnp.einsum('bghqd,bgkd->bghqk', qq, kwb) * scale
        p_w = jax.nn.softmax(jnp.where(wmask, s, NEG), axis=-1)
        o_win = jnp.einsum('bghqk,bgkd->bghqd', p_w, vwb)
        g = lax.dynamic_slice_in_dim(gates, t0, Q_BLOCK, axis=3)
        return g[..., 0:1] * o_cmp + g[..., 1:2] * o_slc + g[..., 2:3] * o_win

    o = lax.map(query_block, jnp.arange(T // Q_BLOCK))
    o = o.transpose(1, 0, 4, 2, 3, 5).reshape(B, T, N_HEADS * HD)
    return (o @ w_out).astype(h.dtype)


def conv_ffn(h, w_up, conv_w, conv_b, w_down):
    a, b = jnp.split(h @ w_up, 2, axis=-1)
    a = causal_dwconv(a, conv_w, conv_b)
    return (jax.nn.gelu(a) * b) @ w_down


def setup_inputs(seed: int = 0) -> dict:
    key = jax.random.key(seed)
    keys = iter(jax.random.split(key, 64))
    f32 = jnp.float32
    NE = (DEPTH + 1) // 2
    NO = DEPTH // 2

    def nrm(shape, scale):
        return scale * jax.random.normal(next(keys), shape, f32)

    def gain(shape):
        return 1.0 + nrm(shape, 0.02)

    return {
        'x': nrm((BATCH, SEQ, D_MODEL), 1.0),
        'p': nrm((DEPTH, BATCH, SEQ, PLE_DIM), 1.0),
        'ab_norm_g': gain((NE, D_MODEL)),
        'ab_w_in': nrm((NE, D_MODEL, AB_COLS), D_MODEL ** -0.5),
        'ab_conv_w': nrm((NE, M_CONV, 2 * M_HEADS * M_DK), M_CONV ** -0.5),
        'ab_conv_b': nrm((NE, 2 * M_HEADS * M_DK), 0.02),
        'ab_ret_norm_g': gain((NE, R_HEADS * R_DV)),
        'ab_ig_b': nrm((NE, M_HEADS), 0.1),
        'ab_fg_b': jnp.linspace(3.0, 6.0, M_HEADS, dtype=f32)[None, :] + nrm((NE, M_HEADS), 0.1),
        'ab_m_norm_g': gain((NE, M_HEADS * M_DV)),
        'ab_w_out': nrm((NE, AB_MIX, D_MODEL), AB_MIX ** -0.5),
        'nsa_norm_g': gain((NO, D_MODEL)),
        'nsa_w_in': nrm((NO, D_MODEL, NSA_COLS), D_MODEL ** -0.5),
        'nsa_q_norm_g': gain((NO, HEAD_DIM)),
        'nsa_k_norm_g': gain((NO, 3, HEAD_DIM)),
        'nsa_cmp_pos_k': nrm((NO, CMP_BLOCK, HEAD_DIM), 0.1),
        'nsa_cmp_pos_v': nrm((NO, CMP_BLOCK, HEAD_DIM), 0.1),
        'nsa_cmp_w1k': nrm((NO, CMP_BLOCK * HEAD_DIM, CMP_HIDDEN), (CMP_BLOCK * HEAD_DIM) ** -0.5),
        'nsa_cmp_w2k': nrm((NO, CMP_HIDDEN, HEAD_DIM), CMP_HIDDEN ** -0.5),
        'nsa_cmp_w1v': nrm((NO, CMP_BLOCK * HEAD_DIM, CMP_HIDDEN), (CMP_BLOCK * HEAD_DIM) ** -0.5),
        'nsa_cmp_w2v': nrm((NO, CMP_HIDDEN, HEAD_DIM), CMP_HIDDEN ** -0.5),
        'nsa_gate_b': nrm((NO, N_HEADS * 3), 0.1),
        'nsa_w_out': nrm((NO, N_HEADS * HEAD_DIM, D_MODEL), (N_HEADS * HEAD_DIM) ** -0.5),
        'ffn_norm_g': gain((DEPTH, D_MODEL)),
        'ffn_w_up': nrm((DEPTH, D_MODEL, 2 * D_FF), D_MODEL ** -0.5),
        'ffn_conv_w': nrm((DEPTH, FFN_CONV, D_FF), FFN_CONV ** -0.5),
        'ffn_conv_b': nrm((DEPTH, D_FF), 0.02),
        'ffn_w_down': nrm((DEPTH, D_FF, D_MODEL), D_FF ** -0.5),
        'ple_w': nrm((DEPTH, PLE_DIM, D_MODEL), PLE_DIM ** -0.5),
        'ple_norm_g': gain((DEPTH, D_MODEL)),
        'ple_gate_norm_g': gain((DEPTH, D_MODEL)),
        'ple_w_gate': nrm((DEPTH, D_MODEL, D_MODEL), D_MODEL ** -0.5),
    }


def reference(x, p, ab_norm_g, ab_w_in, ab_conv_w, ab_conv_b, ab_ret_norm_g, ab_ig_b, ab_fg_b,
              ab_m_norm_g, ab_w_out, nsa_norm_g, nsa_w_in, nsa_q_norm_g, nsa_k_norm_g,
              nsa_cmp_pos_k, nsa_cmp_pos_v, nsa_cmp_w1k, nsa_cmp_w2k, nsa_cmp_w1v, nsa_cmp_w2v,
              nsa_gate_b, nsa_w_out, ffn_norm_g, ffn_w_up, ffn_conv_w, ffn_conv_b, ffn_w_down,
              ple_w, ple_norm_g, ple_gate_norm_g, ple_w_gate):
    h = x
    for i in range(DEPTH):
        j = i // 2
        if i % 2 == 0:
            h = h + ab_mixer(rms_norm(h, ab_norm_g[j]), ab_w_in[j], ab_conv_w[j], ab_conv_b[j],
                             ab_ret_norm_g[j], ab_ig_b[j], ab_fg_b[j], ab_m_norm_g[j], ab_w_out[j])
        else:
            h = h + nsa_mixer(rms_norm(h, nsa_norm_g[j]), nsa_w_in[j], nsa_q_norm_g[j], nsa_k_norm_g[j],
                              nsa_cmp_pos_k[j], nsa_cmp_pos_v[j], nsa_cmp_w1k[j], nsa_cmp_w2k[j],
                              nsa_cmp_w1v[j], nsa_cmp_w2v[j], nsa_gate_b[j], nsa_w_out[j])
        h = h + conv_ffn(rms_norm(h, ffn_norm_g[i]), ffn_w_up[i], ffn_conv_w[i], ffn_conv_b[i], ffn_w_down[i])
        e = rms_norm(p[i] @ ple_w[i], ple_norm_g[i])
        gate = jax.nn.sigmoid(rms_norm(h, ple_gate_norm_g[i]) @ ple_w_gate[i])
        h = h + gate * e
    return h
```

```python
import math
from contextlib import ExitStack
import numpy as np
import ml_dtypes
import concourse.bass as bass
import concourse.mybir as mybir
from concourse.bass_utils import run_bass_kernel_spmd

F32 = mybir.dt.float32
BF16 = mybir.dt.bfloat16
AF = mybir.ActivationFunctionType
ALU = mybir.AluOpType
AX = mybir.AxisListType
NPBF = ml_dtypes.bfloat16

N_DMA_SEMS = 24


class T:
    __slots__ = ("ap", "w", "r", "name")

    def __init__(self, ap, name=""):
        self.ap = ap
        self.w = None
        self.r = []
        self.name = name

    def __getitem__(self, idx):
        return self.ap[idx]

    @property
    def dep(self):
        return self


class V:
    def __init__(self, parent, ap):
        self.parent = parent; self.ap = ap

    def __getitem__(self, idx):
        return self.ap[idx]

    @property
    def dep(self):
        return self.parent


class Op:
    __slots__ = ("eng", "idx", "fn", "deps", "is_dma", "dsem", "dval", "needed", "cnt")

    def __init__(self, eng, idx, fn, is_dma):
        self.eng = eng; self.idx = idx; self.fn = fn; self.deps = []
        self.is_dma = is_dma; self.dsem = None; self.dval = 0; self.needed = False; self.cnt = 0


class K:
    ENGS = ("pe", "dve", "act", "pool", "sp")

    def __init__(self, nc, es):
        self.nc = nc
        self.es = es
        self.ops = {e: [] for e in self.ENGS}
        self.dma_rr = 0
        self.dma_last = [None] * N_DMA_SEMS
        self.dma_cnt = [0] * N_DMA_SEMS
        self.out_dmas = []
        self.n_t = 0

    def arena_init(self, kib):
        self.arena_n = kib * 512
        self.arena = self.es.enter_context(self.nc.sbuf_tensor("arena", [128, self.arena_n], BF16))
        self.a_off = 0
        self.a_base = 0

    def sb(self, shape, dt, name=None):
        P = shape[0]
        n = 1
        for d in shape[1:]:
            n *= d
        sz = n if dt == BF16 else 2 * n
        sz = (sz + 63) // 64 * 64
        assert self.a_off + sz <= self.arena_n, ("SBUF arena overflow", name, self.a_off, sz, self.arena_n)
        ap = self.arena[0:P, self.a_off:self.a_off + (n if dt == BF16 else 2 * n)]
        self.a_off += sz
        if dt != BF16:
            ap = ap.bitcast(dt)
        if len(shape) == 3:
            ap = ap.rearrange("p (a b) -> p a b", a=shape[1])
        return T(ap, name or "")

    def persist(self):
        self.a_base = self.a_off

    def phase_reset(self):
        self.barrier()
        self.a_off = self.a_base

    def barrier(self):
        lasts = []
        for e in self.ENGS:
            for op in reversed(self.ops[e]):
                if not op.is_dma and op.fn is not None:
                    lasts.append(op); break
        lasts += [d for d in self.dma_last if d is not None]
        for e in self.ENGS:
            b = Op(e, len(self.ops[e]), None, False)
            b.deps = [d for d in lasts if not (d.eng == e and not d.is_dma)]
            self.ops[e].append(b)

    def ps(self, shape, dt=F32, name=None):
        self.n_t += 1
        name = name or f"p{self.n_t}"
        t = self.es.enter_context(self.nc.psum_tensor(name, list(shape), dt))
        return T(t, name)

    def view(self, ap, name=""):
        return T(ap, name)

    def _rec(self, eng, fn, reads, writes, is_dma=False):
        lst = self.ops[eng]
        op = Op(eng, len(lst), fn, is_dma)
        deps = []
        reads = [t.dep for t in reads]; writes = [t.dep for t in writes]
        for t in reads:
            if t.w is not None:
                deps.append((t.w, "raw"))
        for t in writes:
            if t.w is not None:
                deps.append((t.w, "waw"))
            for r in t.r:
                deps.append((r, "war"))
        for d, kind in deps:
            if d is op:
                continue
            if (not d.is_dma) and d.eng == eng and not is_dma:
                if eng == "pe" or kind != "raw":
                    continue
            op.deps.append(d)
        if is_dma:
            i = self.dma_rr; self.dma_rr = (self.dma_rr + 1) % N_DMA_SEMS
            prev = self.dma_last[i]
            if prev is not None:
                op.deps.append(prev)
            self.dma_cnt[i] += 1
            op.dsem = i; op.dval = 16 * self.dma_cnt[i]
            self.dma_last[i] = op
        for t in reads:
            t.r.append(op)
        for t in writes:
            t.w = op; t.r = []
        lst.append(op)
        return op

    def op(self, eng, fn, reads=(), writes=()):
        return self._rec(eng, fn, reads, writes)

    def dma(self, eng, out_ap, in_ap, reads=(), writes=(), is_out=False, **kw):
        def fn(e):
            return e.dma_start(out=out_ap, in_=in_ap, **kw)
        op = self._rec(eng, fn, reads, writes, is_dma=True)
        if is_out:
            self.out_dmas.append(op)
        return op

    def emit(self):
        nc = self.nc
        fin = Op("sp", len(self.ops["sp"]), None, False)
        fin.deps = list(self.out_dmas)
        self.ops["sp"].append(fin)
        for e in self.ENGS:
            for op in self.ops[e]:
                for d in op.deps:
                    d.needed = True
        for e in self.ENGS:
            c = 0
            for op in self.ops[e]:
                if op.needed and not op.is_dma:
                    c += 1
                op.cnt = c
        sems = {e: self.es.enter_context(nc.semaphore(f"s_{e}")) for e in self.ENGS}
        dsems = [self.es.enter_context(nc.semaphore(f"s_dma{i}")) for i in range(N_DMA_SEMS)]
        block = self.es.enter_context(nc.Block())
        ops = self.ops

        def run(ename, eng):
            waited = {}
            for op in ops[ename]:
                need = {}
                for d in op.deps:
                    if d.is_dma:
                        key = ("d", d.dsem); val = d.dval
                    else:
                        key = ("e", d.eng); val = d.cnt
                    if waited.get(key, 0) >= val:
                        continue
                    if need.get(key, 0) < val:
                        need[key] = val
                for key, val in need.items():
                    sem = dsems[key[1]] if key[0] == "d" else sems[key[1]]
                    eng.wait_ge(sem, val)
                    waited[key] = val
                if op.fn is None:
                    continue
                ins = op.fn(eng)
                if op.is_dma:
                    ins.then_inc(dsems[op.dsem], 16)
                elif op.needed:
                    ins.then_inc(sems[ename], 1)

        @block.tensor
        def _(e):
            run("pe", e)

        @block.vector
        def _(e):
            run("dve", e)

        @block.scalar
        def _(e):
            run("act", e)

        @block.gpsimd
        def _(e):
            run("pool", e)

        @block.sync
        def _(e):
            run("sp", e)


    def do(self, eng, method, reads, writes, *a, **kw):
        return self.op(eng, lambda e: getattr(e, method)(*a, **kw), reads, writes)

    def mm(self, out_t, out_ap, l_t, l_ap, r_t, r_ap, start, stop, skip=False):
        return self.op("pe", lambda e: e.matmul(out_ap, lhsT=l_ap, rhs=r_ap, start=start, stop=stop, skip_group_check=skip),
                       reads=[l_t, r_t], writes=[out_t])

    def banks_init(self, n=8):
        if not hasattr(self, "allbanks"):
            self.allbanks = [self.ps([128, 512], F32, name=f"bank{i}") for i in range(8)]
        self.banks = self.allbanks[:n]
        self.bank_i = 0

    def bank_bf(self, i):
        b = self.allbanks[i]
        return V(b, b.ap[:, :].bitcast(BF16))

    def bank(self):
        b = self.banks[self.bank_i]
        self.bank_i = (self.bank_i + 1) % len(self.banks)
        return b


class Rot:
    def __init__(self, k, n, shape, dt, name):
        self.t = [k.sb(shape, dt, name=f"{name}{i}") for i in range(n)]
        self.i = 0

    def get(self):
        t = self.t[self.i]
        self.i = (self.i + 1) % len(self.t)
        return t


def slices512(w0, w1):
    out = []
    s = w0
    while s < w1:
        e = min(s + 512, w1)
        out.append((s, e)); s = e
    return out


EPS = 1e-6

T_SEQ = 4096
NCH = 32
DFF = 2816
NJ = DFF // 128


def run_spmd(nc, in_maps):
    res = run_bass_kernel_spmd(nc, in_maps, core_ids=list(range(len(in_maps))))
    return res.results


def wblocks(W, CB):
    Kd, C = W.shape
    KC = Kd // 128
    NB = (C + CB - 1) // CB
    if NB * CB != C:
        Wp = np.zeros((Kd, NB * CB), W.dtype); Wp[:, :C] = W; W = Wp
    return np.ascontiguousarray(W.reshape(KC, 128, NB, CB).transpose(1, 2, 0, 3)).reshape(128, NB * KC * CB)


def fm(a):
    N, Fd = a.shape
    return np.ascontiguousarray(a.T.reshape(Fd // 128, 128, N).transpose(1, 0, 2))


def unfm(a):
    P, KC, N = a.shape
    return np.ascontiguousarray(a.transpose(2, 1, 0)).reshape(N, KC * P)


def gain_fm(g):
    return np.ascontiguousarray(g.reshape(-1, 128).T.astype(np.float32))


class WTab:
    def __init__(self):
        self.parts = []; self.off = 0; self.tab = {}

    def add(self, name, blk, X):
        self.tab[name] = (self.off, X, blk.shape[1] // X)
        self.parts.append(blk); self.off += blk.shape[1]

    def flat(self):
        M = (self.off + 4095) // 4096 * 4096
        out = np.zeros((128, M), np.float32)
        o = 0
        for p_ in self.parts:
            out[:, o:o + p_.shape[1]] = p_; o += p_.shape[1]
        return out, M


def wlayout(inp):
    wt = WTab()
    w_in = inp["ab_w_in"][0]
    rq, rk, rv, rg, mq, mk, mv, mo, mi, mf = np.split(w_in, np.cumsum([256, 256, 512, 512, 256, 256, 512, 512, 4])[:9].tolist() + [3076], axis=1) \
        if False else (w_in[:, 0:256], w_in[:, 256:512], w_in[:, 512:1024], w_in[:, 1024:1536], w_in[:, 1536:1792], w_in[:, 1792:2048],
                       w_in[:, 2048:2560], w_in[:, 2560:3072], w_in[:, 3072:3076], w_in[:, 3076:3080])
    def sw(w):
        return w.reshape(1024, 4, 2, 32)[:, :, ::-1, :].reshape(1024, 256)
    a_fm = np.concatenate([rq, sw(rq), rk, sw(rk), mq, mk], 1)
    wt.add("a_fm", wblocks(a_fm, 128), 8 * 128)
    a_tm = np.concatenate([rv, rg, mv, mo, mi, mf], 1)
    wt.add("a_tm", wblocks(a_tm, 512), 8 * 512)
    n_in = inp["nsa_w_in"][0]
    n_tm = np.concatenate([n_in[:, 0:1024], n_in[:, 1280:1840]], 1)
    wt.add("n_tm", wblocks(n_tm, 512), 8 * 512)
    wt.add("n_fm", wblocks(n_in[:, 1024:1280], 128), 8 * 128)
    for l, mixn in ((0, "ab_w_out"), (1, "nsa_w_out")):
        wt.add("w_out%d" % l, wblocks(inp[mixn][0], 128), 8 * 128)
        up = inp["ffn_w_up"][l]
        a = up[:, :DFF].reshape(1024, NJ, 128); b = up[:, DFF:].reshape(1024, NJ, 128)
        wt.add("w_up%d" % l, wblocks(np.concatenate([a, b], 2).reshape(1024, NJ * 256), 256), 8 * 256)
        wt.add("w_dn%d" % l, wblocks(inp["ffn_w_down"][l], 128), NJ * 128)
        wt.add("w_ple%d" % l, wblocks(inp["ple_w"][l], 128), 2 * 128)
        wt.add("w_gate%d" % l, wblocks(inp["ple_w_gate"][l], 128), 8 * 128)
    for nm in ("nsa_cmp_w1k", "nsa_cmp_w1v"):
        wt.add(nm, wblocks(inp[nm][0], 256), 16 * 256)
    for nm in ("nsa_cmp_w2k", "nsa_cmp_w2v"):
        wt.add(nm, wblocks(inp[nm][0], 64), 2 * 64)
    return wt


class TPCtx:
    def __init__(self, k):
        self.k = k
        self.ones = k.sb([128, 128], BF16, name="ones_bf")
        self.eps = k.sb([128, 1], F32, name="eps")
        k.do("pool", "memset", [], [self.ones], self.ones[:, :], 1.0)
        k.do("pool", "memset", [], [self.eps], self.eps[:, :], EPS)

    def scratch(self):
        k = self.k
        self.sq = Rot(k, 2, [128, 512], BF16, "sq")
        self.rstd = Rot(k, 2, [128, 512], F32, "rstd")
        self.f32tmp = Rot(k, 2, [128, 512], F32, "f32tmp")

    def rmsnorm(self, src, KC, w0, w1, gain, dst, d0):
        k = self.k
        inv = 1.0 / (KC * 128)
        for (s0, s1) in slices512(w0, w1):
            n = s1 - s0
            bk = k.bank()
            for kc in range(KC):
                sq = self.sq.get()
                k.do("act", "activation", [src], [sq], out=sq[:, :n], in_=src[:, kc, s0:s1], func=AF.Square)
                k.mm(bk, bk[:, :n], self.ones, self.ones[:, :], sq, sq[:, :n], kc == 0, kc == KC - 1)
            r = self.rstd.get()
            k.do("act", "activation", [bk, self.eps], [r], out=r[:, :n], in_=bk[:, :n], func=AF.Ln, bias=self.eps[:, 0:1], scale=inv)
            k.do("act", "activation", [r], [r], out=r[:, :n], in_=r[:, :n], func=AF.Exp, scale=-0.5)
            for kc in range(KC):
                k.do("dve", "scalar_tensor_tensor", [src, gain, r], [dst],
                     out=dst[:, kc, d0 + s0 - w0:d0 + s1 - w0], in0=src[:, kc, s0:s1], scalar=gain[:, kc:kc + 1],
                     in1=r[:, :n], op0=ALU.mult, op1=ALU.mult)


def tokmajor_proj(k, xn, KC, NT, x0, wsrc, CB, C, z_ap, wrot, zrot):
    NB = (C + CB - 1) // CB
    for b in range(NB):
        wt = wrot.get()
        k.dma("sp", wt[:, :KC * CB], wsrc(b), writes=[wt])
        cw = min(CB, C - b * CB)
        for ti in range(NT // 128):
            bk = k.bank()
            for kc in range(KC):
                k.mm(bk, bk[:, :cw], xn, xn[:, kc, x0 + ti * 128:x0 + (ti + 1) * 128], wt, wt[:, kc * CB:kc * CB + cw], kc == 0, kc == KC - 1)
            zt = zrot.get()
            if ti % 2 == 0:
                k.do("act", "copy", [bk], [zt], out=zt[:, :cw], in_=bk[:, :cw])
            else:
                k.do("dve", "tensor_copy", [bk], [zt], out=zt[:, :cw], in_=bk[:, :cw])
            k.dma("pool", z_ap[ti * 128:(ti + 1) * 128, b * CB:b * CB + cw], zt[:, :cw], reads=[zt])


def featmajor_proj(k, xn, KC, NT, x0, wsrc, NCK, dst_fn, wrot, zrot):
    for c in range(NCK):
        wt = wrot.get()
        k.dma("sp", wt[:, :KC * 128], wsrc(c), writes=[wt])
        for (s0, s1) in slices512(0, NT):
            bk = k.bank()
            for kc in range(KC):
                k.mm(bk, bk[:, :s1 - s0], wt, wt[:, kc * 128:(kc + 1) * 128], xn, xn[:, kc, x0 + s0:x0 + s1], kc == 0, kc == KC - 1)
            zt = zrot.get()
            k.do("act", "copy", [bk], [zt], out=zt[:, :s1 - s0], in_=bk[:, :s1 - s0])
            k.dma("pool", dst_fn(c)[:, s0:s1], zt[:, :s1 - s0], reads=[zt])


def gelu_tanh(k, src, dst_ap, dst_t, n, uu, sg_, P=128):
    k.do("pool", "tensor_tensor", [src], [uu], out=uu[:P, :n], in0=src[:P, :n], in1=src[:P, :n], op=ALU.mult)
    k.do("pool", "tensor_scalar", [uu], [uu], out=uu[:P, :n], in0=uu[:P, :n], scalar1=0.044715, scalar2=1.0, op0=ALU.mult, op1=ALU.add)
    k.do("pool", "tensor_tensor", [uu, src], [uu], out=uu[:P, :n], in0=uu[:P, :n], in1=src[:P, :n], op=ALU.mult)
    k.do("act", "activation", [uu], [sg_], out=sg_[:P, :n], in_=uu[:P, :n], func=AF.Sigmoid, scale=1.5957691216057308)
    k.do("dve", "tensor_tensor", [src, sg_], [dst_t], out=dst_ap, in0=src[:P, :n], in1=sg_[:P, :n], op=ALU.mult)


def phase_A(k, ctx, D):
    NT = 1024
    T = T_SEQ
    ctx.scratch()
    h = k.sb([128, 8, NT], F32, name="h"); xn = k.sb([128, 8, NT], BF16, name="xn")
    gt = k.sb([128, 8], F32, name="gA")
    wrot = Rot(k, 2, [128, 4096], BF16, "wr"); zrot = Rot(k, 3, [128, 512], F32, "zr")
    zer = k.sb([128, 8], F32, name="zer")
    k.do("pool", "memset", [], [zer], zer[:, :], 0.0)
    for c in range(12):
        k.dma("pool", D["zA_fm"][c][:, 0:3], zer[:, 0:3], reads=[zer])
    k.dma("sp", gt[:, :], D["g_ab"], writes=[gt])
    for ps in range(T // NT):
        for kc in range(8):
            k.dma("sp" if kc % 2 == 0 else "pool", h[:, kc, :], D["xT"][:, kc, 2 + ps * NT:2 + (ps + 1) * NT], writes=[h])
        ctx.rmsnorm(h, 8, 0, NT, gt, xn, 0)
        tokmajor_proj(k, xn, 8, NT, 0, D["w"]("a_tm"), 512, 2056, D["zA_tm"][ps * NT:(ps + 1) * NT, :], wrot, zrot)
        featmajor_proj(k, xn, 8, NT, 0, D["w"]("a_fm"), 12, (lambda c, ps=ps: D["zA_fm"][c][:, 3 + ps * NT:3 + (ps + 1) * NT]), wrot, zrot)


def phase_P2(k, ctx, D):
    T = T_SEQ
    SEG = 512
    k.banks_init(6)
    pbf = [k.bank_bf(6), k.bank_bf(7)]
    eps = ctx.eps
    mask = k.sb([128, 128], F32, name="mask"); ident = k.sb([128, 128], BF16, name="ident")
    ones32 = k.sb([128, 64], F32, name="ones32")
    k.dma("sp", mask[:, :], D["mask_in"], writes=[mask]); k.dma("sp", ident[:, :], D["ident_in"], writes=[ident])
    k.do("pool", "memset", [], [ones32], ones32[:, :], 1.0)
    stg = Rot(k, 12, [64, SEG + 3], F32, "stg")
    tt = Rot(k, 8, [64, SEG], F32, "tt")
    qT = k.sb([64, T], BF16, name="qT"); kT = k.sb([64, T], BF16, name="kT")
    kt = k.sb([128, NCH, 64], BF16, name="kt")
    v = k.sb([128, NCH, 128], F32, name="v"); vp = k.sb([128, NCH, 129], BF16, name="vp")
    gt = k.sb([128, NCH, 128], F32, name="gt"); oall = k.sb([128, NCH, 129], F32, name="oall")
    res = k.sb([128, NCH, 128], F32, name="res"); resb = k.sb([128, NCH, 128], BF16, name="resb")
    ost = Rot(k, 2, [128, 1024], BF16, "ost")
    A = k.sb([128, NCH], F32, name="A"); Bv = k.sb([128, NCH], F32, name="Bv"); cd = k.sb([64, NCH], F32, name="cd")
    gi = k.sb([128, NCH], F32, name="gi"); gf = k.sb([128, NCH], F32, name="gf"); gb = k.sb([128, 2], F32, name="gb")
    l1 = k.sb([128, NCH], F32, name="l1"); ngb = k.sb([128, 1], F32, name="ngb")
    cw = k.sb([64, 10], F32, name="cw")
    g_t = k.sb([128, 128], F32, name="g_t")
    St = k.sb([64, 129], F32, name="St"); Sb = k.sb([64, 129], BF16, name="Sb"); StC = k.sb([64, 129], F32, name="StC")
    sm = Rot(k, 3, [128, 128], BF16, "sm")
    dn = k.sb([128, NCH], F32, name="dn"); ss = k.sb([128, NCH], F32, name="ss")
    ctb = Rot(k, 3, [64, SEG], F32, "ctb"); stb = Rot(k, 3, [64, SEG], F32, "stb")
    zer = k.sb([128, 8], BF16, name="zerb")
    k.do("pool", "memset", [], [zer], zer[:, :], 0.0)
    for c in range(8):
        k.dma("pool", D["mixT"][c][:, 0:2], zer[:, 0:2], reads=[zer])
    zfm = D["zA_fm"]; ztm = D["zA_tm"]

    def tmcols(col, n):
        return ztm[:, col:col + n].rearrange("(c p) e -> p c e", p=128)

    for slot in range(8):
        is_ret = slot < 4
        j = slot % 4
        rs = slice((j % 2) * 64, (j % 2) * 64 + 64)
        k.dma("sp", v[:, :, :], tmcols((0 if is_ret else 1024) + j * 128, 128), writes=[v])
        k.dma("sp", gt[:, :, :], tmcols((512 if is_ret else 1536) + j * 128, 128), writes=[gt])
        k.dma("sp", g_t[:, :], D["ng"][slot], writes=[g_t])
        if is_ret:
            k.dma("sp", A[:, :], D["r_A"][j], writes=[A]); k.dma("sp", Bv[:, :], D["r_B"][j], writes=[Bv]); k.dma("sp", cd[:, :], D["r_c"][j], writes=[cd])
            for sg in range(T // SEG):
                c_t = ctb.get(); s_t = stb.get()
                k.dma("sp", c_t[:, :], D["ctab"][:, sg * SEG:(sg + 1) * SEG], writes=[c_t])
                k.dma("sp", s_t[:, :], D["stab"][:, sg * SEG:(sg + 1) * SEG], writes=[s_t])
                for (c0, dstT) in ((0, qT), (4, kT)):
                    a = stg.get(); b = stg.get()
                    k.dma("sp", a[:, :SEG], zfm[c0 + j // 2][rs, 3 + sg * SEG:3 + (sg + 1) * SEG], writes=[a])
                    k.dma("sp", b[:, :SEG], zfm[c0 + 2 + j // 2][rs, 3 + sg * SEG:3 + (sg + 1) * SEG], writes=[b])
                    t1 = tt.get(); t2 = tt.get()
                    k.do("dve", "tensor_tensor", [a, c_t], [t1], out=t1[:, :], in0=a[:, :SEG], in1=c_t[:, :], op=ALU.mult)
                    k.do("pool", "tensor_tensor", [b, s_t], [t2], out=t2[:, :], in0=b[:, :SEG], in1=s_t[:, :], op=ALU.mult)
                    k.do("dve", "tensor_tensor", [t1, t2], [dstT], out=dstT[:, sg * SEG:(sg + 1) * SEG], in0=t1[:, :], in1=t2[:, :], op=ALU.add)
        else:
            k.dma("sp", gi[:, :], ztm[:, 2048 + j:2049 + j].rearrange("(c p) o -> p (c o)", p=128), writes=[gi], allow_slow_non_contiguous=True)
            k.dma("sp", gf[:, :], ztm[:, 2052 + j:2053 + j].rearrange("(c p) o -> p (c o)", p=128), writes=[gf], allow_slow_non_contiguous=True)
            k.dma("sp", gb[:, :], D["m_gb"][j], writes=[gb])
            k.dma("sp", cw[:, 0:5], D["m_cw"][j, 0], writes=[cw]); k.dma("sp", cw[:, 5:10], D["m_cw"][j, 1], writes=[cw])
            k.do("dve", "tensor_scalar", [gb], [ngb], out=ngb[:, :], in0=gb[:, 1:2], scalar1=-1.0, scalar2=None, op0=ALU.mult)
            k.do("act", "activation", [gf, ngb], [l1], out=l1[:, :], in_=gf[:, :], func=AF.Exp, bias=ngb[:, 0:1], scale=-1.0)
            k.do("act", "activation", [l1], [l1], out=l1[:, :], in_=l1[:, :], func=AF.Ln, bias=1.0, scale=1.0)
            bk = k.bank()
            k.mm(bk, bk[:, 0:NCH], mask, mask[:, :], l1, l1[:, :], True, True)
            k.mm(bk, bk[0:64, 64:64 + NCH], ones32, ones32[:, :], l1, l1[:, :], True, True)
            k.do("act", "activation", [bk], [A], out=A[:, :], in_=bk[:, 0:NCH], func=AF.Exp, bias=math.log(0.125), scale=-1.0)
            k.do("dve", "tensor_tensor", [gi, bk], [Bv], out=Bv[:, :], in0=gi[:, :], in1=bk[:, 0:NCH], op=ALU.add)
            k.do("act", "activation", [Bv, gb], [Bv], out=Bv[:, :], in_=Bv[:, :], func=AF.Exp, bias=gb[:, 0:1], scale=1.0)
            k.do("act", "activation", [bk], [cd], out=cd[:, :], in_=bk[0:64, 64:64 + NCH], func=AF.Exp, scale=-1.0)
            for sg in range(T // SEG):
                for qk, (c0, dstT) in enumerate(((8, qT), (10, kT))):
                    a = stg.get()
                    k.dma("sp", a[:, :SEG + 3], zfm[c0 + j // 2][rs, sg * SEG:sg * SEG + SEG + 3], writes=[a])
                    t1 = tt.get()
                    o5 = qk * 5
                    k.do("dve", "tensor_scalar", [a, cw], [t1], out=t1[:, :], in0=a[:, 3:SEG + 3], scalar1=cw[:, o5 + 3:o5 + 4],
                         scalar2=cw[:, o5 + 4:o5 + 5], op0=ALU.mult, op1=ALU.add)
                    for tap in (2, 1, 0):
                        k.do("dve", "scalar_tensor_tensor", [a, cw, t1], [t1], out=t1[:, :], in0=a[:, tap:SEG + tap],
                             scalar=cw[:, o5 + tap:o5 + tap + 1], in1=t1[:, :], op0=ALU.mult, op1=ALU.add)
                    k.do("act", "activation", [t1], [dstT], out=dstT[:, sg * SEG:(sg + 1) * SEG], in_=t1[:, :], func=AF.Silu)
        k.do("act", "activation", [gt], [gt], out=gt[:, :, :], in_=gt[:, :, :], func=(AF.Silu if is_ret else AF.Sigmoid))
        for c in range(NCH):
            if c % 4 == 3:
                k.do("dve", "tensor_scalar", [v, Bv], [vp], out=vp[:, c, 0:128], in0=v[:, c, :], scalar1=Bv[:, c:c + 1], scalar2=None, op0=ALU.mult)
            else:
                k.do("act", "activation", [v, Bv], [vp], out=vp[:, c, 0:128], in_=v[:, c, :], func=AF.Copy, scale=Bv[:, c:c + 1])
        k.do("dve", "tensor_copy", [Bv], [vp], out=vp[:, :, 128], in_=Bv[:, :])
        for c in range(NCH):
            pb = pbf[c % 2]
            k.op("pe", (lambda pb=pb, c=c: (lambda e: e.transpose(pb[:, 0:64], kT[:, c * 128:(c + 1) * 128], ident[0:64, 0:64])))(),
                 reads=[kT, ident], writes=[pb])
            k.do("act", "copy", [pb], [kt], out=kt[:, c, :], in_=pb[:, 0:64])
        k.do("dve", "memset", [], [StC], StC[:, :], 0.0)
        k.do("pool", "memset", [], [Sb], Sb[:, :], 0.0)

        def emit_S(c):
            cs = slice(c * 128, (c + 1) * 128)
            b1 = k.bank()
            k.mm(b1, b1[:, 0:128], kT, kT[:, cs], qT, qT[:, cs], True, True)
            s_ = sm.get()
            k.do("dve", "tensor_tensor", [b1, mask], [s_], out=s_[:, :], in0=b1[:, 0:128], in1=mask[:, :], op=ALU.mult)
            return s_
        s_next = emit_S(0)
        for c in range(NCH):
            cs = slice(c * 128, (c + 1) * 128)
            s_ = s_next
            if c + 1 < NCH:
                s_next = emit_S(c + 1)
            if c < NCH - 1:
                b3 = k.bank()
                k.mm(b3, b3[0:64, 0:129], kt, kt[:, c, :], vp, vp[:, c, :], True, True)
            b2 = k.bank()
            k.mm(b2, b2[:, 0:129], s_, s_[:, :], vp, vp[:, c, :], True, False)
            k.mm(b2, b2[:, 0:129], qT, qT[:, cs], Sb, Sb[:, :], False, True)
            k.do("act", "activation", [b2, A], [oall], out=oall[:, c, :], in_=b2[:, 0:129], func=AF.Copy, scale=A[:, c:c + 1])
            if c < NCH - 1:
                k.do("dve", "scalar_tensor_tensor", [b3, cd, StC], [St], out=St[:, :], in0=b3[0:64, 0:129], scalar=cd[:, c:c + 1], in1=StC[:, :],
                     op0=ALU.mult, op1=ALU.add)
                k.do("act", "copy", [St], [Sb], out=Sb[:, :], in_=St[:, :])
                if c + 1 < NCH - 1:
                    k.do("act", "activation", [St, cd], [StC], out=StC[:, :], in_=St[:, :], func=AF.Copy, scale=cd[:, c + 1:c + 2])
        if not is_ret:
            k.do("act", "activation", [oall], [dn], out=dn[:, :], in_=oall[:, :, 128], func=AF.Abs)
            k.do("dve", "tensor_scalar", [dn], [dn], out=dn[:, :], in0=dn[:, :], scalar1=1.0, scalar2=None, op0=ALU.max)
            k.do("dve", "reciprocal", [dn], [dn], out=dn[:, :], in_=dn[:, :])
            for c in range(NCH):
                k.do("pool" if c % 2 else "dve", "tensor_scalar", [oall, dn], [oall], out=oall[:, c, 0:128], in0=oall[:, c, 0:128],
                     scalar1=dn[:, c:c + 1], scalar2=None, op0=ALU.mult)
        k.do("dve", "tensor_tensor", [oall], [v], out=v[:, :, :], in0=oall[:, :, 0:128], in1=oall[:, :, 0:128], op=ALU.mult)
        k.do("dve", "tensor_reduce", [v], [ss], out=ss[:, :], in_=v[:, :, :], axis=AX.X, op=ALU.add)
        k.do("act", "activation", [ss, eps], [ss], out=ss[:, :], in_=ss[:, :], func=AF.Sqrt, bias=eps[:, 0:1], scale=1.0 / 128)
        k.do("dve", "reciprocal", [ss], [ss], out=ss[:, :], in_=ss[:, :])
        for c in range(NCH):
            k.do("dve", "scalar_tensor_tensor", [oall, ss, g_t], [res], out=res[:, c, :], in0=oall[:, c, 0:128], scalar=ss[:, c:c + 1],
                 in1=g_t[:, :], op0=ALU.mult, op1=ALU.mult)
        k.do("dve", "tensor_tensor", [res, gt], [resb], out=resb[:, :, :], in0=res[:, :, :], in1=gt[:, :, :], op=ALU.mult)
        mch = j if is_ret else 4 + j
        for c in range(NCH):
            pb = pbf[(c // 8) % 2]
            k.op("pe", (lambda pb=pb, c=c: (lambda e: e.transpose(pb[:, (c % 8) * 128:(c % 8 + 1) * 128], resb[:, c, :], ident[:, :])))(),
                 reads=[resb, ident], writes=[pb])
            if c % 8 == 7:
                o_ = ost.get()
                k.do("act", "copy", [pb], [o_], out=o_[:, :], in_=pb[:, 0:1024])
                k.dma("sp", D["mixT"][mch][:, 2 + (c - 7) * 128:2 + (c + 1) * 128], o_[:, :], reads=[o_])


def phase_TP(k, ctx, D, layer):
    NT = 1024
    W = NT + 2
    T = T_SEQ
    ctx.scratch()
    k.banks_init(8)
    xsrc = D["xT"] if layer == 0 else D["h1T"]
    msrc = D["mixT"] if layer == 0 else D["oT"]
    wsrc = D["w"]
    h = k.sb([128, 8, W], F32, name="h")
    mixb = k.sb([128, 8, W], BF16, name="mixb")
    xn = k.sb([128, 8, W], BF16, name="xn")
    big = k.sb([128, NJ * NT], BF16, name="big")
    g = V(big, big.ap[:, :].rearrange("p (j n) -> p j n", j=NJ))
    e = V(big, big.ap[:, 0:2 * 8 * W].bitcast(F32).rearrange("p (c w) -> p c w", c=8))
    pst = V(big, big.ap[:, 2 * 8 * W:2 * 8 * W + 2 * 2 * NT].bitcast(F32).rearrange("p (c w) -> p c w", c=2))
    pb = k.sb([128, 2, NT], BF16, name="pb")
    gn = k.sb([128, 4, 8], F32, name="gn"); cw = k.sb([128, NJ, 4], F32, name="cw")
    wrot = Rot(k, 2, [128, 4096], BF16, "wr")
    zrot = Rot(k, 3, [128, 512], F32, "zr")
    a_sb = Rot(k, 2, [128, W], F32, "a_sb"); cc = Rot(k, 2, [128, NT], F32, "cc"); b_sb = Rot(k, 2, [128, NT], BF16, "b_sb")
    uur = Rot(k, 2, [128, NT], F32, "uu"); sgr = Rot(k, 2, [128, NT], F32, "sg")
    k.dma("sp", gn[:, :, :], D["gains"][layer], writes=[gn]); k.dma("sp", cw[:, :, :], D["cwin"][layer], writes=[cw])
    gviews = [V(gn, gn.ap[:, i, :]) for i in range(4)]
    if layer == 0:
        zer = k.sb([128, 16], F32, name="zer")
        k.do("pool", "memset", [], [zer], zer[:, :], 0.0)
        k.dma("pool", D["h1T"][:, :, 0:2], zer.ap[:, 0:16].rearrange("p (c w) -> p c w", c=8), reads=[zer])

    split = (layer == 1)
    if split:
        sel = k.sb([128, 2], F32, name="sel")
        k.dma("sp", sel[:, :], D["sel"], writes=[sel])
    for ps in range(2 if split else T // NT):
        t0 = ps * NT
        for kc in range(8):
            k.dma("sp", h[:, kc, :], xsrc[:, kc, t0:t0 + W], writes=[h])
            k.dma("sp" if split else "pool", mixb[:, kc, :], msrc[kc][:, t0:t0 + W], writes=[mixb])
        if split:
            tB = T // 2 + t0
            for kc in range(8):
                k.dma("sp", e[:, kc, :], xsrc[:, kc, tB:tB + W], writes=[e])
                k.dma("sp", xn[:, kc, :], msrc[kc][:, tB:tB + W], writes=[xn])
            for kc in range(8):
                k.do("dve", "tensor_scalar", [h, sel], [h], out=h[:, kc, :], in0=h[:, kc, :], scalar1=sel[:, 0:1], scalar2=None, op0=ALU.mult)
                k.do("dve", "scalar_tensor_tensor", [e, sel, h], [h], out=h[:, kc, :], in0=e[:, kc, :], scalar=sel[:, 1:2], in1=h[:, kc, :],
                     op0=ALU.mult, op1=ALU.add)
                k.do("dve", "tensor_scalar", [mixb, sel], [mixb], out=mixb[:, kc, :], in0=mixb[:, kc, :], scalar1=sel[:, 0:1], scalar2=None, op0=ALU.mult)
                k.do("dve", "scalar_tensor_tensor", [xn, sel, mixb], [mixb], out=mixb[:, kc, :], in0=xn[:, kc, :], scalar=sel[:, 1:2], in1=mixb[:, kc, :],
                     op0=ALU.mult, op1=ALU.add)
        for c in range(8):
            wt = wrot.get()
            k.dma("sp", wt[:, :1024], wsrc("w_out%d" % layer)(c), writes=[wt])
            for (s0, s1) in slices512(0, W):
                bk = k.bank()
                for kc in range(8):
                    k.mm(bk, bk[:, :s1 - s0], wt, wt[:, kc * 128:(kc + 1) * 128], mixb, mixb[:, kc, s0:s1], kc == 0, kc == 7)
                k.do("dve", "tensor_tensor", [h, bk], [h], out=h[:, c, s0:s1], in0=h[:, c, s0:s1], in1=bk[:, :s1 - s0], op=ALU.add)
        ctx.rmsnorm(h, 8, 0, W, gviews[0], xn, 0)
        for j in range(NJ):
            wt = wrot.get()
            k.dma("sp", wt[:, :2048], wsrc("w_up%d" % layer)(j), writes=[wt])
            a = a_sb.get(); bb = b_sb.get()
            for (s0, s1) in slices512(0, W):
                bk = k.bank()
                for kc in range(8):
                    k.mm(bk, bk[:, :s1 - s0], wt, wt[:, kc * 256:kc * 256 + 128], xn, xn[:, kc, s0:s1], kc == 0, kc == 7)
                k.do("act", "copy", [bk], [a], out=a[:, s0:s1], in_=bk[:, :s1 - s0])
            for (s0, s1) in slices512(2, W):
                bk = k.bank()
                for kc in range(8):
                    k.mm(bk, bk[:, :s1 - s0], wt, wt[:, kc * 256 + 128:kc * 256 + 256], xn, xn[:, kc, s0:s1], kc == 0, kc == 7)
                k.do("act", "copy", [bk], [bb], out=bb[:, s0 - 2:s1 - 2], in_=bk[:, :s1 - s0])
            c_ = cc.get(); u_ = uur.get(); sg_ = sgr.get()
            k.do("pool", "tensor_scalar", [a, cw], [c_], out=c_[:, :], in0=a[:, 2:W], scalar1=cw[:, j, 2:3], scalar2=cw[:, j, 3:4],
                 op0=ALU.mult, op1=ALU.add)
            k.do("dve", "scalar_tensor_tensor", [a, cw, c_], [c_], out=c_[:, :], in0=a[:, 1:W - 1], scalar=cw[:, j, 1:2], in1=c_[:, :],
                 op0=ALU.mult, op1=ALU.add)
            k.do("dve", "scalar_tensor_tensor", [a, cw, c_], [c_], out=c_[:, :], in0=a[:, 0:W - 2], scalar=cw[:, j, 0:1], in1=c_[:, :],
                 op0=ALU.mult, op1=ALU.add)
            k.do("act", "activation", [c_], [u_], out=u_[:, :], in_=c_[:, :], func=AF.Square, scale=math.sqrt(0.044715))
            k.do("dve", "scalar_tensor_tensor", [u_, c_], [u_], out=u_[:, :], in0=u_[:, :], scalar=1.0, in1=c_[:, :], op0=ALU.add, op1=ALU.mult)
            k.do("act", "activation", [u_], [sg_], out=sg_[:, :], in_=u_[:, :], func=AF.Sigmoid, scale=1.5957691216057308)
            k.do("dve", "tensor_tensor", [c_, sg_], [g], out=g[:, j, :], in0=c_[:, :], in1=sg_[:, :], op=ALU.mult)
            k.do("dve", "tensor_tensor", [g, bb], [g], out=g[:, j, :], in0=g[:, j, :], in1=bb[:, :], op=ALU.mult)
        for c in range(8):
            wt = wrot.get()
            k.dma("sp", wt[:, :NJ * 128], wsrc("w_dn%d" % layer)(c), writes=[wt])
            for (s0, s1) in slices512(0, NT):
                bk = k.bank()
                for j in range(NJ):
                    k.mm(bk, bk[:, :s1 - s0], wt, wt[:, j * 128:(j + 1) * 128], g, g[:, j, s0:s1], j == 0, j == NJ - 1)
                k.do("dve", "tensor_tensor", [h, bk], [h], out=h[:, c, 2 + s0:2 + s1], in0=h[:, c, 2 + s0:2 + s1], in1=bk[:, :s1 - s0], op=ALU.add)
        for kc in range(2):
            k.dma("pool", pst[:, kc, :], (D["pT1h"][:, kc, t0:t0 + NT] if split else D["pT"][layer][:, kc, t0:t0 + NT]), writes=[pst])
        for kc in range(2):
            k.do("act", "copy", [pst], [pb], out=pb[:, kc, :], in_=pst[:, kc, :])
        for c in range(8):
            wt = wrot.get()
            k.dma("sp", wt[:, :256], wsrc("w_ple%d" % layer)(c), writes=[wt])
            for (s0, s1) in slices512(0, NT):
                bk = k.bank()
                for kc in range(2):
                    k.mm(bk, bk[:, :s1 - s0], wt, wt[:, kc * 128:(kc + 1) * 128], pb, pb[:, kc, s0:s1], kc == 0, kc == 1)
                k.do("act", "copy", [bk], [e], out=e[:, c, s0:s1], in_=bk[:, :s1 - s0])
        ctx.rmsnorm(e, 8, 0, NT, gviews[1], e, 0)
        ctx.rmsnorm(h, 8, 2, W, gviews[2], xn, 2)
        for c in range(8):
            wt = wrot.get()
            k.dma("sp", wt[:, :1024], wsrc("w_gate%d" % layer)(c), writes=[wt])
            for (s0, s1) in slices512(2, W):
                bk = k.bank()
                for kc in range(8):
                    k.mm(bk, bk[:, :s1 - s0], wt, wt[:, kc * 128:(kc + 1) * 128], xn, xn[:, kc, s0:s1], kc == 0, kc == 7)
                t_ = ctx.f32tmp.get()
                k.do("act", "activation", [bk], [t_], out=t_[:, :s1 - s0], in_=bk[:, :s1 - s0], func=AF.Sigmoid)
                k.do("dve", "tensor_tensor", [t_, e], [t_], out=t_[:, :s1 - s0], in0=t_[:, :s1 - s0], in1=e[:, c, s0 - 2:s1 - 2], op=ALU.mult)
                k.do("dve", "tensor_tensor", [h, t_], [h], out=h[:, c, s0:s1], in0=h[:, c, s0:s1], in1=t_[:, :s1 - s0], op=ALU.add)
        if layer == 0:
            for kc in range(8):
                k.dma("pool", D["h1T"][:, kc, 2 + t0:2 + t0 + NT], h[:, kc, 2:W], reads=[h])
            ctx.rmsnorm(h, 8, 2, W, gviews[3], xn, 2)
            tokmajor_proj(k, xn, 8, NT, 2, wsrc("n_tm"), 512, 1584, D["z1tm"][t0:t0 + NT, :], wrot, zrot)
            featmajor_proj(k, xn, 8, NT, 2, wsrc("n_fm"), 2, (lambda c, t0=t0: D["c2T"][c][:, t0:t0 + NT]), wrot, zrot)
        else:
            for kc in range(8):
                k.dma("pool", D["out"][:, kc, t0:t0 + NT], h[:, kc, 2:W], reads=[h], is_out=True)


def phase_P5(k, ctx, D, g):
    T = T_SEQ
    NQ = T // 128
    def kv4(i):
        return D["z1tm"][:, 1024 + i * 128 + g * 64:1024 + i * 128 + (g + 1) * 64].rearrange("(c p) d -> p c d", p=128)
    if g == 0:
        zer = k.sb([128, 8], BF16, name="zerb")
        k.do("pool", "memset", [], [zer], zer[:, :], 0.0)
        for c in range(8):
            k.dma("pool", D["oT"][c][:, 0:2], zer[:, 0:2], reads=[zer])
    SC = 0.125
    k.banks_init(4)
    O = [k.allbanks[4], k.allbanks[5]]
    IMP = k.allbanks[6]
    pbf = k.bank_bf(7)
    ident = k.sb([128, 128], BF16, name="ident"); tri = k.sb([128, 2, 128], BF16, name="tri")
    OV = k.sb([128, 2, 64], BF16, name="OV"); iota = k.sb([128, 128], F32, name="iota"); thr = k.sb([128, 2 * NQ], F32, name="thr")
    qg = k.sb([128, 64], F32, name="qg"); kg = k.sb([128, 3, 64], F32, name="kg")
    eps = ctx.eps
    k.dma("sp", ident[:, :], D["ident_in"], writes=[ident])
    for i in range(2):
        k.dma("sp", tri[:, i, :], D["tri_in"][i], writes=[tri])
    k.dma("sp", OV[:, :, :], D["OV_in"], writes=[OV]); k.dma("sp", iota[:, :], D["iota_in"], writes=[iota]); k.dma("sp", thr[:, :], D["thr_in"], writes=[thr])
    k.dma("sp", qg[:, :], D["qg_in"], writes=[qg]); k.dma("sp", kg[:, :, :], D["kg_in"], writes=[kg])
    KE = k.sb([128, T], BF16, name="KE"); KwT = k.sb([64, T], BF16, name="KwT")
    k.dma("pool", KE[64:128, :], D["E_in"], writes=[KE])
    VsA = k.sb([128, NQ, 65], BF16, name="VsA"); VwA = k.sb([128, NQ, 65], BF16, name="VwA")
    KcT = k.sb([64, 256], BF16, name="KcT"); VcA = k.sb([128, 2, 65], BF16, name="VcA")
    raw = k.sb([128, NQ, 64], F32, name="raw"); sqr = k.sb([128, NQ, 64], F32, name="sqr")
    ss = k.sb([128, NQ], F32, name="ss"); kn = k.sb([128, NQ, 64], BF16, name="kn")
    for idx, (src_i, dstT, gi_) in enumerate(((0, KE, 1), (2, KwT, 2))):
        k.dma("sp", raw[:, :, :], kv4(src_i), writes=[raw])
        k.do("dve", "tensor_tensor", [raw], [sqr], out=sqr[:, :, :], in0=raw[:, :, :], in1=raw[:, :, :], op=ALU.mult)
        k.do("dve", "tensor_reduce", [sqr], [ss], out=ss[:, :], in_=sqr[:, :, :], axis=AX.X, op=ALU.add)
        k.do("act", "activation", [ss, eps], [ss], out=ss[:, :], in_=ss[:, :], func=AF.Sqrt, bias=eps[:, 0:1], scale=1.0 / 64)
        k.do("dve", "reciprocal", [ss], [ss], out=ss[:, :], in_=ss[:, :])
        for c in range(NQ):
            k.do("dve", "scalar_tensor_tensor", [raw, ss, kg], [kn], out=kn[:, c, :], in0=raw[:, c, :], scalar=ss[:, c:c + 1],
                 in1=kg[:, gi_, :], op0=ALU.mult, op1=ALU.mult)
        for c in range(NQ):
            k.op("pe", (lambda c=c: (lambda e: e.transpose(pbf[0:64, (c % 8) * 128:(c % 8 + 1) * 128], kn[:, c, :], ident[:, :])))(),
                 reads=[kn, ident], writes=[pbf])
            if c % 8 == 7:
                k.do("act", "copy", [pbf], [dstT], out=dstT[0:64, (c - 7) * 128:(c + 1) * 128], in_=pbf[0:64, 0:1024])
    for (src_i, dstV) in ((1, VsA), (3, VwA)):
        k.dma("sp", raw[:, :, :], kv4(src_i), writes=[raw])
        k.do("dve", "tensor_copy", [raw], [dstV], out=dstV[:, :, 0:64], in_=raw[:, :, :])
        k.do("pool", "memset", [], [dstV], dstV[:, :, 64:65], 1.0)
    x2 = k.sb([128, T + 16], F32, name="x2"); x2b = k.sb([128, T + 16], BF16, name="x2b")
    w1 = k.sb([128, 16 * 256], BF16, name="w1"); w2 = k.sb([128, 2 * 64], BF16, name="w2")
    posf = k.sb([128, 16], F32, name="posf"); posb = k.sb([128, 16], BF16, name="posb")
    c0 = k.sb([128, 2], F32, name="c0"); Hx = k.sb([128, 256], F32, name="Hx"); Hg = k.sb([128, 2, 256], BF16, name="Hg")
    uu = k.sb([128, 512], F32, name="uu"); sg_ = k.sb([128, 512], F32, name="sg")
    kc_f = k.sb([128, 64], F32, name="kc_f"); kc_s = k.sb([128, 64], F32, name="kc_s"); ss1 = k.sb([128, 1], F32, name="ss1")
    Mb = k.sb([128, 128], BF16, name="Mb")
    k.do("pool", "memset", [], [Mb], Mb[:, :], 0.0)
    k.do("pool", "memset", [], [VcA], VcA[:, :, :], 0.0)
    k.do("pool", "memset", [], [VcA], VcA[:, :, 64:65], 1.0)
    for ci in range(2):
        k.do("pool", "memset", [], [x2], x2[:, T - 16:T + 16], 0.0)
        k.dma("sp", x2[0:64, 0:T], D["c2T"][ci][g * 64:(g + 1) * 64, 0:T], writes=[x2]); k.dma("pool", x2[64:128, 0:T - 1], D["c2T"][ci][g * 64:(g + 1) * 64, 1:T], writes=[x2])
        k.dma("sp", w1[:, :], D["w"]("nsa_cmp_w1k" if ci == 0 else "nsa_cmp_w1v")(0), writes=[w1]); k.dma("sp", w2[:, :], D["w"]("nsa_cmp_w2k" if ci == 0 else "nsa_cmp_w2v")(0), writes=[w2]); k.dma("sp", posf[:, :], D["pos_in"][ci], writes=[posf])
        k.do("dve", "tensor_copy", [x2], [x2b], out=x2b[:, 0:2048], in_=x2[:, 0:2048])
        k.do("act", "copy", [x2], [x2b], out=x2b[:, 2048:T + 16], in_=x2[:, 2048:T + 16])
        k.do("dve", "tensor_copy", [posf], [posb], out=posb[:, :], in_=posf[:, :])
        for jc in range(2):
            bk = k.bank(); bk2 = k.bank()
            for m in range(16):
                rhs = x2b.ap[:, 2 * m:2 * m + 16 * 255].rearrange("p (n s) -> p n s", s=16)[:, :, 0]
                k.mm(bk, bk[:, 0:255], w1, w1[:, m * 256 + jc * 128:m * 256 + (jc + 1) * 128], x2b, rhs, m == 0, m == 15)
            for m in range(16):
                k.mm(bk2, bk2[:, 0:1], w1, w1[:, m * 256 + jc * 128:m * 256 + (jc + 1) * 128], posb, posb[:, m:m + 1], m == 0, m == 15)
            k.do("act", "copy", [bk2], [c0], out=c0[:, jc:jc + 1], in_=bk2[:, 0:1])
            k.do("act", "activation", [bk, c0], [Hx], out=Hx[:, 0:255], in_=bk[:, 0:255], func=AF.Identity, bias=c0[:, jc:jc + 1], scale=1.0)
            gelu_tanh(k, Hx, Hg[:, jc, 0:255], Hg, 255, uu, sg_)
        for nt, cnt in ((0, 128), (1, 127)):
            bk = k.bank()
            for jc in range(2):
                k.mm(bk, bk[0:cnt, 0:64], Hg, Hg[:, jc, nt * 128:nt * 128 + cnt], w2, w2[:, jc * 64:(jc + 1) * 64], jc == 0, jc == 1)
            if ci == 0:
                k.do("act", "copy", [bk], [kc_f], out=kc_f[0:cnt, :], in_=bk[0:cnt, 0:64])
                k.do("dve", "tensor_tensor", [kc_f], [kc_s], out=kc_s[0:cnt, :], in0=kc_f[0:cnt, :], in1=kc_f[0:cnt, :], op=ALU.mult)
                k.do("dve", "tensor_reduce", [kc_s], [ss1], out=ss1[0:cnt, :], in_=kc_s[0:cnt, :], axis=AX.X, op=ALU.add)
                k.do("act", "activation", [ss1, eps], [ss1], out=ss1[0:cnt, :], in_=ss1[0:cnt, :], func=AF.Sqrt, bias=eps[0:cnt, 0:1], scale=1.0 / 64)
                k.do("dve", "reciprocal", [ss1], [ss1], out=ss1[0:cnt, :], in_=ss1[0:cnt, :])
                k.do("dve", "scalar_tensor_tensor", [kc_f, ss1, kg], [kn], out=kn[0:cnt, 0, :], in0=kc_f[0:cnt, :], scalar=ss1[0:cnt, 0:1],
                     in1=kg[0:cnt, 0, :], op0=ALU.mult, op1=ALU.mult)
                k.op("pe", (lambda cnt=cnt: (lambda e: e.transpose(pbf[0:64, 0:cnt], kn[0:cnt, 0, :], ident[0:cnt, 0:cnt])))(),
                     reads=[kn, ident], writes=[pbf])
                k.do("act", "copy", [pbf], [KcT], out=KcT[0:64, nt * 128:nt * 128 + cnt], in_=pbf[0:64, 0:cnt])
            else:
                k.do("act", "copy", [bk], [VcA], out=VcA[0:cnt, nt, 0:64], in_=bk[0:cnt, 0:64])
    gts = k.sb([128, NQ, 24], F32, name="gts"); gbt = k.sb([128, NQ, 24], F32, name="gbt")
    k.dma("sp", gts[:, :, :], D["z1tm"][:, 1536 + g * 24:1536 + (g + 1) * 24].rearrange("(c p) d -> p c d", p=128), writes=[gts]); k.dma("sp", gbt[:, :, :], D["gb_in"][g], writes=[gbt])
    k.do("dve", "tensor_tensor", [gts, gbt], [gts], out=gts[:, :, :], in0=gts[:, :, :], in1=gbt[:, :, :], op=ALU.add)
    k.do("act", "activation", [gts], [gts], out=gts[:, :, :], in_=gts[:, :, :], func=AF.Sigmoid)
    LOOK = 2
    qblk = Rot(k, 2, [128, 512], F32, "qblk"); qsq = k.sb([128, 512], F32, name="qsq"); ss8 = k.sb([128, 8], F32, name="ss8")
    qn = Rot(k, 2, [128, 512], BF16, "qn"); QMr = Rot(k, 4, [128, 1024], BF16, "QM")
    oacc = Rot(k, 4, [128, 512], F32, "oacc"); Pt = Rot(k, LOOK + 3, [128, 512], BF16, "Pt")
    selc = Rot(k, 3, [128, 2, 64], F32, "selc")
    rc4 = Rot(k, 2, [128, 4], F32, "rc4"); rc8r = Rot(k, 3, [128, 8], F32, "rc8"); cf4 = Rot(k, 2, [128, 4], F32, "cf4")
    impt = k.sb([128, 64], F32, name="impt"); sc1 = k.sb([128, 64], F32, name="sc1"); sc2 = k.sb([128, 64], F32, name="sc2")
    m8a = k.sb([128, 8], F32, name="m8a"); m8b = k.sb([128, 8], F32, name="m8b")
    mkc = Rot(k, 4, [128, 128], BF16, "mkc"); ftmp = Rot(k, 3, [128, 256], F32, "ftmp"); imptmp = k.sb([128, 512], F32, name="imptmp")
    obf = Rot(k, 3, [128, 512], BF16, "obf"); ostage = Rot(k, 2, [128, 4, 1024], BF16, "ostage")
    osg_box = [None]
    Mbr = Rot(k, 3, [128, 128], BF16, "Mbr")
    for t_ in Mbr.t:
        k.do("pool", "memset", [], [t_], t_[:, :], 0.0)

    def prep(qb):
        st = dict(qb=qb, QM=QMr.get(), oa=oacc.get(), sl=selc.get(), rc8=rc8r.get())
        st["QMhi"] = type(st["QM"])(st["QM"].ap, "QMhi")
        qt = qblk.get(); qnt = qn.get(); QM = st["QM"]
        k.dma("sp", qt[:, :], D["z1tm"][qb * 128:(qb + 1) * 128, g * 512:(g + 1) * 512], writes=[qt])
        k.dma("sp", st["sl"][:, :, :], D["selc_in"][qb], writes=[st["sl"]])
        k.do("pool", "tensor_tensor", [qt], [qsq], out=qsq[:, :], in0=qt[:, :], in1=qt[:, :], op=ALU.mult)
        k.do("dve", "tensor_reduce", [qsq], [ss8], out=ss8[:, :], in_=qsq.ap[:, :].rearrange("p (h d) -> p h d", h=8), axis=AX.X, op=ALU.add)
        k.do("act", "activation", [ss8, eps], [ss8], out=ss8[:, :], in_=ss8[:, :], func=AF.Ln, bias=eps[:, 0:1], scale=1.0 / 64)
        k.do("act", "activation", [ss8], [ss8], out=ss8[:, :], in_=ss8[:, :], func=AF.Exp, scale=-0.5)
        k.do("dve", "tensor_tensor", [qt, ss8], [qsq], out=qsq.ap[:, :].rearrange("p (h d) -> p h d", h=8),
             in0=qt.ap[:, :].rearrange("p (h d) -> p h d", h=8), in1=ss8.ap[:, :].unsqueeze(2).to_broadcast([128, 8, 64]), op=ALU.mult)
        k.do("pool", "tensor_tensor", [qsq, qg], [qnt], out=qnt.ap[:, :].rearrange("p (h d) -> p h d", h=8),
             in0=qsq.ap[:, :].rearrange("p (h d) -> p h d", h=8), in1=qg.ap[:, :].unsqueeze(1).to_broadcast([128, 8, 64]), op=ALU.mult)
        st["qnt"] = qnt
        return st

    def prep_pe(st):
        qnt = st["qnt"]; QM = st["QM"]
        for h in range(8):
            k.op("pe", (lambda h=h, qnt=qnt: (lambda e: e.transpose(pbf[0:64, h * 128:(h + 1) * 128], qnt[:, h * 64:(h + 1) * 64], ident[:, :])))(),
                 reads=[qnt, ident], writes=[pbf])
        k.do("act", "copy", [pbf], [QM], out=QM[0:64, :], in_=pbf[0:64, 0:1024])

    def finalize(st, br, hh):
        oa = st["oa"]; qb = st["qb"]; rc8 = st["rc8"]
        Ob = O[hh]
        c4 = cf4.get()
        den = Ob.ap[:, 0:260].rearrange("p (h c) -> p h c", c=65)[:, :, 64]
        O4 = Ob.ap[:, 0:260].rearrange("p (h c) -> p h c", c=65)[:, :, 0:64]
        oa4 = oa.ap[:, hh * 256:(hh + 1) * 256].rearrange("p (h d) -> p h d", h=4)
        if br == 0:
            r4 = V(rc8, rc8.ap[:, hh * 4:(hh + 1) * 4])
            k.do("dve", "tensor_scalar", [Ob], [r4], out=r4[:, :], in0=den, scalar1=1e-30, scalar2=None, op0=ALU.max)
            k.do("dve", "reciprocal", [r4], [r4], out=r4[:, :], in_=r4[:, :])
        else:
            r4 = rc4.get()
            k.do("dve", "reciprocal", [Ob], [r4], out=r4[:, :], in_=den)
        gap = gts.ap[:, qb, :].rearrange("p (h b) -> p h b", b=3)[:, hh * 4:(hh + 1) * 4, br]
        k.do("dve", "tensor_tensor", [r4, gts], [c4], out=c4[:, :], in0=r4[:, :], in1=gap, op=ALU.mult)
        c4b = c4.ap[:, :].unsqueeze(2).to_broadcast([128, 4, 64])
        if br == 0:
            k.do("dve", "tensor_tensor", [Ob, c4], [oa], out=oa4, in0=O4, in1=c4b, op=ALU.mult)
        else:
            t4 = ftmp.get()
            t4v = t4.ap[:, :].rearrange("p (h d) -> p h d", h=4)
            k.do("dve", "tensor_tensor", [Ob, c4], [t4], out=t4v, in0=O4, in1=c4b, op=ALU.mult)
            k.do("pool", "tensor_tensor", [oa, t4], [oa], out=oa4, in0=oa4, in1=t4v, op=ALU.add)

    def select(st):
        rc8 = st["rc8"]; sl = st["sl"]
        Mb = Mbr.get(); st["Mb"] = Mb
        k.do("dve", "tensor_tensor", [IMP, rc8], [imptmp], out=imptmp.ap[:, :].rearrange("p (h j) -> p h j", h=8),
             in0=IMP.ap[:, :].rearrange("p (h j) -> p h j", h=8), in1=rc8.ap[:, :].unsqueeze(2).to_broadcast([128, 8, 64]), op=ALU.mult)
        k.do("dve", "tensor_reduce", [imptmp], [impt], out=impt[:, :], in_=imptmp.ap[:, :].rearrange("p (h j) -> p j h", h=8), axis=AX.X, op=ALU.add)
        k.do("dve", "tensor_tensor", [impt, sl], [sc1], out=sc1[:, :], in0=impt[:, :], in1=sl[:, 0, :], op=ALU.mult)
        k.do("dve", "tensor_tensor", [sc1, sl], [sc1], out=sc1[:, :], in0=sc1[:, :], in1=sl[:, 1, :], op=ALU.add)
        k.do("dve", "max", [sc1], [m8a], out=m8a[:, :], in_=sc1[:, :])
        k.do("dve", "tensor_scalar", [sc1, m8a], [sc2], out=sc2[:, :], in0=sc1[:, :], scalar1=m8a[:, 7:8], scalar2=-3e38, op0=ALU.is_ge, op1=ALU.mult)
        k.do("dve", "tensor_tensor", [sc2, sc1], [sc2], out=sc2[:, :], in0=sc2[:, :], in1=sc1[:, :], op=ALU.add)
        k.do("dve", "max", [sc2], [m8b], out=m8b[:, :], in_=sc2[:, :])
        k.do("dve", "tensor_scalar", [sc1, m8b], [Mb], out=Mb[:, 64:128], in0=sc1[:, :], scalar1=m8b[:, 7:8], scalar2=-1.0, op0=ALU.is_ge, op1=ALU.add)

    def select_pe(st):
        QM = st["QM"]; QMhi = st["QMhi"]; Mb = st["Mb"]
        k.op("pe", (lambda Mb=Mb: (lambda e: e.transpose(pbf[:, 0:128], Mb[:, :], ident[:, :])))(), reads=[Mb, ident], writes=[pbf])
        k.do("act", "copy", [pbf], [QMhi], out=QM.ap[64:128, :].rearrange("p (h q) -> p h q", h=8),
             in_=pbf.ap[64:128, 0:128].unsqueeze(1).to_broadcast([64, 8, 128]))

    pending_out = []

    def output_pre(st):
        ob_ = obf.get(); st["ob"] = ob_
        k.do("pool", "tensor_copy", [st["oa"]], [ob_], out=ob_[:, :], in_=st["oa"][:, :])
        pending_out.append(st)

    def flush_out():
        while pending_out:
            output_pe(pending_out.pop(0))

    def output_pe(st):
        qb = st["qb"]; ob_ = st["ob"]
        for c4_ in range(4):
            k.op("pe", (lambda c4_=c4_, ob_=ob_: (lambda e: e.transpose(pbf[:, c4_ * 128:(c4_ + 1) * 128], ob_[:, c4_ * 128:(c4_ + 1) * 128], ident[:, :])))(),
                 reads=[ob_, ident], writes=[pbf])
        if qb % 8 == 0:
            osg_box[0] = ostage.get()
        osg = osg_box[0]
        k.do("dve", "tensor_copy", [pbf], [osg], out=osg[:, :, (qb % 8) * 128:(qb % 8 + 1) * 128],
             in_=pbf.ap[:, 0:512].rearrange("p (c q) -> p c q", c=4))
        if qb % 8 == 7:
            for c4_ in range(4):
                k.dma("sp", D["oT"][g * 4 + c4_][:, 2 + (qb - 7) * 128:2 + (qb + 1) * 128], osg[:, c4_, :], reads=[osg])

    jobs = []

    def add_branch(st, br, tiles, pre_first=None, post_last=None):
        nt_ = len(tiles)
        for ti, tl in enumerate(tiles):
            for hh in range(2):
                jb = dict(st=st, br=br, tl=tl, hh=hh, first=(ti == 0), last=(ti == nt_ - 1), pre=[], post=[])
                if ti == 0 and hh == 0 and pre_first is not None:
                    jb["pre"].append(pre_first)
                if ti == nt_ - 1:
                    jb["post"].append((lambda st=st, br=br, hh=hh: finalize(sts[st], br, hh)))
                    if hh == 1 and post_last is not None:
                        jb["post"].append(post_last)
                jobs.append(jb)

    def cmp_tiles(qb):
        nk = min(8 * qb + 7, 255)
        ctiles = [(0, min(nk, 128))] + ([(1, nk - 128)] if nk > 128 else [])
        out_ = []
        for (nt, cnt) in ctiles:
            def mfn(qb=qb, nt=nt, cnt=cnt):
                mk = mkc.get()
                k.do("dve", "tensor_scalar", [iota, thr], [mk], out=mk[:, :], in0=iota[:, :], scalar1=thr[:, 2 * qb + nt:2 * qb + nt + 1], scalar2=None,
                     op0=ALU.is_ge)
                return mk, mk.ap[0:cnt, :]
            out_.append((KcT, KcT[0:64, nt * 128:nt * 128 + cnt], 64, cnt, VcA[0:cnt, nt, :], VcA, mfn, OV[0:cnt, nt, :]))
        return out_

    def win_tiles(qb):
        out_ = []
        for kt_ in range(max(0, qb - 4), qb + 1):
            mi_ = 0 if kt_ == qb else (1 if kt_ == qb - 4 else None)
            mfn = (lambda mi_=mi_: (tri, tri.ap[:, mi_, :])) if mi_ is not None else None
            out_.append((KwT, KwT[0:64, kt_ * 128:(kt_ + 1) * 128], 64, 128, VwA[:, kt_, :], VwA, mfn, None))
        return out_

    def slc_tiles(qb):
        out_ = []
        for kt_ in range(qb + 1):
            mfn = (lambda: (tri, tri.ap[:, 0, :])) if kt_ == qb else None
            out_.append((KE, KE[:, kt_ * 128:(kt_ + 1) * 128], 128, 128, VsA[:, kt_, :], VsA, mfn, None))
        return out_

    sts = {}

    def mk_prep(qb):
        def f():
            sts[qb] = prep(qb)
        return f

    order = []
    for qb in range(NQ):
        if qb == 0:
            order.append(("cmp", 0))
        if qb + 1 < NQ:
            order.append(("cmp", qb + 1))
        order.append(("win", qb)); order.append(("slc", qb))
    def pre_cmp(q):
        def f():
            if q == 0:
                sts[0] = prep(0); prep_pe(sts[0])
                if NQ > 1:
                    sts[1] = prep(1); prep_pe(sts[1])
            if q >= 1 and q + 1 < NQ:
                sts[q + 1] = prep(q + 1)
        return f

    def pre_win(q):
        def f():
            select_pe(sts[q])
            flush_out()
        return f

    def pre_slc(q):
        def f():
            if q + 2 < NQ and q + 2 >= 2 and (q + 2) in sts:
                prep_pe(sts[q + 2])
        return f
    for kind, qb in order:
        if kind == "cmp":
            add_branch(qb, 0, cmp_tiles(qb), pre_first=pre_cmp(qb), post_last=(lambda qb=qb: select(sts[qb])))
        elif kind == "win":
            add_branch(qb, 2, win_tiles(qb), pre_first=pre_win(qb))
        else:
            add_branch(qb, 1, slc_tiles(qb), pre_first=pre_slc(qb), post_last=(lambda qb=qb: output_pre(sts[qb])))

    live = {}
    for n in range(len(jobs) + LOOK):
        if n < len(jobs):
            jb = jobs[n]
            for f in jb["pre"]:
                f()
            st = sts[jb["st"]]
            (Kt, Kap, Kp, cnt, Vap, Vt, mfn, Xap) = jb["tl"]
            hh = jb["hh"]; QM = st["QM"]
            bk = k.bank()
            rds = [Kt, QM] + ([st["QMhi"]] if Kp == 128 else [])
            k.op("pe", (lambda bk=bk, Kap=Kap, QM=QM, Kp=Kp, hh=hh, cnt=cnt:
                        (lambda e: e.matmul(bk[0:cnt, 0:512], lhsT=Kap, rhs=QM[0:Kp, hh * 512:(hh + 1) * 512], start=True, stop=True)))(),
                 reads=rds, writes=[bk])
            pt = Pt.get()
            k.do("act", "activation", [bk], [pt], out=pt[0:cnt, :], in_=bk[0:cnt, 0:512], func=AF.Exp, scale=SC)
            if mfn is not None:
                Mt, Map = mfn()
                k.do("dve", "tensor_tensor", [pt, Mt], [pt], out=pt.ap[0:cnt, :].rearrange("p (h q) -> p h q", h=4),
                     in0=pt.ap[0:cnt, :].rearrange("p (h q) -> p h q", h=4),
                     in1=Map.unsqueeze(1).to_broadcast([cnt, 4, 128]), op=ALU.mult)
            live[n] = pt
        m = n - LOOK
        if m >= 0:
            jb = jobs[m]
            (Kt, Kap, Kp, cnt, Vap, Vt, mfn, Xap) = jb["tl"]
            hh = jb["hh"]; ppt = live.pop(m)
            for h in range(4):
                k.mm(O[hh], O[hh][:, h * 65:(h + 1) * 65], ppt, ppt[0:cnt, h * 128:(h + 1) * 128], Vt, Vap,
                     jb["first"] and h == 0, jb["last"], skip=True)
                if Xap is not None:
                    k.mm(IMP, IMP[:, (hh * 4 + h) * 64:(hh * 4 + h + 1) * 64], ppt, ppt[0:cnt, h * 128:(h + 1) * 128], OV, Xap,
                         jb["first"] and h == 0 and hh == 0, jb["last"], skip=True)
            for f in jb["post"]:
                f()
    flush_out()

def build_fused(M, wtab):
    nc = bass.Bass("TRN2", target_bir_lowering=False)
    T = T_SEQ
    NQ = T // 128
    di = lambda n, s, dt=F32: nc.dram_tensor(n, s, dt, kind="ExternalInput").ap()
    dsc = lambda n, s, dt=F32: nc.dram_tensor(n, s, dt).ap()
    D = {}
    wsrc_in = di("wsrc", [128, M])
    D["xT"] = di("xT", [128, 8, T + 2]); D["pT"] = di("pT", [2, 128, 2, T])
    D["g_ab"] = di("g_ab", [128, 8]); D["gains"] = di("gains", [2, 128, 4, 8]); D["cwin"] = di("cwin", [2, 128, NJ, 4])
    D["ctab"] = di("ctab", [64, T]); D["stab"] = di("stab", [64, T])
    D["r_A"] = di("r_A", [4, 128, NCH]); D["r_B"] = di("r_B", [4, 128, NCH]); D["r_c"] = di("r_c", [4, 64, NCH])
    D["m_cw"] = di("m_cw", [4, 2, 64, 5]); D["m_gb"] = di("m_gb", [4, 128, 2]); D["ng"] = di("ng", [8, 128, 128])
    D["mask_in"] = di("mask_in", [128, 128]); D["ident_in"] = di("ident_in", [128, 128], BF16)
    D["gb_in"] = di("gb_in", [2, 128, NQ, 24]); D["qg_in"] = di("qg_in", [128, 64]); D["kg_in"] = di("kg_in", [128, 3, 64])
    D["pos_in"] = di("pos_in", [2, 128, 16]); D["tri_in"] = di("tri_in", [2, 128, 128], BF16)
    D["E_in"] = di("E_in", [64, T], BF16); D["OV_in"] = di("OV_in", [128, 2, 64], BF16)
    D["iota_in"] = di("iota_in", [128, 128]); D["thr_in"] = di("thr_in", [128, 2 * NQ]); D["selc_in"] = di("selc_in", [NQ, 128, 2, 64])
    D["out"] = nc.dram_tensor("out", [128, 8, T // 2], F32, kind="ExternalOutput").ap()
    D["sel"] = di("sel", [128, 2]); D["pT1h"] = di("pT1h", [128, 2, T // 2])
    wbf = dsc("wbf", [128, M], BF16)
    D["zA_fm"] = dsc("zA_fm", [12, 128, 3 + T]); D["zA_tm"] = dsc("zA_tm", [T, 2056])
    D["mixT"] = dsc("mixT", [8, 128, 2 + T], BF16); D["h1T"] = dsc("h1T", [128, 8, 2 + T])
    D["z1tm"] = dsc("z1tm", [T, 1584]); D["c2T"] = dsc("c2T", [2, 128, T + 16]); D["oT"] = dsc("oT", [8, 128, 2 + T], BF16)

    def wfn(name):
        off, X, NB = wtab[name]
        return lambda b: wbf[:, off + b * X:off + (b + 1) * X]
    D["w"] = wfn
    with ExitStack() as es:
        k = K(nc, es)
        k.arena_init(ARENA_KIB)
        k.banks_init(8)
        ctx = TPCtx(k)
        k.persist()
        CB = 4096
        st = Rot(k, 3, [128, CB], F32, "st"); ob = Rot(k, 3, [128, CB], BF16, "ob")
        engs = ["dve", "act", "pool"]
        for bi, c0 in enumerate(range(0, M, CB)):
            a = st.get(); b = ob.get()
            k.dma("sp", a[:, :], wsrc_in[:, c0:c0 + CB], writes=[a])
            e = engs[bi % 3]
            if e == "act":
                k.do("act", "copy", [a], [b], out=b[:, :], in_=a[:, :])
            else:
                k.do(e, "tensor_copy", [a], [b], out=b[:, :], in_=a[:, :])
            k.dma("pool" if e != "pool" else "sp", wbf[:, c0:c0 + CB], b[:, :], reads=[b])
        k.phase_reset()
        phase_A(k, ctx, D)
        k.phase_reset()
        phase_P2(k, ctx, D)
        k.phase_reset()
        phase_TP(k, ctx, D, 0)
        for g in range(2):
            k.phase_reset()
            phase_P5(k, ctx, D, g)
        k.phase_reset()
        phase_TP(k, ctx, D, 1)
        k.emit()
    return nc


ARENA_KIB = 190


def p2_consts():
    T = T_SEQ
    half = 32
    inv = 10000.0 ** (-np.arange(half, dtype=np.float64) / half)
    ang = np.arange(T, dtype=np.float64)[None, :] * inv[:, None]
    cos = np.cos(ang); sin = np.sin(ang)
    ctab = np.concatenate([cos, cos], 0).astype(np.float32)
    stab = np.concatenate([-sin, sin], 0).astype(np.float32)
    l = np.arange(128, dtype=np.float64)
    rA = np.zeros((4, 128, NCH), np.float32); rB = np.zeros((4, 128, NCH), np.float32); rc = np.zeros((4, 64, NCH), np.float32)
    for h in range(4):
        lg = np.log1p(-2.0 ** (-5.0 - h))
        rA[h] = np.exp((l + 1) * lg)[:, None]
        rB[h] = (np.exp(-(l + 1) * lg) * 0.125)[:, None]
        rc[h] = np.exp(128 * lg)
    mask = (l[:, None] <= l[None, :]).astype(np.float32)
    ident = np.eye(128, dtype=np.float32).astype(NPBF)
    return dict(ctab=ctab, stab=stab, r_A=rA, r_B=rB, r_c=rc, mask_in=mask, ident_in=ident)


def p5_consts():
    T = T_SEQ; NQ = T // 128
    l = np.arange(128)
    tri = (l[:, None] <= l[None, :]).astype(np.float32)
    triU = 1.0 - tri
    E = (np.arange(T)[None, :] // 64 == np.arange(64)[:, None]).astype(np.float32)
    n = np.arange(256); j = np.arange(64)
    ov = ((16 * n[:, None] < (j[None, :] + 1) * 64) & (16 * n[:, None] + 32 > j[None, :] * 64)).astype(np.float32)
    ov[255] = 0
    OV = np.ascontiguousarray(ov.reshape(2, 128, 64).transpose(1, 0, 2))
    iota = np.tile(l[None, :].astype(np.float32), (128, 1))
    thr = np.zeros((128, 2 * NQ), np.float32)
    for qb in range(NQ):
        for nt in range(2):
            thr[:, 2 * qb + nt] = 16 * (nt * 128 + l) + 31 - 128 * qb
    selc = np.zeros((NQ, 128, 2, 64), np.float32)
    for qb in range(NQ):
        for q in range(128):
            cur = (128 * qb + q) // 64
            mult = np.ones(64, np.float32); add = np.zeros(64, np.float32)
            mult[0] = 0; add[0] = 1e30
            if cur >= 1:
                mult[cur - 1] = 0; add[cur - 1] = 3e30
            mult[cur] = 0; add[cur] = 2e30
            mult[cur + 1:] = 0; add[cur + 1:] = -1e30
            selc[qb, q, 0] = mult; selc[qb, q, 1] = add
    return dict(tri_in=np.stack([tri, triU]).astype(NPBF), E_in=(E * 30000.0).astype(NPBF), OV_in=OV.astype(NPBF), iota_in=iota, thr_in=thr, selc_in=selc)


def kernel(**inp):
    inp = {k_: np.asarray(v) for k_, v in inp.items()}
    x = inp["x"].astype(np.float32); p = inp["p"].astype(np.float32)
    T = T_SEQ; NQ = T // 128
    wt = wlayout(inp)
    wflat, M = wt.flat()
    nc = build_fused(M, wt.tab)
    shared = dict(p2_consts()); shared.update(p5_consts())
    shared["wsrc"] = wflat
    shared["g_ab"] = gain_fm(inp["ab_norm_g"][0])
    gl = []
    for l in range(2):
        nxt = inp["nsa_norm_g"][0] if l == 0 else inp["ple_gate_norm_g"][1]
        gl.append(np.stack([gain_fm(inp["ffn_norm_g"][l]), gain_fm(inp["ple_norm_g"][l]), gain_fm(inp["ple_gate_norm_g"][l]), gain_fm(nxt)], 1))
    shared["gains"] = np.ascontiguousarray(np.stack(gl))
    cwl = []
    for l in range(2):
        cwm = np.concatenate([inp["ffn_conv_w"][l], inp["ffn_conv_b"][l][None]], 0)
        cwl.append(np.ascontiguousarray(cwm.T.reshape(NJ, 128, 4).transpose(1, 0, 2)))
    shared["cwin"] = np.stack(cwl).astype(np.float32)
    conv_w = inp["ab_conv_w"][0]; conv_b = inp["ab_conv_b"][0]
    mcw = []
    for h in range(4):
        qw = np.concatenate([conv_w[:, h * 64:(h + 1) * 64].T, conv_b[h * 64:(h + 1) * 64, None]], 1)
        kw = np.concatenate([conv_w[:, 256 + h * 64:256 + (h + 1) * 64].T, conv_b[256 + h * 64:256 + (h + 1) * 64, None]], 1)
        mcw.append(np.stack([qw, kw]))
    shared["m_cw"] = np.stack(mcw).astype(np.float32)
    shared["m_gb"] = np.stack([np.tile(np.array([[inp["ab_ig_b"][0][h], inp["ab_fg_b"][0][h]]], np.float32), (128, 1)) for h in range(4)])
    shared["ng"] = np.stack([np.tile(inp["ab_ret_norm_g"][0][h * 128:(h + 1) * 128][None], (128, 1)) for h in range(4)] +
                            [np.tile(inp["ab_m_norm_g"][0][h * 128:(h + 1) * 128][None], (128, 1)) for h in range(4)]).astype(np.float32)
    shared["gb_in"] = np.stack([np.broadcast_to(inp["nsa_gate_b"][0][g * 24:(g + 1) * 24][None, None, :], (128, NQ, 24)) for g in range(2)]).astype(np.float32)
    shared["qg_in"] = np.broadcast_to(inp["nsa_q_norm_g"][0][None, :], (128, 64)).astype(np.float32)
    shared["kg_in"] = np.broadcast_to(inp["nsa_k_norm_g"][0][None], (128, 3, 64)).astype(np.float32)
    def pos2(pos):
        return np.ascontiguousarray(pos.reshape(16, 2, 64).transpose(1, 2, 0).reshape(128, 16)).astype(np.float32)
    shared["pos_in"] = np.stack([pos2(inp["nsa_cmp_pos_k"][0]), pos2(inp["nsa_cmp_pos_v"][0])])
    shared = {k_: np.ascontiguousarray(v) for k_, v in shared.items()}
    maps = []
    for c in range(8):
        b = c // 2
        d = dict(shared)
        xt = np.zeros((128, 8, T + 2), np.float32); xt[:, :, 2:] = fm(x[b])
        d["xT"] = xt
        d["pT"] = np.stack([fm(p[l, b]) for l in range(2)])
        hf = c % 2
        d["pT1h"] = fm(p[1, b, hf * (T // 2):(hf + 1) * (T // 2)])
        sel = np.zeros((128, 2), np.float32); sel[:, hf] = 1.0
        d["sel"] = sel
        maps.append(d)
    res = run_spmd(nc, maps)
    out = np.stack([np.concatenate([unfm(np.asarray(res[2 * b]["out"])), unfm(np.asarray(res[2 * b + 1]["out"]))], 0) for b in range(4)])
    return out.astype(np.float32)
```

```python
import math
from contextlib import ExitStack
import numpy as np
import ml_dtypes
import concourse.bass as bass
import concourse.mybir as mybir
from concourse.bass_utils import run_bass_kernel_spmd

F32 = mybir.dt.float32
BF16 = mybir.dt.bfloat16
AF = mybir.ActivationFunctionType
ALU = mybir.AluOpType
AX = mybir.AxisListType
NPBF = ml_dtypes.bfloat16

N_DMA_SEMS = 24


class T:
    __slots__ = ("ap", "w", "r", "name")

    def __init__(self, ap, name=""):
        self.ap = ap
        self.w = None
        self.r = []
        self.name = name

    def __getitem__(self, idx):
        return self.ap[idx]

    @property
    def dep(self):
        return self


class V:
    def __init__(self, parent, ap):
        self.parent = parent; self.ap = ap

    def __getitem__(self, idx):
        return self.ap[idx]

    @property
    def dep(self):
        return self.parent


class Op:
    __slots__ = ("eng", "idx", "fn", "deps", "is_dma", "dsem", "dval", "needed", "cnt")

    def __init__(self, eng, idx, fn, is_dma):
        self.eng = eng; self.idx = idx; self.fn = fn; self.deps = []
        self.is_dma = is_dma; self.dsem = None; self.dval = 0; self.needed = False; self.cnt = 0


class K:
    ENGS = ("pe", "dve", "act", "pool", "sp")

    def __init__(self, nc, es):
        self.nc = nc
        self.es = es
        self.ops = {e: [] for e in self.ENGS}
        self.dma_rr = 0
        self.dma_last = [None] * N_DMA_SEMS
        self.dma_cnt = [0] * N_DMA_SEMS
        self.out_dmas = []
        self.n_t = 0

    def arena_init(self, kib):
        self.arena_n = kib * 512
        self.arena = self.es.enter_context(self.nc.sbuf_tensor("arena", [128, self.arena_n], BF16))
        self.a_off = 0
        self.a_base = 0

    def sb(self, shape, dt, name=None):
        P = shape[0]
        n = 1
        for d in shape[1:]:
            n *= d
        sz = n if dt == BF16 else 2 * n
        sz = (sz + 63) // 64 * 64
        assert self.a_off + sz <= self.arena_n, ("SBUF arena overflow", name, self.a_off, sz, self.arena_n)
        ap = self.arena[0:P, self.a_off:self.a_off + (n if dt == BF16 else 2 * n)]
        self.a_off += sz
        if dt != BF16:
            ap = ap.bitcast(dt)
        if len(shape) == 3:
            ap = ap.rearrange("p (a b) -> p a b", a=shape[1])
        return T(ap, name or "")

    def persist(self):
        self.a_base = self.a_off

    def phase_reset(self):
        self.barrier()
        self.a_off = self.a_base

    def barrier(self):
        lasts = []
        for e in self.ENGS:
            for op in reversed(self.ops[e]):
                if not op.is_dma and op.fn is not None:
                    lasts.append(op); break
        lasts += [d for d in self.dma_last if d is not None]
        for e in self.ENGS:
            b = Op(e, len(self.ops[e]), None, False)
            b.deps = [d for d in lasts if not (d.eng == e and not d.is_dma)]
            self.ops[e].append(b)

    def ps(self, shape, dt=F32, name=None):
        self.n_t += 1
        name = name or f"p{self.n_t}"
        t = self.es.enter_context(self.nc.psum_tensor(name, list(shape), dt))
        return T(t, name)

    def view(self, ap, name=""):
        return T(ap, name)

    def _rec(self, eng, fn, reads, writes, is_dma=False):
        lst = self.ops[eng]
        op = Op(eng, len(lst), fn, is_dma)
        deps = []
        reads = [t.dep for t in reads]; writes = [t.dep for t in writes]
        for t in reads:
            if t.w is not None:
                deps.append((t.w, "raw"))
        for t in writes:
            if t.w is not None:
                deps.append((t.w, "waw"))
            for r in t.r:
                deps.append((r, "war"))
        for d, kind in deps:
            if d is op:
                continue
            if (not d.is_dma) and d.eng == eng and not is_dma:
                if eng == "pe" or kind != "raw":
                    continue
            op.deps.append(d)
        if is_dma:
            i = self.dma_rr; self.dma_rr = (self.dma_rr + 1) % N_DMA_SEMS
            prev = self.dma_last[i]
            if prev is not None:
                op.deps.append(prev)
            self.dma_cnt[i] += 1
            op.dsem = i; op.dval = 16 * self.dma_cnt[i]
            self.dma_last[i] = op
        for t in reads:
            t.r.append(op)
        for t in writes:
            t.w = op; t.r = []
        lst.append(op)
        return op

    def op(self, eng, fn, reads=(), writes=()):
        return self._rec(eng, fn, reads, writes)

    def dma(self, eng, out_ap, in_ap, reads=(), writes=(), is_out=False, **kw):
        def fn(e):
            return e.dma_start(out=out_ap, in_=in_ap, **kw)
        op = self._rec(eng, fn, reads, writes, is_dma=True)
        if is_out:
            self.out_dmas.append(op)
        return op

    def emit(self):
        nc = self.nc
        fin = Op("sp", len(self.ops["sp"]), None, False)
        fin.deps = list(self.out_dmas)
        self.ops["sp"].append(fin)
        for e in self.ENGS:
            for op in self.ops[e]:
                for d in op.deps:
                    d.needed = True
        for e in self.ENGS:
            c = 0
            for op in self.ops[e]:
                if op.needed and not op.is_dma:
                    c += 1
                op.cnt = c
        sems = {e: self.es.enter_context(nc.semaphore(f"s_{e}")) for e in self.ENGS}
        dsems = [self.es.enter_context(nc.semaphore(f"s_dma{i}")) for i in range(N_DMA_SEMS)]
        block = self.es.enter_context(nc.Block())
        ops = self.ops

        def run(ename, eng):
            waited = {}
            for op in ops[ename]:
                need = {}
                for d in op.deps:
                    if d.is_dma:
                        key = ("d", d.dsem); val = d.dval
                    else:
                        key = ("e", d.eng); val = d.cnt
                    if waited.get(key, 0) >= val:
                        continue
                    if need.get(key, 0) < val:
                        need[key] = val
                for key, val in need.items():
                    sem = dsems[key[1]] if key[0] == "d" else sems[key[1]]
                    eng.wait_ge(sem, val)
                    waited[key] = val
                if op.fn is None:
                    continue
                ins = op.fn(eng)
                if op.is_dma:
                    ins.then_inc(dsems[op.dsem], 16)
                elif op.needed:
                    ins.then_inc(sems[ename], 1)

        @block.tensor
        def _(e):
            run("pe", e)

        @block.vector
        def _(e):
            run("dve", e)

        @block.scalar
        def _(e):
            run("act", e)

        @block.gpsimd
        def _(e):
            run("pool", e)

        @block.sync
        def _(e):
            run("sp", e)


    def do(self, eng, method, reads, writes, *a, **kw):
        return self.op(eng, lambda e: getattr(e, method)(*a, **kw), reads, writes)

    def mm(self, out_t, out_ap, l_t, l_ap, r_t, r_ap, start, stop, skip=False):
        return self.op("pe", lambda e: e.matmul(out_ap, lhsT=l_ap, rhs=r_ap, start=start, stop=stop, skip_group_check=skip),
                       reads=[l_t, r_t], writes=[out_t])

    def banks_init(self, n=8):
        if not hasattr(self, "allbanks"):
            self.allbanks = [self.ps([128, 512], F32, name=f"bank{i}") for i in range(8)]
        self.banks = self.allbanks[:n]
        self.bank_i = 0

    def bank_bf(self, i):
        b = self.allbanks[i]
        return V(b, b.ap[:, :].bitcast(BF16))

    def bank(self):
        b = self.banks[self.bank_i]
        self.bank_i = (self.bank_i + 1) % len(self.banks)
        return b


class Rot:
    def __init__(self, k, n, shape, dt, name):
        self.t = [k.sb(shape, dt, name=f"{name}{i}") for i in range(n)]
        self.i = 0

    def get(self):
        t = self.t[self.i]
        self.i = (self.i + 1) % len(self.t)
        return t


def slices512(w0, w1):
    out = []
    s = w0
    while s < w1:
        e = min(s + 512, w1)
        out.append((s, e)); s = e
    return out


EPS = 1e-6

T_SEQ = 4096
NCH = 32
DFF = 2816
NJ = DFF // 128


def run_spmd(nc, in_maps):
    res = run_bass_kernel_spmd(nc, in_maps, core_ids=list(range(len(in_maps))))
    return res.results


def wblocks(W, CB):
    Kd, C = W.shape
    KC = Kd // 128
    NB = (C + CB - 1) // CB
    if NB * CB != C:
        Wp = np.zeros((Kd, NB * CB), W.dtype); Wp[:, :C] = W; W = Wp
    return np.ascontiguousarray(W.reshape(KC, 128, NB, CB).transpose(1, 2, 0, 3)).reshape(128, NB * KC * CB)


def fm(a):
    N, Fd = a.shape
    return np.ascontiguousarray(a.T.reshape(Fd // 128, 128, N).transpose(1, 0, 2))


def unfm(a):
    P, KC, N = a.shape
    return np.ascontiguousarray(a.transpose(2, 1, 0)).reshape(N, KC * P)


def gain_fm(g):
    return np.ascontiguousarray(g.reshape(-1, 128).T.astype(np.float32))


class WTab:
    def __init__(self):
        self.parts = []; self.off = 0; self.tab = {}

    def add(self, name, blk, X):
        self.tab[name] = (self.off, X, blk.shape[1] // X)
        self.parts.append(blk); self.off += blk.shape[1]

    def flat(self):
        M = (self.off + 4095) // 4096 * 4096
        out = np.zeros((128, M), np.float32)
        o = 0
        for p_ in self.parts:
            out[:, o:o + p_.shape[1]] = p_; o += p_.shape[1]
        return out, M


def wlayout(inp):
    wt = WTab()
    w_in = inp["ab_w_in"][0]
    rq, rk, rv, rg, mq, mk, mv, mo, mi, mf = np.split(w_in, np.cumsum([256, 256, 512, 512, 256, 256, 512, 512, 4])[:9].tolist() + [3076], axis=1) \
        if False else (w_in[:, 0:256], w_in[:, 256:512], w_in[:, 512:1024], w_in[:, 1024:1536], w_in[:, 1536:1792], w_in[:, 1792:2048],
                       w_in[:, 2048:2560], w_in[:, 2560:3072], w_in[:, 3072:3076], w_in[:, 3076:3080])
    def sw(w):
        return w.reshape(1024, 4, 2, 32)[:, :, ::-1, :].reshape(1024, 256)
    a_fm = np.concatenate([rq, sw(rq), rk, sw(rk), mq, mk], 1)
    wt.add("a_fm", wblocks(a_fm, 128), 8 * 128)
    a_tm = np.concatenate([rv, rg, mv, mo, mi, mf], 1)
    wt.add("a_tm", wblocks(a_tm, 512), 8 * 512)
    n_in = inp["nsa_w_in"][0]
    n_tm = np.concatenate([n_in[:, 0:1024], n_in[:, 1280:1840]], 1)
    wt.add("n_tm", wblocks(n_tm, 512), 8 * 512)
    wt.add("n_fm", wblocks(n_in[:, 1024:1280], 128), 8 * 128)
    for l, mixn in ((0, "ab_w_out"), (1, "nsa_w_out")):
        wt.add("w_out%d" % l, wblocks(inp[mixn][0], 128), 8 * 128)
        up = inp["ffn_w_up"][l]
        a = up[:, :DFF].reshape(1024, NJ, 128); b = up[:, DFF:].reshape(1024, NJ, 128)
        wt.add("w_up%d" % l, wblocks(np.concatenate([a, b], 2).reshape(1024, NJ * 256), 256), 8 * 256)
        wt.add("w_dn%d" % l, wblocks(inp["ffn_w_down"][l], 128), NJ * 128)
        wt.add("w_ple%d" % l, wblocks(inp["ple_w"][l], 128), 2 * 128)
        wt.add("w_gate%d" % l, wblocks(inp["ple_w_gate"][l], 128), 8 * 128)
    for nm in ("nsa_cmp_w1k", "nsa_cmp_w1v"):
        wt.add(nm, wblocks(inp[nm][0], 256), 16 * 256)
    for nm in ("nsa_cmp_w2k", "nsa_cmp_w2v"):
        wt.add(nm, wblocks(inp[nm][0], 64), 2 * 64)
    return wt


class TPCtx:
    def __init__(self, k):
        self.k = k
        self.ones = k.sb([128, 128], BF16, name="ones_bf")
        self.eps = k.sb([128, 1], F32, name="eps")
        k.do("pool", "memset", [], [self.ones], self.ones[:, :], 1.0)
        k.do("pool", "memset", [], [self.eps], self.eps[:, :], EPS)

    def scratch(self):
        k = self.k
        self.sq = Rot(k, 2, [128, 512], BF16, "sq")
        self.rstd = Rot(k, 2, [128, 512], F32, "rstd")
        self.f32tmp = Rot(k, 2, [128, 512], F32, "f32tmp")

    def rmsnorm(self, src, KC, w0, w1, gain, dst, d0):
        k = self.k
        inv = 1.0 / (KC * 128)
        for (s0, s1) in slices512(w0, w1):
            n = s1 - s0
            bk = k.bank()
            for kc in range(KC):
                sq = self.sq.get()
                k.do("act", "activation", [src], [sq], out=sq[:, :n], in_=src[:, kc, s0:s1], func=AF.Square)
                k.mm(bk, bk[:, :n], self.ones, self.ones[:, :], sq, sq[:, :n], kc == 0, kc == KC - 1)
            r = self.rstd.get()
            k.do("act", "activation", [bk, self.eps], [r], out=r[:, :n], in_=bk[:, :n], func=AF.Ln, bias=self.eps[:, 0:1], scale=inv)
            k.do("act", "activation", [r], [r], out=r[:, :n], in_=r[:, :n], func=AF.Exp, scale=-0.5)
            for kc in range(KC):
                k.do("dve", "scalar_tensor_tensor", [src, gain, r], [dst],
                     out=dst[:, kc, d0 + s0 - w0:d0 + s1 - w0], in0=src[:, kc, s0:s1], scalar=gain[:, kc:kc + 1],
                     in1=r[:, :n], op0=ALU.mult, op1=ALU.mult)


def tokmajor_proj(k, xn, KC, NT, x0, wsrc, CB, C, z_ap, wrot, zrot):
    NB = (C + CB - 1) // CB
    for b in range(NB):
        wt = wrot.get()
        k.dma("sp", wt[:, :KC * CB], wsrc(b), writes=[wt])
        cw = min(CB, C - b * CB)
        for ti in range(NT // 128):
            bk = k.bank()
            for kc in range(KC):
                k.mm(bk, bk[:, :cw], xn, xn[:, kc, x0 + ti * 128:x0 + (ti + 1) * 128], wt, wt[:, kc * CB:kc * CB + cw], kc == 0, kc == KC - 1)
            zt = zrot.get()
            if ti % 2 == 0:
                k.do("act", "copy", [bk], [zt], out=zt[:, :cw], in_=bk[:, :cw])
            else:
                k.do("dve", "tensor_copy", [bk], [zt], out=zt[:, :cw], in_=bk[:, :cw])
            k.dma("pool", z_ap[ti * 128:(ti + 1) * 128, b * CB:b * CB + cw], zt[:, :cw], reads=[zt])


def featmajor_proj(k, xn, KC, NT, x0, wsrc, NCK, dst_fn, wrot, zrot):
    for c in range(NCK):
        wt = wrot.get()
        k.dma("sp", wt[:, :KC * 128], wsrc(c), writes=[wt])
        for (s0, s1) in slices512(0, NT):
            bk = k.bank()
            for kc in range(KC):
                k.mm(bk, bk[:, :s1 - s0], wt, wt[:, kc * 128:(kc + 1) * 128], xn, xn[:, kc, x0 + s0:x0 + s1], kc == 0, kc == KC - 1)
            zt = zrot.get()
            k.do("act", "copy", [bk], [zt], out=zt[:, :s1 - s0], in_=bk[:, :s1 - s0])
            k.dma("pool", dst_fn(c)[:, s0:s1], zt[:, :s1 - s0], reads=[zt])


def gelu_tanh(k, src, dst_ap, dst_t, n, uu, sg_, P=128):
    k.do("pool", "tensor_tensor", [src], [uu], out=uu[:P, :n], in0=src[:P, :n], in1=src[:P, :n], op=ALU.mult)
    k.do("pool", "tensor_scalar", [uu], [uu], out=uu[:P, :n], in0=uu[:P, :n], scalar1=0.044715, scalar2=1.0, op0=ALU.mult, op1=ALU.add)
    k.do("pool", "tensor_tensor", [uu, src], [uu], out=uu[:P, :n], in0=uu[:P, :n], in1=src[:P, :n], op=ALU.mult)
    k.do("act", "activation", [uu], [sg_], out=sg_[:P, :n], in_=uu[:P, :n], func=AF.Sigmoid, scale=1.5957691216057308)
    k.do("dve", "tensor_tensor", [src, sg_], [dst_t], out=dst_ap, in0=src[:P, :n], in1=sg_[:P, :n], op=ALU.mult)


def phase_A(k, ctx, D):
    NT = 1024
    T = T_SEQ
    ctx.scratch()
    h = k.sb([128, 8, NT], F32, name="h"); xn = k.sb([128, 8, NT], BF16, name="xn")
    gt = k.sb([128, 8], F32, name="gA")
    wrot = Rot(k, 2, [128, 4096], BF16, "wr"); zrot = Rot(k, 3, [128, 512], F32, "zr")
    zer = k.sb([128, 8], F32, name="zer")
    k.do("pool", "memset", [], [zer], zer[:, :], 0.0)
    for c in range(12):
        k.dma("pool", D["zA_fm"][c][:, 0:3], zer[:, 0:3], reads=[zer])
    k.dma("sp", gt[:, :], D["g_ab"], writes=[gt])
    for ps in range(T // NT):
        for kc in range(8):
            k.dma("sp" if kc % 2 == 0 else "pool", h[:, kc, :], D["xT"][:, kc, 2 + ps * NT:2 + (ps + 1) * NT], writes=[h])
        ctx.rmsnorm(h, 8, 0, NT, gt, xn, 0)
        tokmajor_proj(k, xn, 8, NT, 0, D["w"]("a_tm"), 512, 2056, D["zA_tm"][ps * NT:(ps + 1) * NT, :], wrot, zrot)
        featmajor_proj(k, xn, 8, NT, 0, D["w"]("a_fm"), 12, (lambda c, ps=ps: D["zA_fm"][c][:, 3 + ps * NT:3 + (ps + 1) * NT]), wrot, zrot)


def phase_P2(k, ctx, D):
    T = T_SEQ
    SEG = 512
    k.banks_init(6)
    pbf = [k.bank_bf(6), k.bank_bf(7)]
    eps = ctx.eps
    mask = k.sb([128, 128], F32, name="mask"); ident = k.sb([128, 128], BF16, name="ident")
    ones32 = k.sb([128, 64], F32, name="ones32")
    k.dma("sp", mask[:, :], D["mask_in"], writes=[mask]); k.dma("sp", ident[:, :], D["ident_in"], writes=[ident])
    k.do("pool", "memset", [], [ones32], ones32[:, :], 1.0)
    stg = Rot(k, 12, [64, SEG + 3], F32, "stg")
    tt = Rot(k, 8, [64, SEG], F32, "tt")
    qT = k.sb([64, T], BF16, name="qT"); kT = k.sb([64, T], BF16, name="kT")
    kt = k.sb([128, NCH, 64], BF16, name="kt")
    v = k.sb([128, NCH, 128], F32, name="v"); vp = k.sb([128, NCH, 129], BF16, name="vp")
    gt = k.sb([128, NCH, 128], F32, name="gt"); oall = k.sb([128, NCH, 129], F32, name="oall")
    res = k.sb([128, NCH, 128], F32, name="res"); resb = k.sb([128, NCH, 128], BF16, name="resb")
    ost = Rot(k, 2, [128, 1024], BF16, "ost")
    A = k.sb([128, NCH], F32, name="A"); Bv = k.sb([128, NCH], F32, name="Bv"); cd = k.sb([64, NCH], F32, name="cd")
    gi = k.sb([128, NCH], F32, name="gi"); gf = k.sb([128, NCH], F32, name="gf"); gb = k.sb([128, 2], F32, name="gb")
    l1 = k.sb([128, NCH], F32, name="l1"); ngb = k.sb([128, 1], F32, name="ngb")
    cw = k.sb([64, 10], F32, name="cw")
    g_t = k.sb([128, 128], F32, name="g_t")
    St = k.sb([64, 129], F32, name="St"); Sb = k.sb([64, 129], BF16, name="Sb"); StC = k.sb([64, 129], F32, name="StC")
    sm = Rot(k, 3, [128, 128], BF16, "sm")
    dn = k.sb([128, NCH], F32, name="dn"); ss = k.sb([128, NCH], F32, name="ss")
    ctb = Rot(k, 3, [64, SEG], F32, "ctb"); stb = Rot(k, 3, [64, SEG], F32, "stb")
    zer = k.sb([128, 8], BF16, name="zerb")
    k.do("pool", "memset", [], [zer], zer[:, :], 0.0)
    for c in range(8):
        k.dma("pool", D["mixT"][c][:, 0:2], zer[:, 0:2], reads=[zer])
    zfm = D["zA_fm"]; ztm = D["zA_tm"]

    def tmcols(col, n):
        return ztm[:, col:col + n].rearrange("(c p) e -> p c e", p=128)

    for slot in range(8):
        is_ret = slot < 4
        j = slot % 4
        rs = slice((j % 2) * 64, (j % 2) * 64 + 64)
        k.dma("sp", v[:, :, :], tmcols((0 if is_ret else 1024) + j * 128, 128), writes=[v])
        k.dma("sp", gt[:, :, :], tmcols((512 if is_ret else 1536) + j * 128, 128), writes=[gt])
        k.dma("sp", g_t[:, :], D["ng"][slot], writes=[g_t])
        if is_ret:
            k.dma("sp", A[:, :], D["r_A"][j], writes=[A]); k.dma("sp", Bv[:, :], D["r_B"][j], writes=[Bv]); k.dma("sp", cd[:, :], D["r_c"][j], writes=[cd])
            for sg in range(T // SEG):
                c_t = ctb.get(); s_t = stb.get()
                k.dma("sp", c_t[:, :], D["ctab"][:, sg * SEG:(sg + 1) * SEG], writes=[c_t])
                k.dma("sp", s_t[:, :], D["stab"][:, sg * SEG:(sg + 1) * SEG], writes=[s_t])
                for (c0, dstT) in ((0, qT), (4, kT)):
                    a = stg.get(); b = stg.get()
                    k.dma("sp", a[:, :SEG], zfm[c0 + j // 2][rs, 3 + sg * SEG:3 + (sg + 1) * SEG], writes=[a])
                    k.dma("sp", b[:, :SEG], zfm[c0 + 2 + j // 2][rs, 3 + sg * SEG:3 + (sg + 1) * SEG], writes=[b])
                    t1 = tt.get(); t2 = tt.get()
                    k.do("dve", "tensor_tensor", [a, c_t], [t1], out=t1[:, :], in0=a[:, :SEG], in1=c_t[:, :], op=ALU.mult)
                    k.do("pool", "tensor_tensor", [b, s_t], [t2], out=t2[:, :], in0=b[:, :SEG], in1=s_t[:, :], op=ALU.mult)
                    k.do("dve", "tensor_tensor", [t1, t2], [dstT], out=dstT[:, sg * SEG:(sg + 1) * SEG], in0=t1[:, :], in1=t2[:, :], op=ALU.add)
        else:
            k.dma("sp", gi[:, :], ztm[:, 2048 + j:2049 + j].rearrange("(c p) o -> p (c o)", p=128), writes=[gi], allow_slow_non_contiguous=True)
            k.dma("sp", gf[:, :], ztm[:, 2052 + j:2053 + j].rearrange("(c p) o -> p (c o)", p=128), writes=[gf], allow_slow_non_contiguous=True)
            k.dma("sp", gb[:, :], D["m_gb"][j], writes=[gb])
            k.dma("sp", cw[:, 0:5], D["m_cw"][j, 0], writes=[cw]); k.dma("sp", cw[:, 5:10], D["m_cw"][j, 1], writes=[cw])
            k.do("dve", "tensor_scalar", [gb], [ngb], out=ngb[:, :], in0=gb[:, 1:2], scalar1=-1.0, scalar2=None, op0=ALU.mult)
            k.do("act", "activation", [gf, ngb], [l1], out=l1[:, :], in_=gf[:, :], func=AF.Exp, bias=ngb[:, 0:1], scale=-1.0)
            k.do("act", "activation", [l1], [l1], out=l1[:, :], in_=l1[:, :], func=AF.Ln, bias=1.0, scale=1.0)
            bk = k.bank()
            k.mm(bk, bk[:, 0:NCH], mask, mask[:, :], l1, l1[:, :], True, True)
            k.mm(bk, bk[0:64, 64:64 + NCH], ones32, ones32[:, :], l1, l1[:, :], True, True)
            k.do("act", "activation", [bk], [A], out=A[:, :], in_=bk[:, 0:NCH], func=AF.Exp, bias=math.log(0.125), scale=-1.0)
            k.do("dve", "tensor_tensor", [gi, bk], [Bv], out=Bv[:, :], in0=gi[:, :], in1=bk[:, 0:NCH], op=ALU.add)
            k.do("act", "activation", [Bv, gb], [Bv], out=Bv[:, :], in_=Bv[:, :], func=AF.Exp, bias=gb[:, 0:1], scale=1.0)
            k.do("act", "activation", [bk], [cd], out=cd[:, :], in_=bk[0:64, 64:64 + NCH], func=AF.Exp, scale=-1.0)
            for sg in range(T // SEG):
                for qk, (c0, dstT) in enumerate(((8, qT), (10, kT))):
                    a = stg.get()
                    k.dma("sp", a[:, :SEG + 3], zfm[c0 + j // 2][rs, sg * SEG:sg * SEG + SEG + 3], writes=[a])
                    t1 = tt.get()
                    o5 = qk * 5
                    k.do("dve", "tensor_scalar", [a, cw], [t1], out=t1[:, :], in0=a[:, 3:SEG + 3], scalar1=cw[:, o5 + 3:o5 + 4],
                         scalar2=cw[:, o5 + 4:o5 + 5], op0=ALU.mult, op1=ALU.add)
                    for tap in (2, 1, 0):
                        k.do("dve", "scalar_tensor_tensor", [a, cw, t1], [t1], out=t1[:, :], in0=a[:, tap:SEG + tap],
                             scalar=cw[:, o5 + tap:o5 + tap + 1], in1=t1[:, :], op0=ALU.mult, op1=ALU.add)
                    k.do("act", "activation", [t1], [dstT], out=dstT[:, sg * SEG:(sg + 1) * SEG], in_=t1[:, :], func=AF.Silu)
        k.do("act", "activation", [gt], [gt], out=gt[:, :, :], in_=gt[:, :, :], func=(AF.Silu if is_ret else AF.Sigmoid))
        for c in range(NCH):
            if c % 4 == 3:
                k.do("dve", "tensor_scalar", [v, Bv], [vp], out=vp[:, c, 0:128], in0=v[:, c, :], scalar1=Bv[:, c:c + 1], scalar2=None, op0=ALU.mult)
            else:
                k.do("act", "activation", [v, Bv], [vp], out=vp[:, c, 0:128], in_=v[:, c, :], func=AF.Copy, scale=Bv[:, c:c + 1])
        k.do("dve", "tensor_copy", [Bv], [vp], out=vp[:, :, 128], in_=Bv[:, :])
        for c in range(NCH):
            pb = pbf[c % 2]
            k.op("pe", (lambda pb=pb, c=c: (lambda e: e.transpose(pb[:, 0:64], kT[:, c * 128:(c + 1) * 128], ident[0:64, 0:64])))(),
                 reads=[kT, ident], writes=[pb])
            k.do("act", "copy", [pb], [kt], out=kt[:, c, :], in_=pb[:, 0:64])
        k.do("dve", "memset", [], [StC], StC[:, :], 0.0)
        k.do("pool", "memset", [], [Sb], Sb[:, :], 0.0)

        def emit_S(c):
            cs = slice(c * 128, (c + 1) * 128)
            b1 = k.bank()
            k.mm(b1, b1[:, 0:128], kT, kT[:, cs], qT, qT[:, cs], True, True)
            s_ = sm.get()
            k.do("dve", "tensor_tensor", [b1, mask], [s_], out=s_[:, :], in0=b1[:, 0:128], in1=mask[:, :], op=ALU.mult)
            return s_
        s_next = emit_S(0)
        for c in range(NCH):
            cs = slice(c * 128, (c + 1) * 128)
            s_ = s_next
            if c + 1 < NCH:
                s_next = emit_S(c + 1)
            if c < NCH - 1:
                b3 = k.bank()
                k.mm(b3, b3[0:64, 0:129], kt, kt[:, c, :], vp, vp[:, c, :], True, True)
            b2 = k.bank()
            k.mm(b2, b2[:, 0:129], s_, s_[:, :], vp, vp[:, c, :], True, False)
            k.mm(b2, b2[:, 0:129], qT, qT[:, cs], Sb, Sb[:, :], False, True)
            k.do("act", "activation", [b2, A], [oall], out=oall[:, c, :], in_=b2[:, 0:129], func=AF.Copy, scale=A[:, c:c + 1])
            if c < NCH - 1:
                k.do("dve", "scalar_tensor_tensor", [b3, cd, StC], [St], out=St[:, :], in0=b3[0:64, 0:129], scalar=cd[:, c:c + 1], in1=StC[:, :],
                     op0=ALU.mult, op1=ALU.add)
                k.do("act", "copy", [St], [Sb], out=Sb[:, :], in_=St[:, :])
                if c + 1 < NCH - 1:
                    k.do("act", "activation", [St, cd], [StC], out=StC[:, :], in_=St[:, :], func=AF.Copy, scale=cd[:, c + 1:c + 2])
        if not is_ret:
            k.do("act", "activation", [oall], [dn], out=dn[:, :], in_=oall[:, :, 128], func=AF.Abs)
            k.do("dve", "tensor_scalar", [dn], [dn], out=dn[:, :], in0=dn[:, :], scalar1=1.0, scalar2=None, op0=ALU.max)
            k.do("dve", "reciprocal", [dn], [dn], out=dn[:, :], in_=dn[:, :])
            for c in range(NCH):
                k.do("pool" if c % 2 else "dve", "tensor_scalar", [oall, dn], [oall], out=oall[:, c, 0:128], in0=oall[:, c, 0:128],
                     scalar1=dn[:, c:c + 1], scalar2=None, op0=ALU.mult)
        k.do("dve", "tensor_tensor", [oall], [v], out=v[:, :, :], in0=oall[:, :, 0:128], in1=oall[:, :, 0:128], op=ALU.mult)
        k.do("dve", "tensor_reduce", [v], [ss], out=ss[:, :], in_=v[:, :, :], axis=AX.X, op=ALU.add)
        k.do("act", "activation", [ss, eps], [ss], out=ss[:, :], in_=ss[:, :], func=AF.Sqrt, bias=eps[:, 0:1], scale=1.0 / 128)
        k.do("dve", "reciprocal", [ss], [ss], out=ss[:, :], in_=ss[:, :])
        for c in range(NCH):
            k.do("dve", "scalar_tensor_tensor", [oall, ss, g_t], [res], out=res[:, c, :], in0=oall[:, c, 0:128], scalar=ss[:, c:c + 1],
                 in1=g_t[:, :], op0=ALU.mult, op1=ALU.mult)
        k.do("dve", "tensor_tensor", [res, gt], [resb], out=resb[:, :, :], in0=res[:, :, :], in1=gt[:, :, :], op=ALU.mult)
        mch = j if is_ret else 4 + j
        for c in range(NCH):
            pb = pbf[(c // 8) % 2]
            k.op("pe", (lambda pb=pb, c=c: (lambda e: e.transpose(pb[:, (c % 8) * 128:(c % 8 + 1) * 128], resb[:, c, :], ident[:, :])))(),
                 reads=[resb, ident], writes=[pb])
            if c % 8 == 7:
                o_ = ost.get()
                k.do("act", "copy", [pb], [o_], out=o_[:, :], in_=pb[:, 0:1024])
                k.dma("sp", D["mixT"][mch][:, 2 + (c - 7) * 128:2 + (c + 1) * 128], o_[:, :], reads=[o_])


def phase_TP(k, ctx, D, layer):
    NT = 1024
    W = NT + 2
    T = T_SEQ
    ctx.scratch()
    k.banks_init(8)
    xsrc = D["xT"] if layer == 0 else D["h1T"]
    msrc = D["mixT"] if layer == 0 else D["oT"]
    wsrc = D["w"]
    h = k.sb([128, 8, W], F32, name="h")
    mixb = k.sb([128, 8, W], BF16, name="mixb")
    xn = k.sb([128, 8, W], BF16, name="xn")
    big = k.sb([128, NJ * NT], BF16, name="big")
    g = V(big, big.ap[:, :].rearrange("p (j n) -> p j n", j=NJ))
    e = V(big, big.ap[:, 0:2 * 8 * W].bitcast(F32).rearrange("p (c w) -> p c w", c=8))
    pst = V(big, big.ap[:, 2 * 8 * W:2 * 8 * W + 2 * 2 * NT].bitcast(F32).rearrange("p (c w) -> p c w", c=2))
    pb = k.sb([128, 2, NT], BF16, name="pb")
    gn = k.sb([128, 4, 8], F32, name="gn"); cw = k.sb([128, NJ, 4], F32, name="cw")
    wrot = Rot(k, 2, [128, 4096], BF16, "wr")
    zrot = Rot(k, 3, [128, 512], F32, "zr")
    a_sb = Rot(k, 2, [128, W], F32, "a_sb"); cc = Rot(k, 2, [128, NT], F32, "cc"); b_sb = Rot(k, 2, [128, NT], BF16, "b_sb")
    uur = Rot(k, 2, [128, NT], F32, "uu"); sgr = Rot(k, 2, [128, NT], F32, "sg")
    k.dma("sp", gn[:, :, :], D["gains"][layer], writes=[gn]); k.dma("sp", cw[:, :, :], D["cwin"][layer], writes=[cw])
    gviews = [V(gn, gn.ap[:, i, :]) for i in range(4)]
    if layer == 0:
        zer = k.sb([128, 16], F32, name="zer")
        k.do("pool", "memset", [], [zer], zer[:, :], 0.0)
        k.dma("pool", D["h1T"][:, :, 0:2], zer.ap[:, 0:16].rearrange("p (c w) -> p c w", c=8), reads=[zer])

    split = (layer == 1)
    if split:
        sel = k.sb([128, 2], F32, name="sel")
        k.dma("sp", sel[:, :], D["sel"], writes=[sel])
    for ps in range(2 if split else T // NT):
        t0 = ps * NT
        for kc in range(8):
            k.dma("sp", h[:, kc, :], xsrc[:, kc, t0:t0 + W], writes=[h])
            k.dma("sp" if split else "pool", mixb[:, kc, :], msrc[kc][:, t0:t0 + W], writes=[mixb])
        if split:
            tB = T // 2 + t0
            for kc in range(8):
                k.dma("sp", e[:, kc, :], xsrc[:, kc, tB:tB + W], writes=[e])
                k.dma("sp", xn[:, kc, :], msrc[kc][:, tB:tB + W], writes=[xn])
            for kc in range(8):
                k.do("dve", "tensor_scalar", [h, sel], [h], out=h[:, kc, :], in0=h[:, kc, :], scalar1=sel[:, 0:1], scalar2=None, op0=ALU.mult)
                k.do("dve", "scalar_tensor_tensor", [e, sel, h], [h], out=h[:, kc, :], in0=e[:, kc, :], scalar=sel[:, 1:2], in1=h[:, kc, :],
                     op0=ALU.mult, op1=ALU.add)
                k.do("dve", "tensor_scalar", [mixb, sel], [mixb], out=mixb[:, kc, :], in0=mixb[:, kc, :], scalar1=sel[:, 0:1], scalar2=None, op0=ALU.mult)
                k.do("dve", "scalar_tensor_tensor", [xn, sel, mixb], [mixb], out=mixb[:, kc, :], in0=xn[:, kc, :], scalar=sel[:, 1:2], in1=mixb[:, kc, :],
                     op0=ALU.mult, op1=ALU.add)
        for c in range(8):
            wt = wrot.get()
            k.dma("sp", wt[:, :1024], wsrc("w_out%d" % layer)(c), writes=[wt])
            for (s0, s1) in slices512(0, W):
                bk = k.bank()
                for kc in range(8):
                    k.mm(bk, bk[:, :s1 - s0], wt, wt[:, kc * 128:(kc + 1) * 128], mixb, mixb[:, kc, s0:s1], kc == 0, kc == 7)
                k.do("dve", "tensor_tensor", [h, bk], [h], out=h[:, c, s0:s1], in0=h[:, c, s0:s1], in1=bk[:, :s1 - s0], op=ALU.add)
        ctx.rmsnorm(h, 8, 0, W, gviews[0], xn, 0)
        for j in range(NJ):
            wt = wrot.get()
            k.dma("sp", wt[:, :2048], wsrc("w_up%d" % layer)(j), writes=[wt])
            a = a_sb.get(); bb = b_sb.get()
            for (s0, s1) in slices512(0, W):
                bk = k.bank()
                for kc in range(8):
                    k.mm(bk, bk[:, :s1 - s0], wt, wt[:, kc * 256:kc * 256 + 128], xn, xn[:, kc, s0:s1], kc == 0, kc == 7)
                k.do("act", "copy", [bk], [a], out=a[:, s0:s1], in_=bk[:, :s1 - s0])
            for (s0, s1) in slices512(2, W):
                bk = k.bank()
                for kc in range(8):
                    k.mm(bk, bk[:, :s1 - s0], wt, wt[:, kc * 256 + 128:kc * 256 + 256], xn, xn[:, kc, s0:s1], kc == 0, kc == 7)
                k.do("act", "copy", [bk], [bb], out=bb[:, s0 - 2:s1 - 2], in_=bk[:, :s1 - s0])
            c_ = cc.get(); u_ = uur.get(); sg_ = sgr.get()
            k.do("pool", "tensor_scalar", [a, cw], [c_], out=c_[:, :], in0=a[:, 2:W], scalar1=cw[:, j, 2:3], scalar2=cw[:, j, 3:4],
                 op0=ALU.mult, op1=ALU.add)
            k.do("dve", "scalar_tensor_tensor", [a, cw, c_], [c_], out=c_[:, :], in0=a[:, 1:W - 1], scalar=cw[:, j, 1:2], in1=c_[:, :],
                 op0=ALU.mult, op1=ALU.add)
            k.do("dve", "scalar_tensor_tensor", [a, cw, c_], [c_], out=c_[:, :], in0=a[:, 0:W - 2], scalar=cw[:, j, 0:1], in1=c_[:, :],
                 op0=ALU.mult, op1=ALU.add)
            k.do("act", "activation", [c_], [u_], out=u_[:, :], in_=c_[:, :], func=AF.Square, scale=math.sqrt(0.044715))
            k.do("dve", "scalar_tensor_tensor", [u_, c_], [u_], out=u_[:, :], in0=u_[:, :], scalar=1.0, in1=c_[:, :], op0=ALU.add, op1=ALU.mult)
            k.do("act", "activation", [u_], [sg_], out=sg_[:, :], in_=u_[:, :], func=AF.Sigmoid, scale=1.5957691216057308)
            k.do("dve", "tensor_tensor", [c_, sg_], [g], out=g[:, j, :], in0=c_[:, :], in1=sg_[:, :], op=ALU.mult)
            k.do("dve", "tensor_tensor", [g, bb], [g], out=g[:, j, :], in0=g[:, j, :], in1=bb[:, :], op=ALU.mult)
        for c in range(8):
            wt = wrot.get()
            k.dma("sp", wt[:, :NJ * 128], wsrc("w_dn%d" % layer)(c), writes=[wt])
            for (s0, s1) in slices512(0, NT):
                bk = k.bank()
                for j in range(NJ):
                    k.mm(bk, bk[:, :s1 - s0], wt, wt[:, j * 128:(j + 1) * 128], g, g[:, j, s0:s1], j == 0, j == NJ - 1)
                k.do("dve", "tensor_tensor", [h, bk], [h], out=h[:, c, 2 + s0:2 + s1], in0=h[:, c, 2 + s0:2 + s1], in1=bk[:, :s1 - s0], op=ALU.add)
        for kc in range(2):
            k.dma("pool", pst[:, kc, :], (D["pT1h"][:, kc, t0:t0 + NT] if split else D["pT"][layer][:, kc, t0:t0 + NT]), writes=[pst])
        for kc in range(2):
            k.do("act", "copy", [pst], [pb], out=pb[:, kc, :], in_=pst[:, kc, :])
        for c in range(8):
            wt = wrot.get()
            k.dma("sp", wt[:, :256], wsrc("w_ple%d" % layer)(c), writes=[wt])
            for (s0, s1) in slices512(0, NT):
                bk = k.bank()
                for kc in range(2):
                    k.mm(bk, bk[:, :s1 - s0], wt, wt[:, kc * 128:(kc + 1) * 128], pb, pb[:, kc, s0:s1], kc == 0, kc == 1)
                k.do("act", "copy", [bk], [e], out=e[:, c, s0:s1], in_=bk[:, :s1 - s0])
        ctx.rmsnorm(e, 8, 0, NT, gviews[1], e, 0)
        ctx.rmsnorm(h, 8, 2, W, gviews[2], xn, 2)
        for c in range(8):
            wt = wrot.get()
            k.dma("sp", wt[:, :1024], wsrc("w_gate%d" % layer)(c), writes=[wt])
            for (s0, s1) in slices512(2, W):
                bk = k.bank()
                for kc in range(8):
                    k.mm(bk, bk[:, :s1 - s0], wt, wt[:, kc * 128:(kc + 1) * 128], xn, xn[:, kc, s0:s1], kc == 0, kc == 7)
                t_ = ctx.f32tmp.get()
                k.do("act", "activation", [bk], [t_], out=t_[:, :s1 - s0], in_=bk[:, :s1 - s0], func=AF.Sigmoid)
                k.do("dve", "tensor_tensor", [t_, e], [t_], out=t_[:, :s1 - s0], in0=t_[:, :s1 - s0], in1=e[:, c, s0 - 2:s1 - 2], op=ALU.mult)
                k.do("dve", "tensor_tensor", [h, t_], [h], out=h[:, c, s0:s1], in0=h[:, c, s0:s1], in1=t_[:, :s1 - s0], op=ALU.add)
        if layer == 0:
            for kc in range(8):
                k.dma("pool", D["h1T"][:, kc, 2 + t0:2 + t0 + NT], h[:, kc, 2:W], reads=[h])
            ctx.rmsnorm(h, 8, 2, W, gviews[3], xn, 2)
            tokmajor_proj(k, xn, 8, NT, 2, wsrc("n_tm"), 512, 1584, D["z1tm"][t0:t0 + NT, :], wrot, zrot)
            featmajor_proj(k, xn, 8, NT, 2, wsrc("n_fm"), 2, (lambda c, t0=t0: D["c2T"][c][:, t0:t0 + NT]), wrot, zrot)
        else:
            for kc in range(8):
                k.dma("pool", D["out"][:, kc, t0:t0 + NT], h[:, kc, 2:W], reads=[h], is_out=True)


def phase_P5(k, ctx, D, g):
    T = T_SEQ
    NQ = T // 128
    def kv4(i):
        return D["z1tm"][:, 1024 + i * 128 + g * 64:1024 + i * 128 + (g + 1) * 64].rearrange("(c p) d -> p c d", p=128)
    if g == 0:
        zer = k.sb([128, 8], BF16, name="zerb")
        k.do("pool", "memset", [], [zer], zer[:, :], 0.0)
        for c in range(8):
            k.dma("pool", D["oT"][c][:, 0:2], zer[:, 0:2], reads=[zer])
    SC = 0.125
    k.banks_init(4)
    O = [k.allbanks[4], k.allbanks[5]]
    IMP = k.allbanks[6]
    pbf = k.bank_bf(7)
    ident = k.sb([128, 128], BF16, name="ident"); tri = k.sb([128, 2, 128], BF16, name="tri")
    OV = k.sb([128, 2, 64], BF16, name="OV"); iota = k.sb([128, 128], F32, name="iota"); thr = k.sb([128, 2 * NQ], F32, name="thr")
    qg = k.sb([128, 64], F32, name="qg"); kg = k.sb([128, 3, 64], F32, name="kg")
    eps = ctx.eps
    k.dma("sp", ident[:, :], D["ident_in"], writes=[ident])
    for i in range(2):
        k.dma("sp", tri[:, i, :], D["tri_in"][i], writes=[tri])
    k.dma("sp", OV[:, :, :], D["OV_in"], writes=[OV]); k.dma("sp", iota[:, :], D["iota_in"], writes=[iota]); k.dma("sp", thr[:, :], D["thr_in"], writes=[thr])
    k.dma("sp", qg[:, :], D["qg_in"], writes=[qg]); k.dma("sp", kg[:, :, :], D["kg_in"], writes=[kg])
    KE = k.sb([128, T], BF16, name="KE"); KwT = k.sb([64, T], BF16, name="KwT")
    k.dma("pool", KE[64:128, :], D["E_in"], writes=[KE])
    VsA = k.sb([128, NQ, 65], BF16, name="VsA"); VwA = k.sb([128, NQ, 65], BF16, name="VwA")
    KcT = k.sb([64, 256], BF16, name="KcT"); VcA = k.sb([128, 2, 65], BF16, name="VcA")
    raw = k.sb([128, NQ, 64], F32, name="raw"); sqr = k.sb([128, NQ, 64], F32, name="sqr")
    ss = k.sb([128, NQ], F32, name="ss"); kn = k.sb([128, NQ, 64], BF16, name="kn")
    for idx, (src_i, dstT, gi_) in enumerate(((0, KE, 1), (2, KwT, 2))):
        k.dma("sp", raw[:, :, :], kv4(src_i), writes=[raw])
        k.do("dve", "tensor_tensor", [raw], [sqr], out=sqr[:, :, :], in0=raw[:, :, :], in1=raw[:, :, :], op=ALU.mult)
        k.do("dve", "tensor_reduce", [sqr], [ss], out=ss[:, :], in_=sqr[:, :, :], axis=AX.X, op=ALU.add)
        k.do("act", "activation", [ss, eps], [ss], out=ss[:, :], in_=ss[:, :], func=AF.Sqrt, bias=eps[:, 0:1], scale=1.0 / 64)
        k.do("dve", "reciprocal", [ss], [ss], out=ss[:, :], in_=ss[:, :])
        for c in range(NQ):
            k.do("dve", "scalar_tensor_tensor", [raw, ss, kg], [kn], out=kn[:, c, :], in0=raw[:, c, :], scalar=ss[:, c:c + 1],
                 in1=kg[:, gi_, :], op0=ALU.mult, op1=ALU.mult)
        for c in range(NQ):
            k.op("pe", (lambda c=c: (lambda e: e.transpose(pbf[0:64, (c % 8) * 128:(c % 8 + 1) * 128], kn[:, c, :], ident[:, :])))(),
                 reads=[kn, ident], writes=[pbf])
            if c % 8 == 7:
                k.do("act", "copy", [pbf], [dstT], out=dstT[0:64, (c - 7) * 128:(c + 1) * 128], in_=pbf[0:64, 0:1024])
    for (src_i, dstV) in ((1, VsA), (3, VwA)):
        k.dma("sp", raw[:, :, :], kv4(src_i), writes=[raw])
        k.do("dve", "tensor_copy", [raw], [dstV], out=dstV[:, :, 0:64], in_=raw[:, :, :])
        k.do("pool", "memset", [], [dstV], dstV[:, :, 64:65], 1.0)
    x2 = k.sb([128, T + 16], F32, name="x2"); x2b = k.sb([128, T + 16], BF16, name="x2b")
    w1 = k.sb([128, 16 * 256], BF16, name="w1"); w2 = k.sb([128, 2 * 64], BF16, name="w2")
    posf = k.sb([128, 16], F32, name="posf"); posb = k.sb([128, 16], BF16, name="posb")
    c0 = k.sb([128, 2], F32, name="c0"); Hx = k.sb([128, 256], F32, name="Hx"); Hg = k.sb([128, 2, 256], BF16, name="Hg")
    uu = k.sb([128, 512], F32, name="uu"); sg_ = k.sb([128, 512], F32, name="sg")
    kc_f = k.sb([128, 64], F32, name="kc_f"); kc_s = k.sb([128, 64], F32, name="kc_s"); ss1 = k.sb([128, 1], F32, name="ss1")
    Mb = k.sb([128, 128], BF16, name="Mb")
    k.do("pool", "memset", [], [Mb], Mb[:, :], 0.0)
    k.do("pool", "memset", [], [VcA], VcA[:, :, :], 0.0)
    k.do("pool", "memset", [], [VcA], VcA[:, :, 64:65], 1.0)
    for ci in range(2):
        k.do("pool", "memset", [], [x2], x2[:, T - 16:T + 16], 0.0)
        k.dma("sp", x2[0:64, 0:T], D["c2T"][ci][g * 64:(g + 1) * 64, 0:T], writes=[x2]); k.dma("pool", x2[64:128, 0:T - 1], D["c2T"][ci][g * 64:(g + 1) * 64, 1:T], writes=[x2])
        k.dma("sp", w1[:, :], D["w"]("nsa_cmp_w1k" if ci == 0 else "nsa_cmp_w1v")(0), writes=[w1]); k.dma("sp", w2[:, :], D["w"]("nsa_cmp_w2k" if ci == 0 else "nsa_cmp_w2v")(0), writes=[w2]); k.dma("sp", posf[:, :], D["pos_in"][ci], writes=[posf])
        k.do("dve", "tensor_copy", [x2], [x2b], out=x2b[:, 0:2048], in_=x2[:, 0:2048])
        k.do("act", "copy", [x2], [x2b], out=x2b[:, 2048:T + 16], in_=x2[:, 2048:T + 16])
        k.do("dve", "tensor_copy", [posf], [posb], out=posb[:, :], in_=posf[:, :])
        for jc in range(2):
            bk = k.bank(); bk2 = k.bank()
            for m in range(16):
                rhs = x2b.ap[:, 2 * m:2 * m + 16 * 255].rearrange("p (n s) -> p n s", s=16)[:, :, 0]
                k.mm(bk, bk[:, 0:255], w1, w1[:, m * 256 + jc * 128:m * 256 + (jc + 1) * 128], x2b, rhs, m == 0, m == 15)
            for m in range(16):
                k.mm(bk2, bk2[:, 0:1], w1, w1[:, m * 256 + jc * 128:m * 256 + (jc + 1) * 128], posb, posb[:, m:m + 1], m == 0, m == 15)
            k.do("act", "copy", [bk2], [c0], out=c0[:, jc:jc + 1], in_=bk2[:, 0:1])
            k.do("act", "activation", [bk, c0], [Hx], out=Hx[:, 0:255], in_=bk[:, 0:255], func=AF.Identity, bias=c0[:, jc:jc + 1], scale=1.0)
            gelu_tanh(k, Hx, Hg[:, jc, 0:255], Hg, 255, uu, sg_)
        for nt, cnt in ((0, 128), (1, 127)):
            bk = k.bank()
            for jc in range(2):
                k.mm(bk, bk[0:cnt, 0:64], Hg, Hg[:, jc, nt * 128:nt * 128 + cnt], w2, w2[:, jc * 64:(jc + 1) * 64], jc == 0, jc == 1)
            if ci == 0:
                k.do("act", "copy", [bk], [kc_f], out=kc_f[0:cnt, :], in_=bk[0:cnt, 0:64])
                k.do("dve", "tensor_tensor", [kc_f], [kc_s], out=kc_s[0:cnt, :], in0=kc_f[0:cnt, :], in1=kc_f[0:cnt, :], op=ALU.mult)
                k.do("dve", "tensor_reduce", [kc_s], [ss1], out=ss1[0:cnt, :], in_=kc_s[0:cnt, :], axis=AX.X, op=ALU.add)
                k.do("act", "activation", [ss1, eps], [ss1], out=ss1[0:cnt, :], in_=ss1[0:cnt, :], func=AF.Sqrt, bias=eps[0:cnt, 0:1], scale=1.0 / 64)
                k.do("dve", "reciprocal", [ss1], [ss1], out=ss1[0:cnt, :], in_=ss1[0:cnt, :])
                k.do("dve", "scalar_tensor_tensor", [kc_f, ss1, kg], [kn], out=kn[0:cnt, 0, :], in0=kc_f[0:cnt, :], scalar=ss1[0:cnt, 0:1],
                     in1=kg[0:cnt, 0, :], op0=ALU.mult, op1=ALU.mult)
                k.op("pe", (lambda cnt=cnt: (lambda e: e.transpose(pbf[0:64, 0:cnt], kn[0:cnt, 0, :], ident[0:cnt, 0:cnt])))(),
                     reads=[kn, ident], writes=[pbf])
                k.do("act", "copy", [pbf], [KcT], out=KcT[0:64, nt * 128:nt * 128 + cnt], in_=pbf[0:64, 0:cnt])
            else:
                k.do("act", "copy", [bk], [VcA], out=VcA[0:cnt, nt, 0:64], in_=bk[0:cnt, 0:64])
    gts = k.sb([128, NQ, 24], F32, name="gts"); gbt = k.sb([128, NQ, 24], F32, name="gbt")
    k.dma("sp", gts[:, :, :], D["z1tm"][:, 1536 + g * 24:1536 + (g + 1) * 24].rearrange("(c p) d -> p c d", p=128), writes=[gts]); k.dma("sp", gbt[:, :, :], D["gb_in"][g], writes=[gbt])
    k.do("dve", "tensor_tensor", [gts, gbt], [gts], out=gts[:, :, :], in0=gts[:, :, :], in1=gbt[:, :, :], op=ALU.add)
    k.do("act", "activation", [gts], [gts], out=gts[:, :, :], in_=gts[:, :, :], func=AF.Sigmoid)
    LOOK = 2
    qblk = Rot(k, 2, [128, 512], F32, "qblk"); qsq = k.sb([128, 512], F32, name="qsq"); ss8 = k.sb([128, 8], F32, name="ss8")
    qn = Rot(k, 2, [128, 512], BF16, "qn"); QMr = Rot(k, 4, [128, 1024], BF16, "QM")
    oacc = Rot(k, 4, [128, 512], F32, "oacc"); Pt = Rot(k, LOOK + 3, [128, 512], BF16, "Pt")
    selc = Rot(k, 3, [128, 2, 64], F32, "selc")
    rc4 = Rot(k, 2, [128, 4], F32, "rc4"); rc8r = Rot(k, 3, [128, 8], F32, "rc8"); cf4 = Rot(k, 2, [128, 4], F32, "cf4")
    impt = k.sb([128, 64], F32, name="impt"); sc1 = k.sb([128, 64], F32, name="sc1"); sc2 = k.sb([128, 64], F32, name="sc2")
    m8a = k.sb([128, 8], F32, name="m8a"); m8b = k.sb([128, 8], F32, name="m8b")
    mkc = Rot(k, 4, [128, 128], BF16, "mkc"); ftmp = Rot(k, 3, [128, 256], F32, "ftmp"); imptmp = k.sb([128, 512], F32, name="imptmp")
    obf = Rot(k, 3, [128, 512], BF16, "obf"); ostage = Rot(k, 2, [128, 4, 1024], BF16, "ostage")
    osg_box = [None]
    Mbr = Rot(k, 3, [128, 128], BF16, "Mbr")
    for t_ in Mbr.t:
        k.do("pool", "memset", [], [t_], t_[:, :], 0.0)

    def prep(qb):
        st = dict(qb=qb, QM=QMr.get(), oa=oacc.get(), sl=selc.get(), rc8=rc8r.get())
        st["QMhi"] = type(st["QM"])(st["QM"].ap, "QMhi")
        qt = qblk.get(); qnt = qn.get(); QM = st["QM"]
        k.dma("sp", qt[:, :], D["z1tm"][qb * 128:(qb + 1) * 128, g * 512:(g + 1) * 512], writes=[qt])
        k.dma("sp", st["sl"][:, :, :], D["selc_in"][qb], writes=[st["sl"]])
        k.do("pool", "tensor_tensor", [qt], [qsq], out=qsq[:, :], in0=qt[:, :], in1=qt[:, :], op=ALU.mult)
        k.do("dve", "tensor_reduce", [qsq], [ss8], out=ss8[:, :], in_=qsq.ap[:, :].rearrange("p (h d) -> p h d", h=8), axis=AX.X, op=ALU.add)
        k.do("act", "activation", [ss8, eps], [ss8], out=ss8[:, :], in_=ss8[:, :], func=AF.Ln, bias=eps[:, 0:1], scale=1.0 / 64)
        k.do("act", "activation", [ss8], [ss8], out=ss8[:, :], in_=ss8[:, :], func=AF.Exp, scale=-0.5)
        k.do("dve", "tensor_tensor", [qt, ss8], [qsq], out=qsq.ap[:, :].rearrange("p (h d) -> p h d", h=8),
             in0=qt.ap[:, :].rearrange("p (h d) -> p h d", h=8), in1=ss8.ap[:, :].unsqueeze(2).to_broadcast([128, 8, 64]), op=ALU.mult)
        k.do("pool", "tensor_tensor", [qsq, qg], [qnt], out=qnt.ap[:, :].rearrange("p (h d) -> p h d", h=8),
             in0=qsq.ap[:, :].rearrange("p (h d) -> p h d", h=8), in1=qg.ap[:, :].unsqueeze(1).to_broadcast([128, 8, 64]), op=ALU.mult)
        st["qnt"] = qnt
        return st

    def prep_pe(st):
        qnt = st["qnt"]; QM = st["QM"]
        for h in range(8):
            k.op("pe", (lambda h=h, qnt=qnt: (lambda e: e.transpose(pbf[0:64, h * 128:(h + 1) * 128], qnt[:, h * 64:(h + 1) * 64], ident[:, :])))(),
                 reads=[qnt, ident], writes=[pbf])
        k.do("dve", "tensor_copy", [pbf], [QM], out=QM[0:64, :], in_=pbf[0:64, 0:1024])

    def finalize(st, br, hh):
        oa = st["oa"]; qb = st["qb"]; rc8 = st["rc8"]
        Ob = O[hh]
        c4 = cf4.get()
        den = Ob.ap[:, 0:260].rearrange("p (h c) -> p h c", c=65)[:, :, 64]
        O4 = Ob.ap[:, 0:260].rearrange("p (h c) -> p h c", c=65)[:, :, 0:64]
        oa4 = oa.ap[:, hh * 256:(hh + 1) * 256].rearrange("p (h d) -> p h d", h=4)
        if br == 0:
            r4 = V(rc8, rc8.ap[:, hh * 4:(hh + 1) * 4])
            k.do("dve", "tensor_scalar", [Ob], [r4], out=r4[:, :], in0=den, scalar1=1e-30, scalar2=None, op0=ALU.max)
            k.do("dve", "reciprocal", [r4], [r4], out=r4[:, :], in_=r4[:, :])
        else:
            r4 = rc4.get()
            k.do("dve", "reciprocal", [Ob], [r4], out=r4[:, :], in_=den)
        gap = gts.ap[:, qb, :].rearrange("p (h b) -> p h b", b=3)[:, hh * 4:(hh + 1) * 4, br]
        k.do("dve", "tensor_tensor", [r4, gts], [c4], out=c4[:, :], in0=r4[:, :], in1=gap, op=ALU.mult)
        c4b = c4.ap[:, :].unsqueeze(2).to_broadcast([128, 4, 64])
        if br == 0:
            k.do("dve", "tensor_tensor", [Ob, c4], [oa], out=oa4, in0=O4, in1=c4b, op=ALU.mult)
        else:
            t4 = ftmp.get()
            t4v = t4.ap[:, :].rearrange("p (h d) -> p h d", h=4)
            k.do("dve", "tensor_tensor", [Ob, c4], [t4], out=t4v, in0=O4, in1=c4b, op=ALU.mult)
            k.do("pool", "tensor_tensor", [oa, t4], [oa], out=oa4, in0=oa4, in1=t4v, op=ALU.add)

    def select(st):
        rc8 = st["rc8"]; sl = st["sl"]
        Mb = Mbr.get(); st["Mb"] = Mb
        k.do("dve", "tensor_tensor", [IMP, rc8], [imptmp], out=imptmp.ap[:, :].rearrange("p (h j) -> p h j", h=8),
             in0=IMP.ap[:, :].rearrange("p (h j) -> p h j", h=8), in1=rc8.ap[:, :].unsqueeze(2).to_broadcast([128, 8, 64]), op=ALU.mult)
        k.do("dve", "tensor_reduce", [imptmp], [impt], out=impt[:, :], in_=imptmp.ap[:, :].rearrange("p (h j) -> p j h", h=8), axis=AX.X, op=ALU.add)
        k.do("dve", "tensor_tensor", [impt, sl], [sc1], out=sc1[:, :], in0=impt[:, :], in1=sl[:, 0, :], op=ALU.mult)
        k.do("dve", "tensor_tensor", [sc1, sl], [sc1], out=sc1[:, :], in0=sc1[:, :], in1=sl[:, 1, :], op=ALU.add)
        k.do("dve", "max", [sc1], [m8a], out=m8a[:, :], in_=sc1[:, :])
        k.do("dve", "tensor_scalar", [sc1, m8a], [sc2], out=sc2[:, :], in0=sc1[:, :], scalar1=m8a[:, 7:8], scalar2=-3e38, op0=ALU.is_ge, op1=ALU.mult)
        k.do("dve", "tensor_tensor", [sc2, sc1], [sc2], out=sc2[:, :], in0=sc2[:, :], in1=sc1[:, :], op=ALU.add)
        k.do("dve", "max", [sc2], [m8b], out=m8b[:, :], in_=sc2[:, :])
        k.do("dve", "tensor_scalar", [sc1, m8b], [Mb], out=Mb[:, 64:128], in0=sc1[:, :], scalar1=m8b[:, 7:8], scalar2=-1.0, op0=ALU.is_ge, op1=ALU.add)

    def select_pe(st):
        QM = st["QM"]; QMhi = st["QMhi"]; Mb = st["Mb"]
        k.op("pe", (lambda Mb=Mb: (lambda e: e.transpose(pbf[:, 0:128], Mb[:, :], ident[:, :])))(), reads=[Mb, ident], writes=[pbf])
        k.do("act", "copy", [pbf], [QMhi], out=QM.ap[64:128, :].rearrange("p (h q) -> p h q", h=8),
             in_=pbf.ap[64:128, 0:128].unsqueeze(1).to_broadcast([64, 8, 128]))

    pending_out = []

    def output_pre(st):
        ob_ = obf.get(); st["ob"] = ob_
        k.do("pool", "tensor_copy", [st["oa"]], [ob_], out=ob_[:, :], in_=st["oa"][:, :])
        pending_out.append(st)

    def flush_out():
        while pending_out:
            output_pe(pending_out.pop(0))

    def output_pe(st):
        qb = st["qb"]; ob_ = st["ob"]
        for c4_ in range(4):
            k.op("pe", (lambda c4_=c4_, ob_=ob_: (lambda e: e.transpose(pbf[:, c4_ * 128:(c4_ + 1) * 128], ob_[:, c4_ * 128:(c4_ + 1) * 128], ident[:, :])))(),
                 reads=[ob_, ident], writes=[pbf])
        if qb % 8 == 0:
            osg_box[0] = ostage.get()
        osg = osg_box[0]
        k.do("dve", "tensor_copy", [pbf], [osg], out=osg[:, :, (qb % 8) * 128:(qb % 8 + 1) * 128],
             in_=pbf.ap[:, 0:512].rearrange("p (c q) -> p c q", c=4))
        if qb % 8 == 7:
            for c4_ in range(4):
                k.dma("sp", D["oT"][g * 4 + c4_][:, 2 + (qb - 7) * 128:2 + (qb + 1) * 128], osg[:, c4_, :], reads=[osg])

    jobs = []

    def add_branch(st, br, tiles, pre_first=None, post_last=None):
        nt_ = len(tiles)
        for ti, tl in enumerate(tiles):
            for hh in range(2):
                jb = dict(st=st, br=br, tl=tl, hh=hh, first=(ti == 0), last=(ti == nt_ - 1), pre=[], post=[])
                if ti == 0 and hh == 0 and pre_first is not None:
                    jb["pre"].append(pre_first)
                if ti == nt_ - 1:
                    jb["post"].append((lambda st=st, br=br, hh=hh: finalize(sts[st], br, hh)))
                    if hh == 1 and post_last is not None:
                        jb["post"].append(post_last)
                jobs.append(jb)

    def cmp_tiles(qb):
        nk = min(8 * qb + 7, 255)
        ctiles = [(0, min(nk, 128))] + ([(1, nk - 128)] if nk > 128 else [])
        out_ = []
        for (nt, cnt) in ctiles:
            def mfn(qb=qb, nt=nt, cnt=cnt):
                mk = mkc.get()
                k.do("dve", "tensor_scalar", [iota, thr], [mk], out=mk[:, :], in0=iota[:, :], scalar1=thr[:, 2 * qb + nt:2 * qb + nt + 1], scalar2=None,
                     op0=ALU.is_ge)
                return mk, mk.ap[0:cnt, :]
            out_.append((KcT, KcT[0:64, nt * 128:nt * 128 + cnt], 64, cnt, VcA[0:cnt, nt, :], VcA, mfn, OV[0:cnt, nt, :]))
        return out_

    def win_tiles(qb):
        out_ = []
        for kt_ in range(max(0, qb - 4), qb + 1):
            mi_ = 0 if kt_ == qb else (1 if kt_ == qb - 4 else None)
            mfn = (lambda mi_=mi_: (tri, tri.ap[:, mi_, :])) if mi_ is not None else None
            out_.append((KwT, KwT[0:64, kt_ * 128:(kt_ + 1) * 128], 64, 128, VwA[:, kt_, :], VwA, mfn, None))
        return out_

    def slc_tiles(qb):
        out_ = []
        for kt_ in range(qb + 1):
            mfn = (lambda: (tri, tri.ap[:, 0, :])) if kt_ == qb else None
            out_.append((KE, KE[:, kt_ * 128:(kt_ + 1) * 128], 128, 128, VsA[:, kt_, :], VsA, mfn, None))
        return out_

    sts = {}

    def mk_prep(qb):
        def f():
            sts[qb] = prep(qb)
        return f

    order = []
    for qb in range(NQ):
        if qb == 0:
            order.append(("cmp", 0))
        if qb + 1 < NQ:
            order.append(("cmp", qb + 1))
        order.append(("win", qb)); order.append(("slc", qb))
    def pre_cmp(q):
        def f():
            if q == 0:
                sts[0] = prep(0); prep_pe(sts[0])
                if NQ > 1:
                    sts[1] = prep(1); prep_pe(sts[1])
            if q >= 1 and q + 1 < NQ:
                sts[q + 1] = prep(q + 1)
        return f

    def pre_win(q):
        def f():
            select_pe(sts[q])
            flush_out()
        return f

    def pre_slc(q):
        def f():
            if q + 2 < NQ and q + 2 >= 2 and (q + 2) in sts:
                prep_pe(sts[q + 2])
        return f
    for kind, qb in order:
        if kind == "cmp":
            add_branch(qb, 0, cmp_tiles(qb), pre_first=pre_cmp(qb), post_last=(lambda qb=qb: select(sts[qb])))
        elif kind == "win":
            add_branch(qb, 2, win_tiles(qb), pre_first=pre_win(qb))
        else:
            add_branch(qb, 1, slc_tiles(qb), pre_first=pre_slc(qb), post_last=(lambda qb=qb: output_pre(sts[qb])))

    live = {}
    for n in range(len(jobs) + LOOK):
        if n < len(jobs):
            jb = jobs[n]
            for f in jb["pre"]:
                f()
            st = sts[jb["st"]]
            (Kt, Kap, Kp, cnt, Vap, Vt, mfn, Xap) = jb["tl"]
            hh = jb["hh"]; QM = st["QM"]
            bk = k.bank()
            rds = [Kt, QM] + ([st["QMhi"]] if Kp == 128 else [])
            k.op("pe", (lambda bk=bk, Kap=Kap, QM=QM, Kp=Kp, hh=hh, cnt=cnt:
                        (lambda e: e.matmul(bk[0:cnt, 0:512], lhsT=Kap, rhs=QM[0:Kp, hh * 512:(hh + 1) * 512], start=True, stop=True)))(),
                 reads=rds, writes=[bk])
            pt = Pt.get()
            k.do("act", "activation", [bk], [pt], out=pt[0:cnt, :], in_=bk[0:cnt, 0:512], func=AF.Exp, scale=SC)
            if mfn is not None:
                Mt, Map = mfn()
                k.do("dve", "tensor_tensor", [pt, Mt], [pt], out=pt.ap[0:cnt, :].rearrange("p (h q) -> p h q", h=4),
                     in0=pt.ap[0:cnt, :].rearrange("p (h q) -> p h q", h=4),
                     in1=Map.unsqueeze(1).to_broadcast([cnt, 4, 128]), op=ALU.mult)
            live[n] = pt
        m = n - LOOK
        if m >= 0:
            jb = jobs[m]
            (Kt, Kap, Kp, cnt, Vap, Vt, mfn, Xap) = jb["tl"]
            hh = jb["hh"]; ppt = live.pop(m)
            for h in range(4):
                k.mm(O[hh], O[hh][:, h * 65:(h + 1) * 65], ppt, ppt[0:cnt, h * 128:(h + 1) * 128], Vt, Vap,
                     jb["first"] and h == 0, jb["last"], skip=True)
                if Xap is not None:
                    k.mm(IMP, IMP[:, (hh * 4 + h) * 64:(hh * 4 + h + 1) * 64], ppt, ppt[0:cnt, h * 128:(h + 1) * 128], OV, Xap,
                         jb["first"] and h == 0 and hh == 0, jb["last"], skip=True)
            for f in jb["post"]:
                f()
    flush_out()

def build_fused(M, wtab):
    nc = bass.Bass("TRN2", target_bir_lowering=False)
    T = T_SEQ
    NQ = T // 128
    di = lambda n, s, dt=F32: nc.dram_tensor(n, s, dt, kind="ExternalInput").ap()
    dsc = lambda n, s, dt=F32: nc.dram_tensor(n, s, dt).ap()
    D = {}
    wsrc_in = di("wsrc", [128, M])
    D["xT"] = di("xT", [128, 8, T + 2]); D["pT"] = di("pT", [2, 128, 2, T])
    D["g_ab"] = di("g_ab", [128, 8]); D["gains"] = di("gains", [2, 128, 4, 8]); D["cwin"] = di("cwin", [2, 128, NJ, 4])
    D["ctab"] = di("ctab", [64, T]); D["stab"] = di("stab", [64, T])
    D["r_A"] = di("r_A", [4, 128, NCH]); D["r_B"] = di("r_B", [4, 128, NCH]); D["r_c"] = di("r_c", [4, 64, NCH])
    D["m_cw"] = di("m_cw", [4, 2, 64, 5]); D["m_gb"] = di("m_gb", [4, 128, 2]); D["ng"] = di("ng", [8, 128, 128])
    D["mask_in"] = di("mask_in", [128, 128]); D["ident_in"] = di("ident_in", [128, 128], BF16)
    D["gb_in"] = di("gb_in", [2, 128, NQ, 24]); D["qg_in"] = di("qg_in", [128, 64]); D["kg_in"] = di("kg_in", [128, 3, 64])
    D["pos_in"] = di("pos_in", [2, 128, 16]); D["tri_in"] = di("tri_in", [2, 128, 128], BF16)
    D["E_in"] = di("E_in", [64, T], BF16); D["OV_in"] = di("OV_in", [128, 2, 64], BF16)
    D["iota_in"] = di("iota_in", [128, 128]); D["thr_in"] = di("thr_in", [128, 2 * NQ]); D["selc_in"] = di("selc_in", [NQ, 128, 2, 64])
    D["out"] = nc.dram_tensor("out", [128, 8, T // 2], F32, kind="ExternalOutput").ap()
    D["sel"] = di("sel", [128, 2]); D["pT1h"] = di("pT1h", [128, 2, T // 2])
    wbf = dsc("wbf", [128, M], BF16)
    D["zA_fm"] = dsc("zA_fm", [12, 128, 3 + T]); D["zA_tm"] = dsc("zA_tm", [T, 2056])
    D["mixT"] = dsc("mixT", [8, 128, 2 + T], BF16); D["h1T"] = dsc("h1T", [128, 8, 2 + T])
    D["z1tm"] = dsc("z1tm", [T, 1584]); D["c2T"] = dsc("c2T", [2, 128, T + 16]); D["oT"] = dsc("oT", [8, 128, 2 + T], BF16)

    def wfn(name):
        off, X, NB = wtab[name]
        return lambda b: wbf[:, off + b * X:off + (b + 1) * X]
    D["w"] = wfn
    with ExitStack() as es:
        k = K(nc, es)
        k.arena_init(ARENA_KIB)
        k.banks_init(8)
        ctx = TPCtx(k)
        k.persist()
        CB = 4096
        st = Rot(k, 3, [128, CB], F32, "st"); ob = Rot(k, 3, [128, CB], BF16, "ob")
        engs = ["dve", "act", "pool"]
        for bi, c0 in enumerate(range(0, M, CB)):
            a = st.get(); b = ob.get()
            k.dma("sp", a[:, :], wsrc_in[:, c0:c0 + CB], writes=[a])
            e = engs[bi % 3]
            if e == "act":
                k.do("act", "copy", [a], [b], out=b[:, :], in_=a[:, :])
            else:
                k.do(e, "tensor_copy", [a], [b], out=b[:, :], in_=a[:, :])
            k.dma("pool" if e != "pool" else "sp", wbf[:, c0:c0 + CB], b[:, :], reads=[b])
        k.phase_reset()
        phase_A(k, ctx, D)
        k.phase_reset()
        phase_P2(k, ctx, D)
        k.phase_reset()
        phase_TP(k, ctx, D, 0)
        for g in range(2):
            k.phase_reset()
            phase_P5(k, ctx, D, g)
        k.phase_reset()
        phase_TP(k, ctx, D, 1)
        k.emit()
    return nc


ARENA_KIB = 190


def p2_consts():
    T = T_SEQ
    half = 32
    inv = 10000.0 ** (-np.arange(half, dtype=np.float64) / half)
    ang = np.arange(T, dtype=np.float64)[None, :] * inv[:, None]
    cos = np.cos(ang); sin = np.sin(ang)
    ctab = np.concatenate([cos, cos], 0).astype(np.float32)
    stab = np.concatenate([-sin, sin], 0).astype(np.float32)
    l = np.arange(128, dtype=np.float64)
    rA = np.zeros((4, 128, NCH), np.float32); rB = np.zeros((4, 128, NCH), np.float32); rc = np.zeros((4, 64, NCH), np.float32)
    for h in range(4):
        lg = np.log1p(-2.0 ** (-5.0 - h))
        rA[h] = np.exp((l + 1) * lg)[:, None]
        rB[h] = (np.exp(-(l + 1) * lg) * 0.125)[:, None]
        rc[h] = np.exp(128 * lg)
    mask = (l[:, None] <= l[None, :]).astype(np.float32)
    ident = np.eye(128, dtype=np.float32).astype(NPBF)
    return dict(ctab=ctab, stab=stab, r_A=rA, r_B=rB, r_c=rc, mask_in=mask, ident_in=ident)


def p5_consts():
    T = T_SEQ; NQ = T // 128
    l = np.arange(128)
    tri = (l[:, None] <= l[None, :]).astype(np.float32)
    triU = 1.0 - tri
    E = (np.arange(T)[None, :] // 64 == np.arange(64)[:, None]).astype(np.float32)
    n = np.arange(256); j = np.arange(64)
    ov = ((16 * n[:, None] < (j[None, :] + 1) * 64) & (16 * n[:, None] + 32 > j[None, :] * 64)).astype(np.float32)
    ov[255] = 0
    OV = np.ascontiguousarray(ov.reshape(2, 128, 64).transpose(1, 0, 2))
    iota = np.tile(l[None, :].astype(np.float32), (128, 1))
    thr = np.zeros((128, 2 * NQ), np.float32)
    for qb in range(NQ):
        for nt in range(2):
            thr[:, 2 * qb + nt] = 16 * (nt * 128 + l) + 31 - 128 * qb
    selc = np.zeros((NQ, 128, 2, 64), np.float32)
    for qb in range(NQ):
        for q in range(128):
            cur = (128 * qb + q) // 64
            mult = np.ones(64, np.float32); add = np.zeros(64, np.float32)
            mult[0] = 0; add[0] = 1e30
            if cur >= 1:
                mult[cur - 1] = 0; add[cur - 1] = 3e30
            mult[cur] = 0; add[cur] = 2e30
            mult[cur + 1:] = 0; add[cur + 1:] = -1e30
            selc[qb, q, 0] = mult; selc[qb, q, 1] = add
    return dict(tri_in=np.stack([tri, triU]).astype(NPBF), E_in=(E * 30000.0).astype(NPBF), OV_in=OV.astype(NPBF), iota_in=iota, thr_in=thr, selc_in=selc)


def kernel(**inp):
    inp = {k_: np.asarray(v) for k_, v in inp.items()}
    x = inp["x"].astype(np.float32); p = inp["p"].astype(np.float32)
    T = T_SEQ; NQ = T // 128
    wt = wlayout(inp)
    wflat, M = wt.flat()
    nc = build_fused(M, wt.tab)
    shared = dict(p2_consts()); shared.update(p5_consts())
    shared["wsrc"] = wflat
    shared["g_ab"] = gain_fm(inp["ab_norm_g"][0])
    gl = []
    for l in range(2):
        nxt = inp["nsa_norm_g"][0] if l == 0 else inp["ple_gate_norm_g"][1]
        gl.append(np.stack([gain_fm(inp["ffn_norm_g"][l]), gain_fm(inp["ple_norm_g"][l]), gain_fm(inp["ple_gate_norm_g"][l]), gain_fm(nxt)], 1))
    shared["gains"] = np.ascontiguousarray(np.stack(gl))
    cwl = []
    for l in range(2):
        cwm = np.concatenate([inp["ffn_conv_w"][l], inp["ffn_conv_b"][l][None]], 0)
        cwl.append(np.ascontiguousarray(cwm.T.reshape(NJ, 128, 4).transpose(1, 0, 2)))
    shared["cwin"] = np.stack(cwl).astype(np.float32)
    conv_w = inp["ab_conv_w"][0]; conv_b = inp["ab_conv_b"][0]
    mcw = []
    for h in range(4):
        qw = np.concatenate([conv_w[:, h * 64:(h + 1) * 64].T, conv_b[h * 64:(h + 1) * 64, None]], 1)
        kw = np.concatenate([conv_w[:, 256 + h * 64:256 + (h + 1) * 64].T, conv_b[256 + h * 64:256 + (h + 1) * 64, None]], 1)
        mcw.append(np.stack([qw, kw]))
    shared["m_cw"] = np.stack(mcw).astype(np.float32)
    shared["m_gb"] = np.stack([np.tile(np.array([[inp["ab_ig_b"][0][h], inp["ab_fg_b"][0][h]]], np.float32), (128, 1)) for h in range(4)])
    shared["ng"] = np.stack([np.tile(inp["ab_ret_norm_g"][0][h * 128:(h + 1) * 128][None], (128, 1)) for h in range(4)] +
                            [np.tile(inp["ab_m_norm_g"][0][h * 128:(h + 1) * 128][None], (128, 1)) for h in range(4)]).astype(np.float32)
    shared["gb_in"] = np.stack([np.broadcast_to(inp["nsa_gate_b"][0][g * 24:(g + 1) * 24][None, None, :], (128, NQ, 24)) for g in range(2)]).astype(np.float32)
    shared["qg_in"] = np.broadcast_to(inp["nsa_q_norm_g"][0][None, :], (128, 64)).astype(np.float32)
    shared["kg_in"] = np.broadcast_to(inp["nsa_k_norm_g"][0][None], (128, 3, 64)).astype(np.float32)
    def pos2(pos):
        return np.ascontiguousarray(pos.reshape(16, 2, 64).transpose(1, 2, 0).reshape(128, 16)).astype(np.float32)
    shared["pos_in"] = np.stack([pos2(inp["nsa_cmp_pos_k"][0]), pos2(inp["nsa_cmp_pos_v"][0])])
    shared = {k_: np.ascontiguousarray(v) for k_, v in shared.items()}
    maps = []
    for c in range(8):
        b = c // 2
        d = dict(shared)
        xt = np.zeros((128, 8, T + 2), np.float32); xt[:, :, 2:] = fm(x[b])
        d["xT"] = xt
        d["pT"] = np.stack([fm(p[l, b]) for l in range(2)])
        hf = c % 2
        d["pT1h"] = fm(p[1, b, hf * (T // 2):(hf + 1) * (T // 2)])
        sel = np.zeros((128, 2), np.float32); sel[:, hf] = 1.0
        d["sel"] = sel
        maps.append(d)
    res = run_spmd(nc, maps)
    out = np.stack([np.concatenate([unfm(np.asarray(res[2 * b]["out"])), unfm(np.asarray(res[2 * b + 1]["out"]))], 0) for b in range(4)])
    return out.astype(np.float32)
```

```python
import math
from contextlib import ExitStack
import numpy as np
import ml_dtypes
import concourse.bass as bass
import concourse.mybir as mybir
from concourse.bass_utils import run_bass_kernel_spmd

F32 = mybir.dt.float32
BF16 = mybir.dt.bfloat16
AF = mybir.ActivationFunctionType
ALU = mybir.AluOpType
AX = mybir.AxisListType
NPBF = ml_dtypes.bfloat16

N_DMA_SEMS = 24


class T:
    __slots__ = ("ap", "w", "r", "name")

    def __init__(self, ap, name=""):
        self.ap = ap
        self.w = None
        self.r = []
        self.name = name

    def __getitem__(self, idx):
        return self.ap[idx]

    @property
    def dep(self):
        return self


class V:
    def __init__(self, parent, ap):
        self.parent = parent; self.ap = ap

    def __getitem__(self, idx):
        return self.ap[idx]

    @property
    def dep(self):
        return self.parent


class Op:
    __slots__ = ("eng", "idx", "fn", "deps", "is_dma", "dsem", "dval", "needed", "cnt")

    def __init__(self, eng, idx, fn, is_dma):
        self.eng = eng; self.idx = idx; self.fn = fn; self.deps = []
        self.is_dma = is_dma; self.dsem = None; self.dval = 0; self.needed = False; self.cnt = 0


class K:
    ENGS = ("pe", "dve", "act", "pool", "sp")

    def __init__(self, nc, es):
        self.nc = nc
        self.es = es
        self.ops = {e: [] for e in self.ENGS}
        self.dma_rr = 0
        self.dma_last = [None] * N_DMA_SEMS
        self.dma_cnt = [0] * N_DMA_SEMS
        self.out_dmas = []
        self.n_t = 0

    def arena_init(self, kib):
        self.arena_n = kib * 512
        self.arena = self.es.enter_context(self.nc.sbuf_tensor("arena", [128, self.arena_n], BF16))
        self.a_off = 0
        self.a_base = 0

    def sb(self, shape, dt, name=None):
        P = shape[0]
        n = 1
        for d in shape[1:]:
            n *= d
        sz = n if dt == BF16 else 2 * n
        sz = (sz + 63) // 64 * 64
        assert self.a_off + sz <= self.arena_n, ("SBUF arena overflow", name, self.a_off, sz, self.arena_n)
        ap = self.arena[0:P, self.a_off:self.a_off + (n if dt == BF16 else 2 * n)]
        self.a_off += sz
        if dt != BF16:
            ap = ap.bitcast(dt)
        if len(shape) == 3:
            ap = ap.rearrange("p (a b) -> p a b", a=shape[1])
        return T(ap, name or "")

    def persist(self):
        self.a_base = self.a_off

    def phase_reset(self):
        self.barrier()
        self.a_off = self.a_base

    def barrier(self):
        lasts = []
        for e in self.ENGS:
            for op in reversed(self.ops[e]):
                if not op.is_dma and op.fn is not None:
                    lasts.append(op); break
        lasts += [d for d in self.dma_last if d is not None]
        for e in self.ENGS:
            b = Op(e, len(self.ops[e]), None, False)
            b.deps = [d for d in lasts if not (d.eng == e and not d.is_dma)]
            self.ops[e].append(b)

    def ps(self, shape, dt=F32, name=None):
        self.n_t += 1
        name = name or f"p{self.n_t}"
        t = self.es.enter_context(self.nc.psum_tensor(name, list(shape), dt))
        return T(t, name)

    def view(self, ap, name=""):
        return T(ap, name)

    def _rec(self, eng, fn, reads, writes, is_dma=False):
        lst = self.ops[eng]
        op = Op(eng, len(lst), fn, is_dma)
        deps = []
        reads = [t.dep for t in reads]; writes = [t.dep for t in writes]
        for t in reads:
            if t.w is not None:
                deps.append((t.w, "raw"))
        for t in writes:
            if t.w is not None:
                deps.append((t.w, "waw"))
            for r in t.r:
                deps.append((r, "war"))
        for d, kind in deps:
            if d is op:
                continue
            if (not d.is_dma) and d.eng == eng and not is_dma:
                if eng == "pe" or kind != "raw":
                    continue
            op.deps.append(d)
        if is_dma:
            i = self.dma_rr; self.dma_rr = (self.dma_rr + 1) % N_DMA_SEMS
            prev = self.dma_last[i]
            if prev is not None:
                op.deps.append(prev)
            self.dma_cnt[i] += 1
            op.dsem = i; op.dval = 16 * self.dma_cnt[i]
            self.dma_last[i] = op
        for t in reads:
            t.r.append(op)
        for t in writes:
            t.w = op; t.r = []
        lst.append(op)
        return op

    def op(self, eng, fn, reads=(), writes=()):
        return self._rec(eng, fn, reads, writes)

    def dma(self, eng, out_ap, in_ap, reads=(), writes=(), is_out=False, **kw):
        def fn(e):
            return e.dma_start(out=out_ap, in_=in_ap, **kw)
        op = self._rec(eng, fn, reads, writes, is_dma=True)
        if is_out:
            self.out_dmas.append(op)
        return op

    def emit(self):
        nc = self.nc
        fin = Op("sp", len(self.ops["sp"]), None, False)
        fin.deps = list(self.out_dmas)
        self.ops["sp"].append(fin)
        for e in self.ENGS:
            for op in self.ops[e]:
                for d in op.deps:
                    d.needed = True
        for e in self.ENGS:
            c = 0
            for op in self.ops[e]:
                if op.needed and not op.is_dma:
                    c += 1
                op.cnt = c
        sems = {e: self.es.enter_context(nc.semaphore(f"s_{e}")) for e in self.ENGS}
        dsems = [self.es.enter_context(nc.semaphore(f"s_dma{i}")) for i in range(N_DMA_SEMS)]
        block = self.es.enter_context(nc.Block())
        ops = self.ops

        def run(ename, eng):
            waited = {}
            for op in ops[ename]:
                need = {}
                for d in op.deps:
                    if d.is_dma:
                        key = ("d", d.dsem); val = d.dval
                    else:
                        key = ("e", d.eng); val = d.cnt
                    if waited.get(key, 0) >= val:
                        continue
                    if need.get(key, 0) < val:
                        need[key] = val
                for key, val in need.items():
                    sem = dsems[key[1]] if key[0] == "d" else sems[key[1]]
                    eng.wait_ge(sem, val)
                    waited[key] = val
                if op.fn is None:
                    continue
                ins = op.fn(eng)
                if op.is_dma:
                    ins.then_inc(dsems[op.dsem], 16)
                elif op.needed:
                    ins.then_inc(sems[ename], 1)

        @block.tensor
        def _(e):
            run("pe", e)

        @block.vector
        def _(e):
            run("dve", e)

        @block.scalar
        def _(e):
            run("act", e)

        @block.gpsimd
        def _(e):
            run("pool", e)

        @block.sync
        def _(e):
            run("sp", e)


    def do(self, eng, method, reads, writes, *a, **kw):
        return self.op(eng, lambda e: getattr(e, method)(*a, **kw), reads, writes)

    def mm(self, out_t, out_ap, l_t, l_ap, r_t, r_ap, start, stop, skip=False):
        return self.op("pe", lambda e: e.matmul(out_ap, lhsT=l_ap, rhs=r_ap, start=start, stop=stop, skip_group_check=skip),
                       reads=[l_t, r_t], writes=[out_t])

    def banks_init(self, n=8):
        if not hasattr(self, "allbanks"):
            self.allbanks = [self.ps([128, 512], F32, name=f"bank{i}") for i in range(8)]
        self.banks = self.allbanks[:n]
        self.bank_i = 0

    def bank_bf(self, i):
        b = self.allbanks[i]
        return V(b, b.ap[:, :].bitcast(BF16))

    def bank(self):
        b = self.banks[self.bank_i]
        self.bank_i = (self.bank_i + 1) % len(self.banks)
        return b


class Rot:
    def __init__(self, k, n, shape, dt, name):
        self.t = [k.sb(shape, dt, name=f"{name}{i}") for i in range(n)]
        self.i = 0

    def get(self):
        t = self.t[self.i]
        self.i = (self.i + 1) % len(self.t)
        return t


def slices512(w0, w1):
    out = []
    s = w0
    while s < w1:
        e = min(s + 512, w1)
        out.append((s, e)); s = e
    return out


EPS = 1e-6

T_SEQ = 4096
NCH = 32
DFF = 2816
NJ = DFF // 128


def run_spmd(nc, in_maps):
    res = run_bass_kernel_spmd(nc, in_maps, core_ids=list(range(len(in_maps))))
    return res.results


def wblocks(W, CB):
    Kd, C = W.shape
    KC = Kd // 128
    NB = (C + CB - 1) // CB
    if NB * CB != C:
        Wp = np.zeros((Kd, NB * CB), W.dtype); Wp[:, :C] = W; W = Wp
    return np.ascontiguousarray(W.reshape(KC, 128, NB, CB).transpose(1, 2, 0, 3)).reshape(128, NB * KC * CB)


def fm(a):
    N, Fd = a.shape
    return np.ascontiguousarray(a.T.reshape(Fd // 128, 128, N).transpose(1, 0, 2))


def unfm(a):
    P, KC, N = a.shape
    return np.ascontiguousarray(a.transpose(2, 1, 0)).reshape(N, KC * P)


def gain_fm(g):
    return np.ascontiguousarray(g.reshape(-1, 128).T.astype(np.float32))


class WTab:
    def __init__(self):
        self.parts = []; self.off = 0; self.tab = {}

    def add(self, name, blk, X):
        self.tab[name] = (self.off, X, blk.shape[1] // X)
        self.parts.append(blk); self.off += blk.shape[1]

    def flat(self):
        M = (self.off + 4095) // 4096 * 4096
        out = np.zeros((128, M), np.float32)
        o = 0
        for p_ in self.parts:
            out[:, o:o + p_.shape[1]] = p_; o += p_.shape[1]
        return out, M


def wlayout(inp):
    wt = WTab()
    w_in = inp["ab_w_in"][0]
    rq, rk, rv, rg, mq, mk, mv, mo, mi, mf = np.split(w_in, np.cumsum([256, 256, 512, 512, 256, 256, 512, 512, 4])[:9].tolist() + [3076], axis=1) \
        if False else (w_in[:, 0:256], w_in[:, 256:512], w_in[:, 512:1024], w_in[:, 1024:1536], w_in[:, 1536:1792], w_in[:, 1792:2048],
                       w_in[:, 2048:2560], w_in[:, 2560:3072], w_in[:, 3072:3076], w_in[:, 3076:3080])
    def sw(w):
        return w.reshape(1024, 4, 2, 32)[:, :, ::-1, :].reshape(1024, 256)
    a_fm = np.concatenate([rq, sw(rq), rk, sw(rk), mq, mk], 1)
    wt.add("a_fm", wblocks(a_fm, 128), 8 * 128)
    a_tm = np.concatenate([rv, rg, mv, mo, mi, mf], 1)
    wt.add("a_tm", wblocks(a_tm, 512), 8 * 512)
    n_in = inp["nsa_w_in"][0]
    n_tm = np.concatenate([n_in[:, 0:1024], n_in[:, 1280:1840]], 1)
    wt.add("n_tm", wblocks(n_tm, 512), 8 * 512)
    wt.add("n_fm", wblocks(n_in[:, 1024:1280], 128), 8 * 128)
    for l, mixn in ((0, "ab_w_out"), (1, "nsa_w_out")):
        wt.add("w_out%d" % l, wblocks(inp[mixn][0], 128), 8 * 128)
        up = inp["ffn_w_up"][l]
        a = up[:, :DFF].reshape(1024, NJ, 128); b = up[:, DFF:].reshape(1024, NJ, 128)
        wt.add("w_up%d" % l, wblocks(np.concatenate([a, b], 2).reshape(1024, NJ * 256), 256), 8 * 256)
        wt.add("w_dn%d" % l, wblocks(inp["ffn_w_down"][l], 128), NJ * 128)
        wt.add("w_ple%d" % l, wblocks(inp["ple_w"][l], 128), 2 * 128)
        wt.add("w_gate%d" % l, wblocks(inp["ple_w_gate"][l], 128), 8 * 128)
    for nm in ("nsa_cmp_w1k", "nsa_cmp_w1v"):
        wt.add(nm, wblocks(inp[nm][0], 256), 16 * 256)
    for nm in ("nsa_cmp_w2k", "nsa_cmp_w2v"):
        wt.add(nm, wblocks(inp[nm][0], 64), 2 * 64)
    return wt


class TPCtx:
    def __init__(self, k):
        self.k = k
        self.ones = k.sb([128, 128], BF16, name="ones_bf")
        self.eps = k.sb([128, 1], F32, name="eps")
        k.do("pool", "memset", [], [self.ones], self.ones[:, :], 1.0)
        k.do("pool", "memset", [], [self.eps], self.eps[:, :], EPS)

    def scratch(self):
        k = self.k
        self.sq = Rot(k, 2, [128, 512], BF16, "sq")
        self.rstd = Rot(k, 2, [128, 512], F32, "rstd")
        self.f32tmp = Rot(k, 2, [128, 512], F32, "f32tmp")

    def rmsnorm(self, src, KC, w0, w1, gain, dst, d0):
        k = self.k
        inv = 1.0 / (KC * 128)
        for (s0, s1) in slices512(w0, w1):
            n = s1 - s0
            bk = k.bank()
            for kc in range(KC):
                sq = self.sq.get()
                k.do("act", "activation", [src], [sq], out=sq[:, :n], in_=src[:, kc, s0:s1], func=AF.Square)
                k.mm(bk, bk[:, :n], self.ones, self.ones[:, :], sq, sq[:, :n], kc == 0, kc == KC - 1)
            r = self.rstd.get()
            k.do("act", "activation", [bk, self.eps], [r], out=r[:, :n], in_=bk[:, :n], func=AF.Ln, bias=self.eps[:, 0:1], scale=inv)
            k.do("act", "activation", [r], [r], out=r[:, :n], in_=r[:, :n], func=AF.Exp, scale=-0.5)
            for kc in range(KC):
                k.do("dve", "scalar_tensor_tensor", [src, gain, r], [dst],
                     out=dst[:, kc, d0 + s0 - w0:d0 + s1 - w0], in0=src[:, kc, s0:s1], scalar=gain[:, kc:kc + 1],
                     in1=r[:, :n], op0=ALU.mult, op1=ALU.mult)


def tokmajor_proj(k, xn, KC, NT, x0, wsrc, CB, C, z_ap, wrot, zrot):
    NB = (C + CB - 1) // CB
    for b in range(NB):
        wt = wrot.get()
        k.dma("sp", wt[:, :KC * CB], wsrc(b), writes=[wt])
        cw = min(CB, C - b * CB)
        for ti in range(NT // 128):
            bk = k.bank()
            for kc in range(KC):
                k.mm(bk, bk[:, :cw], xn, xn[:, kc, x0 + ti * 128:x0 + (ti + 1) * 128], wt, wt[:, kc * CB:kc * CB + cw], kc == 0, kc == KC - 1)
            zt = zrot.get()
            if ti % 2 == 0:
                k.do("act", "copy", [bk], [zt], out=zt[:, :cw], in_=bk[:, :cw])
            else:
                k.do("dve", "tensor_copy", [bk], [zt], out=zt[:, :cw], in_=bk[:, :cw])
            k.dma("pool", z_ap[ti * 128:(ti + 1) * 128, b * CB:b * CB + cw], zt[:, :cw], reads=[zt])


def featmajor_proj(k, xn, KC, NT, x0, wsrc, NCK, dst_fn, wrot, zrot):
    for c in range(NCK):
        wt = wrot.get()
        k.dma("sp", wt[:, :KC * 128], wsrc(c), writes=[wt])
        for (s0, s1) in slices512(0, NT):
            bk = k.bank()
            for kc in range(KC):
                k.mm(bk, bk[:, :s1 - s0], wt, wt[:, kc * 128:(kc + 1) * 128], xn, xn[:, kc, x0 + s0:x0 + s1], kc == 0, kc == KC - 1)
            zt = zrot.get()
            k.do("act", "copy", [bk], [zt], out=zt[:, :s1 - s0], in_=bk[:, :s1 - s0])
            k.dma("pool", dst_fn(c)[:, s0:s1], zt[:, :s1 - s0], reads=[zt])


def gelu_tanh(k, src, dst_ap, dst_t, n, uu, sg_, P=128):
    k.do("pool", "tensor_tensor", [src], [uu], out=uu[:P, :n], in0=src[:P, :n], in1=src[:P, :n], op=ALU.mult)
    k.do("pool", "tensor_scalar", [uu], [uu], out=uu[:P, :n], in0=uu[:P, :n], scalar1=0.044715, scalar2=1.0, op0=ALU.mult, op1=ALU.add)
    k.do("pool", "tensor_tensor", [uu, src], [uu], out=uu[:P, :n], in0=uu[:P, :n], in1=src[:P, :n], op=ALU.mult)
    k.do("act", "activation", [uu], [sg_], out=sg_[:P, :n], in_=uu[:P, :n], func=AF.Sigmoid, scale=1.5957691216057308)
    k.do("dve", "tensor_tensor", [src, sg_], [dst_t], out=dst_ap, in0=src[:P, :n], in1=sg_[:P, :n], op=ALU.mult)


def phase_A(k, ctx, D):
    NT = 1024
    T = T_SEQ
    ctx.scratch()
    h = k.sb([128, 8, NT], F32, name="h"); xn = k.sb([128, 8, NT], BF16, name="xn")
    gt = k.sb([128, 8], F32, name="gA")
    wrot = Rot(k, 2, [128, 4096], BF16, "wr"); zrot = Rot(k, 3, [128, 512], F32, "zr")
    zer = k.sb([128, 8], F32, name="zer")
    k.do("pool", "memset", [], [zer], zer[:, :], 0.0)
    for c in range(12):
        k.dma("pool", D["zA_fm"][c][:, 0:3], zer[:, 0:3], reads=[zer])
    k.dma("sp", gt[:, :], D["g_ab"], writes=[gt])
    for ps in range(T // NT):
        for kc in range(8):
            k.dma("sp" if kc % 2 == 0 else "pool", h[:, kc, :], D["xT"][:, kc, 2 + ps * NT:2 + (ps + 1) * NT], writes=[h])
        ctx.rmsnorm(h, 8, 0, NT, gt, xn, 0)
        tokmajor_proj(k, xn, 8, NT, 0, D["w"]("a_tm"), 512, 2056, D["zA_tm"][ps * NT:(ps + 1) * NT, :], wrot, zrot)
        featmajor_proj(k, xn, 8, NT, 0, D["w"]("a_fm"), 12, (lambda c, ps=ps: D["zA_fm"][c][:, 3 + ps * NT:3 + (ps + 1) * NT]), wrot, zrot)


def phase_P2(k, ctx, D):
    T = T_SEQ
    SEG = 512
    k.banks_init(6)
    pbf = [k.bank_bf(6), k.bank_bf(7)]
    eps = ctx.eps
    mask = k.sb([128, 128], F32, name="mask"); ident = k.sb([128, 128], BF16, name="ident")
    ones32 = k.sb([128, 64], F32, name="ones32")
    k.dma("sp", mask[:, :], D["mask_in"], writes=[mask]); k.dma("sp", ident[:, :], D["ident_in"], writes=[ident])
    k.do("pool", "memset", [], [ones32], ones32[:, :], 1.0)
    stg = Rot(k, 12, [64, SEG + 3], F32, "stg")
    tt = Rot(k, 8, [64, SEG], F32, "tt")
    qT = k.sb([64, T], BF16, name="qT"); kT = k.sb([64, T], BF16, name="kT")
    kt = k.sb([128, NCH, 64], BF16, name="kt")
    v = k.sb([128, NCH, 128], F32, name="v"); vp = k.sb([128, NCH, 129], BF16, name="vp")
    gt = k.sb([128, NCH, 128], F32, name="gt"); oall = k.sb([128, NCH, 129], F32, name="oall")
    res = k.sb([128, NCH, 128], F32, name="res"); resb = k.sb([128, NCH, 128], BF16, name="resb")
    ost = Rot(k, 2, [128, 1024], BF16, "ost")
    A = k.sb([128, NCH], F32, name="A"); Bv = k.sb([128, NCH], F32, name="Bv"); cd = k.sb([64, NCH], F32, name="cd")
    gi = k.sb([128, NCH], F32, name="gi"); gf = k.sb([128, NCH], F32, name="gf"); gb = k.sb([128, 2], F32, name="gb")
    l1 = k.sb([128, NCH], F32, name="l1"); ngb = k.sb([128, 1], F32, name="ngb")
    cw = k.sb([64, 10], F32, name="cw")
    g_t = k.sb([128, 128], F32, name="g_t")
    St = k.sb([64, 129], F32, name="St"); Sb = k.sb([64, 129], BF16, name="Sb"); StC = k.sb([64, 129], F32, name="StC")
    sm = Rot(k, 3, [128, 128], BF16, "sm")
    dn = k.sb([128, NCH], F32, name="dn"); ss = k.sb([128, NCH], F32, name="ss")
    ctb = Rot(k, 3, [64, SEG], F32, "ctb"); stb = Rot(k, 3, [64, SEG], F32, "stb")
    zer = k.sb([128, 8], BF16, name="zerb")
    k.do("pool", "memset", [], [zer], zer[:, :], 0.0)
    for c in range(8):
        k.dma("pool", D["mixT"][c][:, 0:2], zer[:, 0:2], reads=[zer])
    zfm = D["zA_fm"]; ztm = D["zA_tm"]

    def tmcols(col, n):
        return ztm[:, col:col + n].rearrange("(c p) e -> p c e", p=128)

    for slot in range(8):
        is_ret = slot < 4
        j = slot % 4
        rs = slice((j % 2) * 64, (j % 2) * 64 + 64)
        k.dma("sp", v[:, :, :], tmcols((0 if is_ret else 1024) + j * 128, 128), writes=[v])
        k.dma("sp", gt[:, :, :], tmcols((512 if is_ret else 1536) + j * 128, 128), writes=[gt])
        k.dma("sp", g_t[:, :], D["ng"][slot], writes=[g_t])
        if is_ret:
            k.dma("sp", A[:, :], D["r_A"][j], writes=[A]); k.dma("sp", Bv[:, :], D["r_B"][j], writes=[Bv]); k.dma("sp", cd[:, :], D["r_c"][j], writes=[cd])
            for sg in range(T // SEG):
                c_t = ctb.get(); s_t = stb.get()
                k.dma("sp", c_t[:, :], D["ctab"][:, sg * SEG:(sg + 1) * SEG], writes=[c_t])
                k.dma("sp", s_t[:, :], D["stab"][:, sg * SEG:(sg + 1) * SEG], writes=[s_t])
                for (c0, dstT) in ((0, qT), (4, kT)):
                    a = stg.get(); b = stg.get()
                    k.dma("sp", a[:, :SEG], zfm[c0 + j // 2][rs, 3 + sg * SEG:3 + (sg + 1) * SEG], writes=[a])
                    k.dma("sp", b[:, :SEG], zfm[c0 + 2 + j // 2][rs, 3 + sg * SEG:3 + (sg + 1) * SEG], writes=[b])
                    t1 = tt.get(); t2 = tt.get()
                    k.do("dve", "tensor_tensor", [a, c_t], [t1], out=t1[:, :], in0=a[:, :SEG], in1=c_t[:, :], op=ALU.mult)
                    k.do("pool", "tensor_tensor", [b, s_t], [t2], out=t2[:, :], in0=b[:, :SEG], in1=s_t[:, :], op=ALU.mult)
                    k.do("dve", "tensor_tensor", [t1, t2], [dstT], out=dstT[:, sg * SEG:(sg + 1) * SEG], in0=t1[:, :], in1=t2[:, :], op=ALU.add)
        else:
            k.dma("sp", gi[:, :], ztm[:, 2048 + j:2049 + j].rearrange("(c p) o -> p (c o)", p=128), writes=[gi], allow_slow_non_contiguous=True)
            k.dma("sp", gf[:, :], ztm[:, 2052 + j:2053 + j].rearrange("(c p) o -> p (c o)", p=128), writes=[gf], allow_slow_non_contiguous=True)
            k.dma("sp", gb[:, :], D["m_gb"][j], writes=[gb])
            k.dma("sp", cw[:, 0:5], D["m_cw"][j, 0], writes=[cw]); k.dma("sp", cw[:, 5:10], D["m_cw"][j, 1], writes=[cw])
            k.do("dve", "tensor_scalar", [gb], [ngb], out=ngb[:, :], in0=gb[:, 1:2], scalar1=-1.0, scalar2=None, op0=ALU.mult)
            k.do("act", "activation", [gf, ngb], [l1], out=l1[:, :], in_=gf[:, :], func=AF.Exp, bias=ngb[:, 0:1], scale=-1.0)
            k.do("act", "activation", [l1], [l1], out=l1[:, :], in_=l1[:, :], func=AF.Ln, bias=1.0, scale=1.0)
            bk = k.bank()
            k.mm(bk, bk[:, 0:NCH], mask, mask[:, :], l1, l1[:, :], True, True)
            k.mm(bk, bk[0:64, 64:64 + NCH], ones32, ones32[:, :], l1, l1[:, :], True, True)
            k.do("act", "activation", [bk], [A], out=A[:, :], in_=bk[:, 0:NCH], func=AF.Exp, bias=math.log(0.125), scale=-1.0)
            k.do("dve", "tensor_tensor", [gi, bk], [Bv], out=Bv[:, :], in0=gi[:, :], in1=bk[:, 0:NCH], op=ALU.add)
            k.do("act", "activation", [Bv, gb], [Bv], out=Bv[:, :], in_=Bv[:, :], func=AF.Exp, bias=gb[:, 0:1], scale=1.0)
            k.do("act", "activation", [bk], [cd], out=cd[:, :], in_=bk[0:64, 64:64 + NCH], func=AF.Exp, scale=-1.0)
            for sg in range(T // SEG):
                for qk, (c0, dstT) in enumerate(((8, qT), (10, kT))):
                    a = stg.get()
                    k.dma("sp", a[:, :SEG + 3], zfm[c0 + j // 2][rs, sg * SEG:sg * SEG + SEG + 3], writes=[a])
                    t1 = tt.get()
                    o5 = qk * 5
                    k.do("dve", "tensor_scalar", [a, cw], [t1], out=t1[:, :], in0=a[:, 3:SEG + 3], scalar1=cw[:, o5 + 3:o5 + 4],
                         scalar2=cw[:, o5 + 4:o5 + 5], op0=ALU.mult, op1=ALU.add)
                    for tap in (2, 1, 0):
                        k.do("dve", "scalar_tensor_tensor", [a, cw, t1], [t1], out=t1[:, :], in0=a[:, tap:SEG + tap],
                             scalar=cw[:, o5 + tap:o5 + tap + 1], in1=t1[:, :], op0=ALU.mult, op1=ALU.add)
                    k.do("act", "activation", [t1], [dstT], out=dstT[:, sg * SEG:(sg + 1) * SEG], in_=t1[:, :], func=AF.Silu)
        k.do("act", "activation", [gt], [gt], out=gt[:, :, :], in_=gt[:, :, :], func=(AF.Silu if is_ret else AF.Sigmoid))
        for c in range(NCH):
            if c % 4 == 3:
                k.do("dve", "tensor_scalar", [v, Bv], [vp], out=vp[:, c, 0:128], in0=v[:, c, :], scalar1=Bv[:, c:c + 1], scalar2=None, op0=ALU.mult)
            else:
                k.do("act", "activation", [v, Bv], [vp], out=vp[:, c, 0:128], in_=v[:, c, :], func=AF.Copy, scale=Bv[:, c:c + 1])
        k.do("dve", "tensor_copy", [Bv], [vp], out=vp[:, :, 128], in_=Bv[:, :])
        for c in range(NCH):
            pb = pbf[c % 2]
            k.op("pe", (lambda pb=pb, c=c: (lambda e: e.transpose(pb[:, 0:64], kT[:, c * 128:(c + 1) * 128], ident[0:64, 0:64])))(),
                 reads=[kT, ident], writes=[pb])
            k.do("act", "copy", [pb], [kt], out=kt[:, c, :], in_=pb[:, 0:64])
        k.do("dve", "memset", [], [StC], StC[:, :], 0.0)
        k.do("pool", "memset", [], [Sb], Sb[:, :], 0.0)

        def emit_S(c):
            cs = slice(c * 128, (c + 1) * 128)
            b1 = k.bank()
            k.mm(b1, b1[:, 0:128], kT, kT[:, cs], qT, qT[:, cs], True, True)
            s_ = sm.get()
            k.do("dve", "tensor_tensor", [b1, mask], [s_], out=s_[:, :], in0=b1[:, 0:128], in1=mask[:, :], op=ALU.mult)
            return s_
        s_next = emit_S(0)
        for c in range(NCH):
            cs = slice(c * 128, (c + 1) * 128)
            s_ = s_next
            if c + 1 < NCH:
                s_next = emit_S(c + 1)
            if c < NCH - 1:
                b3 = k.bank()
                k.mm(b3, b3[0:64, 0:129], kt, kt[:, c, :], vp, vp[:, c, :], True, True)
            b2 = k.bank()
            k.mm(b2, b2[:, 0:129], s_, s_[:, :], vp, vp[:, c, :], True, False)
            k.mm(b2, b2[:, 0:129], qT, qT[:, cs], Sb, Sb[:, :], False, True)
            k.do("act", "activation", [b2, A], [oall], out=oall[:, c, :], in_=b2[:, 0:129], func=AF.Copy, scale=A[:, c:c + 1])
            if c < NCH - 1:
                k.do("dve", "scalar_tensor_tensor", [b3, cd, StC], [St], out=St[:, :], in0=b3[0:64, 0:129], scalar=cd[:, c:c + 1], in1=StC[:, :],
                     op0=ALU.mult, op1=ALU.add)
                k.do("act", "copy", [St], [Sb], out=Sb[:, :], in_=St[:, :])
                if c + 1 < NCH - 1:
                    k.do("act", "activation", [St, cd], [StC], out=StC[:, :], in_=St[:, :], func=AF.Copy, scale=cd[:, c + 1:c + 2])
        if not is_ret:
            k.do("act", "activation", [oall], [dn], out=dn[:, :], in_=oall[:, :, 128], func=AF.Abs)
            k.do("dve", "tensor_scalar", [dn], [dn], out=dn[:, :], in0=dn[:, :], scalar1=1.0, scalar2=None, op0=ALU.max)
            k.do("dve", "reciprocal", [dn], [dn], out=dn[:, :], in_=dn[:, :])
            for c in range(NCH):
                k.do("pool" if c % 2 else "dve", "tensor_scalar", [oall, dn], [oall], out=oall[:, c, 0:128], in0=oall[:, c, 0:128],
                     scalar1=dn[:, c:c + 1], scalar2=None, op0=ALU.mult)
        k.do("dve", "tensor_tensor", [oall], [v], out=v[:, :, :], in0=oall[:, :, 0:128], in1=oall[:, :, 0:128], op=ALU.mult)
        k.do("dve", "tensor_reduce", [v], [ss], out=ss[:, :], in_=v[:, :, :], axis=AX.X, op=ALU.add)
        k.do("act", "activation", [ss, eps], [ss], out=ss[:, :], in_=ss[:, :], func=AF.Sqrt, bias=eps[:, 0:1], scale=1.0 / 128)
        k.do("dve", "reciprocal", [ss], [ss], out=ss[:, :], in_=ss[:, :])
        for c in range(NCH):
            k.do("dve", "scalar_tensor_tensor", [oall, ss, g_t], [res], out=res[:, c, :], in0=oall[:, c, 0:128], scalar=ss[:, c:c + 1],
                 in1=g_t[:, :], op0=ALU.mult, op1=ALU.mult)
        k.do("dve", "tensor_tensor", [res, gt], [resb], out=resb[:, :, :], in0=res[:, :, :], in1=gt[:, :, :], op=ALU.mult)
        mch = j if is_ret else 4 + j
        for c in range(NCH):
            pb = pbf[(c // 8) % 2]
            k.op("pe", (lambda pb=pb, c=c: (lambda e: e.transpose(pb[:, (c % 8) * 128:(c % 8 + 1) * 128], resb[:, c, :], ident[:, :])))(),
                 reads=[resb, ident], writes=[pb])
            if c % 8 == 7:
                o_ = ost.get()
                k.do("act", "copy", [pb], [o_], out=o_[:, :], in_=pb[:, 0:1024])
                k.dma("sp", D["mixT"][mch][:, 2 + (c - 7) * 128:2 + (c + 1) * 128], o_[:, :], reads=[o_])


def phase_TP(k, ctx, D, layer):
    NT = 1024
    W = NT + 2
    T = T_SEQ
    ctx.scratch()
    k.banks_init(8)
    xsrc = D["xT"] if layer == 0 else D["h1T"]
    msrc = D["mixT"] if layer == 0 else D["oT"]
    wsrc = D["w"]
    h = k.sb([128, 8, W], F32, name="h")
    mixb = k.sb([128, 8, W], BF16, name="mixb")
    xn = k.sb([128, 8, W], BF16, name="xn")
    big = k.sb([128, NJ * NT], BF16, name="big")
    g = V(big, big.ap[:, :].rearrange("p (j n) -> p j n", j=NJ))
    e = V(big, big.ap[:, 0:2 * 8 * W].bitcast(F32).rearrange("p (c w) -> p c w", c=8))
    pst = V(big, big.ap[:, 2 * 8 * W:2 * 8 * W + 2 * 2 * NT].bitcast(F32).rearrange("p (c w) -> p c w", c=2))
    pb = k.sb([128, 2, NT], BF16, name="pb")
    gn = k.sb([128, 4, 8], F32, name="gn"); cw = k.sb([128, NJ, 4], F32, name="cw")
    wrot = Rot(k, 2, [128, 4096], BF16, "wr")
    zrot = Rot(k, 3, [128, 512], F32, "zr")
    a_sb = Rot(k, 2, [128, W], F32, "a_sb"); cc = Rot(k, 2, [128, NT], F32, "cc"); b_sb = Rot(k, 2, [128, NT], BF16, "b_sb")
    uur = Rot(k, 2, [128, NT], F32, "uu"); sgr = Rot(k, 2, [128, NT], F32, "sg")
    k.dma("sp", gn[:, :, :], D["gains"][layer], writes=[gn]); k.dma("sp", cw[:, :, :], D["cwin"][layer], writes=[cw])
    gviews = [V(gn, gn.ap[:, i, :]) for i in range(4)]
    if layer == 0:
        zer = k.sb([128, 16], F32, name="zer")
        k.do("pool", "memset", [], [zer], zer[:, :], 0.0)
        k.dma("pool", D["h1T"][:, :, 0:2], zer.ap[:, 0:16].rearrange("p (c w) -> p c w", c=8), reads=[zer])

    split = (layer == 1)
    if split:
        sel = k.sb([128, 2], F32, name="sel")
        k.dma("sp", sel[:, :], D["sel"], writes=[sel])
    for ps in range(2 if split else T // NT):
        t0 = ps * NT
        for kc in range(8):
            k.dma("sp", h[:, kc, :], xsrc[:, kc, t0:t0 + W], writes=[h])
            k.dma("sp" if split else "pool", mixb[:, kc, :], msrc[kc][:, t0:t0 + W], writes=[mixb])
        if split:
            tB = T // 2 + t0
            k.dma("sp", e[:, :, :], xsrc[:, :, tB:tB + W], writes=[e])
            k.dma("sp", xn[:, :, :], msrc[:, :, tB:tB + W].rearrange("c p w -> p c w"), writes=[xn])
            for kc in range(8):
                k.do("dve", "tensor_scalar", [h, sel], [h], out=h[:, kc, :], in0=h[:, kc, :], scalar1=sel[:, 0:1], scalar2=None, op0=ALU.mult)
                k.do("dve", "scalar_tensor_tensor", [e, sel, h], [h], out=h[:, kc, :], in0=e[:, kc, :], scalar=sel[:, 1:2], in1=h[:, kc, :],
                     op0=ALU.mult, op1=ALU.add)
                k.do("dve", "tensor_scalar", [mixb, sel], [mixb], out=mixb[:, kc, :], in0=mixb[:, kc, :], scalar1=sel[:, 0:1], scalar2=None, op0=ALU.mult)
                k.do("dve", "scalar_tensor_tensor", [xn, sel, mixb], [mixb], out=mixb[:, kc, :], in0=xn[:, kc, :], scalar=sel[:, 1:2], in1=mixb[:, kc, :],
                     op0=ALU.mult, op1=ALU.add)
        for c in range(8):
            wt = wrot.get()
            k.dma("sp", wt[:, :1024], wsrc("w_out%d" % layer)(c), writes=[wt])
            for (s0, s1) in slices512(0, W):
                bk = k.bank()
                for kc in range(8):
                    k.mm(bk, bk[:, :s1 - s0], wt, wt[:, kc * 128:(kc + 1) * 128], mixb, mixb[:, kc, s0:s1], kc == 0, kc == 7)
                k.do("dve", "tensor_tensor", [h, bk], [h], out=h[:, c, s0:s1], in0=h[:, c, s0:s1], in1=bk[:, :s1 - s0], op=ALU.add)
        ctx.rmsnorm(h, 8, 0, W, gviews[0], xn, 0)
        for j in range(NJ):
            wt = wrot.get()
            k.dma("sp", wt[:, :2048], wsrc("w_up%d" % layer)(j), writes=[wt])
            a = a_sb.get(); bb = b_sb.get()
            for (s0, s1) in slices512(0, W):
                bk = k.bank()
                for kc in range(8):
                    k.mm(bk, bk[:, :s1 - s0], wt, wt[:, kc * 256:kc * 256 + 128], xn, xn[:, kc, s0:s1], kc == 0, kc == 7)
                k.do("act", "copy", [bk], [a], out=a[:, s0:s1], in_=bk[:, :s1 - s0])
            for (s0, s1) in slices512(2, W):
                bk = k.bank()
                for kc in range(8):
                    k.mm(bk, bk[:, :s1 - s0], wt, wt[:, kc * 256 + 128:kc * 256 + 256], xn, xn[:, kc, s0:s1], kc == 0, kc == 7)
                k.do("act", "copy", [bk], [bb], out=bb[:, s0 - 2:s1 - 2], in_=bk[:, :s1 - s0])
            c_ = cc.get(); u_ = uur.get(); sg_ = sgr.get()
            k.do("pool", "tensor_scalar", [a, cw], [c_], out=c_[:, :], in0=a[:, 2:W], scalar1=cw[:, j, 2:3], scalar2=cw[:, j, 3:4],
                 op0=ALU.mult, op1=ALU.add)
            k.do("dve", "scalar_tensor_tensor", [a, cw, c_], [c_], out=c_[:, :], in0=a[:, 1:W - 1], scalar=cw[:, j, 1:2], in1=c_[:, :],
                 op0=ALU.mult, op1=ALU.add)
            k.do("dve", "scalar_tensor_tensor", [a, cw, c_], [c_], out=c_[:, :], in0=a[:, 0:W - 2], scalar=cw[:, j, 0:1], in1=c_[:, :],
                 op0=ALU.mult, op1=ALU.add)
            k.do("act", "activation", [c_], [u_], out=u_[:, :], in_=c_[:, :], func=AF.Square, scale=math.sqrt(0.044715))
            k.do("dve", "scalar_tensor_tensor", [u_, c_], [u_], out=u_[:, :], in0=u_[:, :], scalar=1.0, in1=c_[:, :], op0=ALU.add, op1=ALU.mult)
            k.do("act", "activation", [u_], [sg_], out=sg_[:, :], in_=u_[:, :], func=AF.Sigmoid, scale=1.5957691216057308)
            k.do("dve", "tensor_tensor", [c_, sg_], [g], out=g[:, j, :], in0=c_[:, :], in1=sg_[:, :], op=ALU.mult)
            k.do("dve", "tensor_tensor", [g, bb], [g], out=g[:, j, :], in0=g[:, j, :], in1=bb[:, :], op=ALU.mult)
        for c in range(8):
            wt = wrot.get()
            k.dma("sp", wt[:, :NJ * 128], wsrc("w_dn%d" % layer)(c), writes=[wt])
            for (s0, s1) in slices512(0, NT):
                bk = k.bank()
                for j in range(NJ):
                    k.mm(bk, bk[:, :s1 - s0], wt, wt[:, j * 128:(j + 1) * 128], g, g[:, j, s0:s1], j == 0, j == NJ - 1)
                k.do("dve", "tensor_tensor", [h, bk], [h], out=h[:, c, 2 + s0:2 + s1], in0=h[:, c, 2 + s0:2 + s1], in1=bk[:, :s1 - s0], op=ALU.add)
        for kc in range(2):
            k.dma("pool", pst[:, kc, :], (D["pT1h"][:, kc, t0:t0 + NT] if split else D["pT"][layer][:, kc, t0:t0 + NT]), writes=[pst])
        for kc in range(2):
            k.do("act", "copy", [pst], [pb], out=pb[:, kc, :], in_=pst[:, kc, :])
        for c in range(8):
            wt = wrot.get()
            k.dma("sp", wt[:, :256], wsrc("w_ple%d" % layer)(c), writes=[wt])
            for (s0, s1) in slices512(0, NT):
                bk = k.bank()
                for kc in range(2):
                    k.mm(bk, bk[:, :s1 - s0], wt, wt[:, kc * 128:(kc + 1) * 128], pb, pb[:, kc, s0:s1], kc == 0, kc == 1)
                k.do("act", "copy", [bk], [e], out=e[:, c, s0:s1], in_=bk[:, :s1 - s0])
        ctx.rmsnorm(e, 8, 0, NT, gviews[1], e, 0)
        ctx.rmsnorm(h, 8, 2, W, gviews[2], xn, 2)
        for c in range(8):
            wt = wrot.get()
            k.dma("sp", wt[:, :1024], wsrc("w_gate%d" % layer)(c), writes=[wt])
            for (s0, s1) in slices512(2, W):
                bk = k.bank()
                for kc in range(8):
                    k.mm(bk, bk[:, :s1 - s0], wt, wt[:, kc * 128:(kc + 1) * 128], xn, xn[:, kc, s0:s1], kc == 0, kc == 7)
                t_ = ctx.f32tmp.get()
                k.do("act", "activation", [bk], [t_], out=t_[:, :s1 - s0], in_=bk[:, :s1 - s0], func=AF.Sigmoid)
                k.do("dve", "tensor_tensor", [t_, e], [t_], out=t_[:, :s1 - s0], in0=t_[:, :s1 - s0], in1=e[:, c, s0 - 2:s1 - 2], op=ALU.mult)
                k.do("dve", "tensor_tensor", [h, t_], [h], out=h[:, c, s0:s1], in0=h[:, c, s0:s1], in1=t_[:, :s1 - s0], op=ALU.add)
        if layer == 0:
            for kc in range(8):
                k.dma("pool", D["h1T"][:, kc, 2 + t0:2 + t0 + NT], h[:, kc, 2:W], reads=[h])
            ctx.rmsnorm(h, 8, 2, W, gviews[3], xn, 2)
            tokmajor_proj(k, xn, 8, NT, 2, wsrc("n_tm"), 512, 1584, D["z1tm"][t0:t0 + NT, :], wrot, zrot)
            featmajor_proj(k, xn, 8, NT, 2, wsrc("n_fm"), 2, (lambda c, t0=t0: D["c2T"][c][:, t0:t0 + NT]), wrot, zrot)
        else:
            for kc in range(8):
                k.dma("pool", D["out"][:, kc, t0:t0 + NT], h[:, kc, 2:W], reads=[h], is_out=True)


def phase_P5(k, ctx, D, g):
    T = T_SEQ
    NQ = T // 128
    def kv4(i):
        return D["z1tm"][:, 1024 + i * 128 + g * 64:1024 + i * 128 + (g + 1) * 64].rearrange("(c p) d -> p c d", p=128)
    if g == 0:
        zer = k.sb([128, 8], BF16, name="zerb")
        k.do("pool", "memset", [], [zer], zer[:, :], 0.0)
        for c in range(8):
            k.dma("pool", D["oT"][c][:, 0:2], zer[:, 0:2], reads=[zer])
    SC = 0.125
    k.banks_init(4)
    O = [k.allbanks[4], k.allbanks[5]]
    IMP = k.allbanks[6]
    pbf = k.bank_bf(7)
    ident = k.sb([128, 128], BF16, name="ident"); tri = k.sb([128, 2, 128], BF16, name="tri")
    OV = k.sb([128, 2, 64], BF16, name="OV"); iota = k.sb([128, 128], F32, name="iota"); thr = k.sb([128, 2 * NQ], F32, name="thr")
    qg = k.sb([128, 64], F32, name="qg"); kg = k.sb([128, 3, 64], F32, name="kg")
    eps = ctx.eps
    k.dma("sp", ident[:, :], D["ident_in"], writes=[ident])
    for i in range(2):
        k.dma("sp", tri[:, i, :], D["tri_in"][i], writes=[tri])
    k.dma("sp", OV[:, :, :], D["OV_in"], writes=[OV]); k.dma("sp", iota[:, :], D["iota_in"], writes=[iota]); k.dma("sp", thr[:, :], D["thr_in"], writes=[thr])
    k.dma("sp", qg[:, :], D["qg_in"], writes=[qg]); k.dma("sp", kg[:, :, :], D["kg_in"], writes=[kg])
    KE = k.sb([128, T], BF16, name="KE"); KwT = k.sb([64, T], BF16, name="KwT")
    k.dma("pool", KE[64:128, :], D["E_in"], writes=[KE])
    VsA = k.sb([128, NQ, 65], BF16, name="VsA"); VwA = k.sb([128, NQ, 65], BF16, name="VwA")
    KcT = k.sb([64, 256], BF16, name="KcT"); VcA = k.sb([128, 2, 65], BF16, name="VcA")
    raw = k.sb([128, NQ, 64], F32, name="raw"); sqr = k.sb([128, NQ, 64], F32, name="sqr")
    ss = k.sb([128, NQ], F32, name="ss"); kn = k.sb([128, NQ, 64], BF16, name="kn")
    for idx, (src_i, dstT, gi_) in enumerate(((0, KE, 1), (2, KwT, 2))):
        k.dma("sp", raw[:, :, :], kv4(src_i), writes=[raw])
        k.do("dve", "tensor_tensor", [raw], [sqr], out=sqr[:, :, :], in0=raw[:, :, :], in1=raw[:, :, :], op=ALU.mult)
        k.do("dve", "tensor_reduce", [sqr], [ss], out=ss[:, :], in_=sqr[:, :, :], axis=AX.X, op=ALU.add)
        k.do("act", "activation", [ss, eps], [ss], out=ss[:, :], in_=ss[:, :], func=AF.Sqrt, bias=eps[:, 0:1], scale=1.0 / 64)
        k.do("dve", "reciprocal", [ss], [ss], out=ss[:, :], in_=ss[:, :])
        for c in range(NQ):
            k.do("dve", "scalar_tensor_tensor", [raw, ss, kg], [kn], out=kn[:, c, :], in0=raw[:, c, :], scalar=ss[:, c:c + 1],
                 in1=kg[:, gi_, :], op0=ALU.mult, op1=ALU.mult)
        for c in range(NQ):
            k.op("pe", (lambda c=c: (lambda e: e.transpose(pbf[0:64, (c % 8) * 128:(c % 8 + 1) * 128], kn[:, c, :], ident[:, :])))(),
                 reads=[kn, ident], writes=[pbf])
            if c % 8 == 7:
                k.do("act", "copy", [pbf], [dstT], out=dstT[0:64, (c - 7) * 128:(c + 1) * 128], in_=pbf[0:64, 0:1024])
    for (src_i, dstV) in ((1, VsA), (3, VwA)):
        k.dma("sp", raw[:, :, :], kv4(src_i), writes=[raw])
        k.do("dve", "tensor_copy", [raw], [dstV], out=dstV[:, :, 0:64], in_=raw[:, :, :])
        k.do("pool", "memset", [], [dstV], dstV[:, :, 64:65], 1.0)
    x2 = k.sb([128, T + 16], F32, name="x2"); x2b = k.sb([128, T + 16], BF16, name="x2b")
    w1 = k.sb([128, 16 * 256], BF16, name="w1"); w2 = k.sb([128, 2 * 64], BF16, name="w2")
    posf = k.sb([128, 16], F32, name="posf"); posb = k.sb([128, 16], BF16, name="posb")
    c0 = k.sb([128, 2], F32, name="c0"); Hx = k.sb([128, 256], F32, name="Hx"); Hg = k.sb([128, 2, 256], BF16, name="Hg")
    uu = k.sb([128, 512], F32, name="uu"); sg_ = k.sb([128, 512], F32, name="sg")
    kc_f = k.sb([128, 64], F32, name="kc_f"); kc_s = k.sb([128, 64], F32, name="kc_s"); ss1 = k.sb([128, 1], F32, name="ss1")
    Mb = k.sb([128, 128], BF16, name="Mb")
    k.do("pool", "memset", [], [Mb], Mb[:, :], 0.0)
    k.do("pool", "memset", [], [VcA], VcA[:, :, :], 0.0)
    k.do("pool", "memset", [], [VcA], VcA[:, :, 64:65], 1.0)
    for ci in range(2):
        k.do("pool", "memset", [], [x2], x2[:, T - 16:T + 16], 0.0)
        k.dma("sp", x2[0:64, 0:T], D["c2T"][ci][g * 64:(g + 1) * 64, 0:T], writes=[x2]); k.dma("pool", x2[64:128, 0:T - 1], D["c2T"][ci][g * 64:(g + 1) * 64, 1:T], writes=[x2])
        k.dma("sp", w1[:, :], D["w"]("nsa_cmp_w1k" if ci == 0 else "nsa_cmp_w1v")(0), writes=[w1]); k.dma("sp", w2[:, :], D["w"]("nsa_cmp_w2k" if ci == 0 else "nsa_cmp_w2v")(0), writes=[w2]); k.dma("sp", posf[:, :], D["pos_in"][ci], writes=[posf])
        k.do("dve", "tensor_copy", [x2], [x2b], out=x2b[:, 0:2048], in_=x2[:, 0:2048])
        k.do("act", "copy", [x2], [x2b], out=x2b[:, 2048:T + 16], in_=x2[:, 2048:T + 16])
        k.do("dve", "tensor_copy", [posf], [posb], out=posb[:, :], in_=posf[:, :])
        for jc in range(2):
            bk = k.bank(); bk2 = k.bank()
            for m in range(16):
                rhs = x2b.ap[:, 2 * m:2 * m + 16 * 255].rearrange("p (n s) -> p n s", s=16)[:, :, 0]
                k.mm(bk, bk[:, 0:255], w1, w1[:, m * 256 + jc * 128:m * 256 + (jc + 1) * 128], x2b, rhs, m == 0, m == 15)
            for m in range(16):
                k.mm(bk2, bk2[:, 0:1], w1, w1[:, m * 256 + jc * 128:m * 256 + (jc + 1) * 128], posb, posb[:, m:m + 1], m == 0, m == 15)
            k.do("act", "copy", [bk2], [c0], out=c0[:, jc:jc + 1], in_=bk2[:, 0:1])
            k.do("act", "activation", [bk, c0], [Hx], out=Hx[:, 0:255], in_=bk[:, 0:255], func=AF.Identity, bias=c0[:, jc:jc + 1], scale=1.0)
            gelu_tanh(k, Hx, Hg[:, jc, 0:255], Hg, 255, uu, sg_)
        for nt, cnt in ((0, 128), (1, 127)):
            bk = k.bank()
            for jc in range(2):
                k.mm(bk, bk[0:cnt, 0:64], Hg, Hg[:, jc, nt * 128:nt * 128 + cnt], w2, w2[:, jc * 64:(jc + 1) * 64], jc == 0, jc == 1)
            if ci == 0:
                k.do("act", "copy", [bk], [kc_f], out=kc_f[0:cnt, :], in_=bk[0:cnt, 0:64])
                k.do("dve", "tensor_tensor", [kc_f], [kc_s], out=kc_s[0:cnt, :], in0=kc_f[0:cnt, :], in1=kc_f[0:cnt, :], op=ALU.mult)
                k.do("dve", "tensor_reduce", [kc_s], [ss1], out=ss1[0:cnt, :], in_=kc_s[0:cnt, :], axis=AX.X, op=ALU.add)
                k.do("act", "activation", [ss1, eps], [ss1], out=ss1[0:cnt, :], in_=ss1[0:cnt, :], func=AF.Sqrt, bias=eps[0:cnt, 0:1], scale=1.0 / 64)
                k.do("dve", "reciprocal", [ss1], [ss1], out=ss1[0:cnt, :], in_=ss1[0:cnt, :])
                k.do("dve", "scalar_tensor_tensor", [kc_f, ss1, kg], [kn], out=kn[0:cnt, 0, :], in0=kc_f[0:cnt, :], scalar=ss1[0:cnt, 0:1],
                     in1=kg[0:cnt, 0, :], op0=ALU.mult, op1=ALU.mult)
                k.op("pe", (lambda cnt=cnt: (lambda e: e.transpose(pbf[0:64, 0:cnt], kn[0:cnt, 0, :], ident[0:cnt, 0:cnt])))(),
                     reads=[kn, ident], writes=[pbf])
                k.do("act", "copy", [pbf], [KcT], out=KcT[0:64, nt * 128:nt * 128 + cnt], in_=pbf[0:64, 0:cnt])
            else:
                k.do("act", "copy", [bk], [VcA], out=VcA[0:cnt, nt, 0:64], in_=bk[0:cnt, 0:64])
    gts = k.sb([128, NQ, 24], F32, name="gts"); gbt = k.sb([128, NQ, 24], F32, name="gbt")
    k.dma("sp", gts[:, :, :], D["z1tm"][:, 1536 + g * 24:1536 + (g + 1) * 24].rearrange("(c p) d -> p c d", p=128), writes=[gts]); k.dma("sp", gbt[:, :, :], D["gb_in"][g], writes=[gbt])
    k.do("dve", "tensor_tensor", [gts, gbt], [gts], out=gts[:, :, :], in0=gts[:, :, :], in1=gbt[:, :, :], op=ALU.add)
    k.do("act", "activation", [gts], [gts], out=gts[:, :, :], in_=gts[:, :, :], func=AF.Sigmoid)
    LOOK = 2
    qblk = Rot(k, 2, [128, 512], F32, "qblk"); qsq = k.sb([128, 512], F32, name="qsq"); ss8 = k.sb([128, 8], F32, name="ss8")
    qn = Rot(k, 2, [128, 512], BF16, "qn"); QMr = Rot(k, 4, [128, 1024], BF16, "QM")
    oacc = Rot(k, 4, [128, 512], F32, "oacc"); Pt = Rot(k, LOOK + 3, [128, 512], BF16, "Pt")
    selc = Rot(k, 3, [128, 2, 64], F32, "selc")
    rc4 = Rot(k, 2, [128, 4], F32, "rc4"); rc8r = Rot(k, 3, [128, 8], F32, "rc8"); cf4 = Rot(k, 2, [128, 4], F32, "cf4")
    impt = k.sb([128, 64], F32, name="impt"); sc1 = k.sb([128, 64], F32, name="sc1"); sc2 = k.sb([128, 64], F32, name="sc2")
    m8a = k.sb([128, 8], F32, name="m8a"); m8b = k.sb([128, 8], F32, name="m8b")
    mkc = Rot(k, 4, [128, 128], BF16, "mkc"); ftmp = Rot(k, 3, [128, 256], F32, "ftmp"); imptmp = k.sb([128, 512], F32, name="imptmp")
    obf = Rot(k, 3, [128, 512], BF16, "obf"); ostage = Rot(k, 2, [128, 4, 1024], BF16, "ostage")
    osg_box = [None]
    Mbr = Rot(k, 3, [128, 128], BF16, "Mbr")
    for t_ in Mbr.t:
        k.do("pool", "memset", [], [t_], t_[:, :], 0.0)

    def prep(qb):
        st = dict(qb=qb, QM=QMr.get(), oa=oacc.get(), sl=selc.get(), rc8=rc8r.get())
        st["QMhi"] = type(st["QM"])(st["QM"].ap, "QMhi")
        qt = qblk.get(); qnt = qn.get(); QM = st["QM"]
        k.dma("sp", qt[:, :], D["z1tm"][qb * 128:(qb + 1) * 128, g * 512:(g + 1) * 512], writes=[qt])
        k.dma("sp", st["sl"][:, :, :], D["selc_in"][qb], writes=[st["sl"]])
        k.do("pool", "tensor_tensor", [qt], [qsq], out=qsq[:, :], in0=qt[:, :], in1=qt[:, :], op=ALU.mult)
        k.do("dve", "tensor_reduce", [qsq], [ss8], out=ss8[:, :], in_=qsq.ap[:, :].rearrange("p (h d) -> p h d", h=8), axis=AX.X, op=ALU.add)
        k.do("act", "activation", [ss8, eps], [ss8], out=ss8[:, :], in_=ss8[:, :], func=AF.Ln, bias=eps[:, 0:1], scale=1.0 / 64)
        k.do("act", "activation", [ss8], [ss8], out=ss8[:, :], in_=ss8[:, :], func=AF.Exp, scale=-0.5)
        k.do("dve", "tensor_tensor", [qt, ss8], [qsq], out=qsq.ap[:, :].rearrange("p (h d) -> p h d", h=8),
             in0=qt.ap[:, :].rearrange("p (h d) -> p h d", h=8), in1=ss8.ap[:, :].unsqueeze(2).to_broadcast([128, 8, 64]), op=ALU.mult)
        k.do("pool", "tensor_tensor", [qsq, qg], [qnt], out=qnt.ap[:, :].rearrange("p (h d) -> p h d", h=8),
             in0=qsq.ap[:, :].rearrange("p (h d) -> p h d", h=8), in1=qg.ap[:, :].unsqueeze(1).to_broadcast([128, 8, 64]), op=ALU.mult)
        st["qnt"] = qnt
        return st

    def prep_pe(st):
        qnt = st["qnt"]; QM = st["QM"]
        for h in range(8):
            k.op("pe", (lambda h=h, qnt=qnt: (lambda e: e.transpose(pbf[0:64, h * 128:(h + 1) * 128], qnt[:, h * 64:(h + 1) * 64], ident[:, :])))(),
                 reads=[qnt, ident], writes=[pbf])
        k.do("act", "copy", [pbf], [QM], out=QM[0:64, :], in_=pbf[0:64, 0:1024])

    def finalize(st, br, hh):
        oa = st["oa"]; qb = st["qb"]; rc8 = st["rc8"]
        Ob = O[hh]
        c4 = cf4.get()
        den = Ob.ap[:, 0:260].rearrange("p (h c) -> p h c", c=65)[:, :, 64]
        O4 = Ob.ap[:, 0:260].rearrange("p (h c) -> p h c", c=65)[:, :, 0:64]
        oa4 = oa.ap[:, hh * 256:(hh + 1) * 256].rearrange("p (h d) -> p h d", h=4)
        if br == 0:
            r4 = V(rc8, rc8.ap[:, hh * 4:(hh + 1) * 4])
            k.do("dve", "tensor_scalar", [Ob], [r4], out=r4[:, :], in0=den, scalar1=1e-30, scalar2=None, op0=ALU.max)
            k.do("dve", "reciprocal", [r4], [r4], out=r4[:, :], in_=r4[:, :])
        else:
            r4 = rc4.get()
            k.do("dve", "reciprocal", [Ob], [r4], out=r4[:, :], in_=den)
        gap = gts.ap[:, qb, :].rearrange("p (h b) -> p h b", b=3)[:, hh * 4:(hh + 1) * 4, br]
        k.do("dve", "tensor_tensor", [r4, gts], [c4], out=c4[:, :], in0=r4[:, :], in1=gap, op=ALU.mult)
        c4b = c4.ap[:, :].unsqueeze(2).to_broadcast([128, 4, 64])
        if br == 0:
            k.do("dve", "tensor_tensor", [Ob, c4], [oa], out=oa4, in0=O4, in1=c4b, op=ALU.mult)
        else:
            t4 = ftmp.get()
            t4v = t4.ap[:, :].rearrange("p (h d) -> p h d", h=4)
            k.do("dve", "tensor_tensor", [Ob, c4], [t4], out=t4v, in0=O4, in1=c4b, op=ALU.mult)
            k.do("pool", "tensor_tensor", [oa, t4], [oa], out=oa4, in0=oa4, in1=t4v, op=ALU.add)

    def select(st):
        rc8 = st["rc8"]; sl = st["sl"]
        Mb = Mbr.get(); st["Mb"] = Mb
        k.do("dve", "tensor_tensor", [IMP, rc8], [imptmp], out=imptmp.ap[:, :].rearrange("p (h j) -> p h j", h=8),
             in0=IMP.ap[:, :].rearrange("p (h j) -> p h j", h=8), in1=rc8.ap[:, :].unsqueeze(2).to_broadcast([128, 8, 64]), op=ALU.mult)
        k.do("dve", "tensor_reduce", [imptmp], [impt], out=impt[:, :], in_=imptmp.ap[:, :].rearrange("p (h j) -> p j h", h=8), axis=AX.X, op=ALU.add)
        k.do("dve", "tensor_tensor", [impt, sl], [sc1], out=sc1[:, :], in0=impt[:, :], in1=sl[:, 0, :], op=ALU.mult)
        k.do("dve", "tensor_tensor", [sc1, sl], [sc1], out=sc1[:, :], in0=sc1[:, :], in1=sl[:, 1, :], op=ALU.add)
        k.do("dve", "max", [sc1], [m8a], out=m8a[:, :], in_=sc1[:, :])
        k.do("dve", "tensor_scalar", [sc1, m8a], [sc2], out=sc2[:, :], in0=sc1[:, :], scalar1=m8a[:, 7:8], scalar2=-3e38, op0=ALU.is_ge, op1=ALU.mult)
        k.do("dve", "tensor_tensor", [sc2, sc1], [sc2], out=sc2[:, :], in0=sc2[:, :], in1=sc1[:, :], op=ALU.add)
        k.do("dve", "max", [sc2], [m8b], out=m8b[:, :], in_=sc2[:, :])
        k.do("dve", "tensor_scalar", [sc1, m8b], [Mb], out=Mb[:, 64:128], in0=sc1[:, :], scalar1=m8b[:, 7:8], scalar2=-1.0, op0=ALU.is_ge, op1=ALU.add)

    def select_pe(st):
        QM = st["QM"]; QMhi = st["QMhi"]; Mb = st["Mb"]
        k.op("pe", (lambda Mb=Mb: (lambda e: e.transpose(pbf[:, 0:128], Mb[:, :], ident[:, :])))(), reads=[Mb, ident], writes=[pbf])
        k.do("act", "copy", [pbf], [QMhi], out=QM.ap[64:128, :].rearrange("p (h q) -> p h q", h=8),
             in_=pbf.ap[64:128, 0:128].unsqueeze(1).to_broadcast([64, 8, 128]))

    pending_out = []

    def output_pre(st):
        ob_ = obf.get(); st["ob"] = ob_
        k.do("pool", "tensor_copy", [st["oa"]], [ob_], out=ob_[:, :], in_=st["oa"][:, :])
        pending_out.append(st)

    def flush_out():
        while pending_out:
            output_pe(pending_out.pop(0))

    def output_pe(st):
        qb = st["qb"]; ob_ = st["ob"]
        for c4_ in range(4):
            k.op("pe", (lambda c4_=c4_, ob_=ob_: (lambda e: e.transpose(pbf[:, c4_ * 128:(c4_ + 1) * 128], ob_[:, c4_ * 128:(c4_ + 1) * 128], ident[:, :])))(),
                 reads=[ob_, ident], writes=[pbf])
        if qb % 8 == 0:
            osg_box[0] = ostage.get()
        osg = osg_box[0]
        k.do("dve", "tensor_copy", [pbf], [osg], out=osg[:, :, (qb % 8) * 128:(qb % 8 + 1) * 128],
             in_=pbf.ap[:, 0:512].rearrange("p (c q) -> p c q", c=4))
        if qb % 8 == 7:
            for c4_ in range(4):
                k.dma("sp", D["oT"][g * 4 + c4_][:, 2 + (qb - 7) * 128:2 + (qb + 1) * 128], osg[:, c4_, :], reads=[osg])

    jobs = []

    def add_branch(st, br, tiles, pre_first=None, post_last=None):
        nt_ = len(tiles)
        for ti, tl in enumerate(tiles):
            for hh in range(2):
                jb = dict(st=st, br=br, tl=tl, hh=hh, first=(ti == 0), last=(ti == nt_ - 1), pre=[], post=[])
                if ti == 0 and hh == 0 and pre_first is not None:
                    jb["pre"].append(pre_first)
                if ti == nt_ - 1:
                    jb["post"].append((lambda st=st, br=br, hh=hh: finalize(sts[st], br, hh)))
                    if hh == 1 and post_last is not None:
                        jb["post"].append(post_last)
                jobs.append(jb)

    def cmp_tiles(qb):
        nk = min(8 * qb + 7, 255)
        ctiles = [(0, min(nk, 128))] + ([(1, nk - 128)] if nk > 128 else [])
        out_ = []
        for (nt, cnt) in ctiles:
            def mfn(qb=qb, nt=nt, cnt=cnt):
                mk = mkc.get()
                k.do("dve", "tensor_scalar", [iota, thr], [mk], out=mk[:, :], in0=iota[:, :], scalar1=thr[:, 2 * qb + nt:2 * qb + nt + 1], scalar2=None,
                     op0=ALU.is_ge)
                return mk, mk.ap[0:cnt, :]
            out_.append((KcT, KcT[0:64, nt * 128:nt * 128 + cnt], 64, cnt, VcA[0:cnt, nt, :], VcA, mfn, OV[0:cnt, nt, :]))
        return out_

    def win_tiles(qb):
        out_ = []
        for kt_ in range(max(0, qb - 4), qb + 1):
            mi_ = 0 if kt_ == qb else (1 if kt_ == qb - 4 else None)
            mfn = (lambda mi_=mi_: (tri, tri.ap[:, mi_, :])) if mi_ is not None else None
            out_.append((KwT, KwT[0:64, kt_ * 128:(kt_ + 1) * 128], 64, 128, VwA[:, kt_, :], VwA, mfn, None))
        return out_

    def slc_tiles(qb):
        out_ = []
        for kt_ in range(qb + 1):
            mfn = (lambda: (tri, tri.ap[:, 0, :])) if kt_ == qb else None
            out_.append((KE, KE[:, kt_ * 128:(kt_ + 1) * 128], 128, 128, VsA[:, kt_, :], VsA, mfn, None))
        return out_

    sts = {}

    def mk_prep(qb):
        def f():
            sts[qb] = prep(qb)
        return f

    order = []
    for qb in range(NQ):
        if qb == 0:
            order.append(("cmp", 0))
        if qb + 1 < NQ:
            order.append(("cmp", qb + 1))
        order.append(("win", qb)); order.append(("slc", qb))
    def pre_cmp(q):
        def f():
            if q == 0:
                sts[0] = prep(0); prep_pe(sts[0])
                if NQ > 1:
                    sts[1] = prep(1); prep_pe(sts[1])
            if q >= 1 and q + 1 < NQ:
                sts[q + 1] = prep(q + 1)
        return f

    def pre_win(q):
        def f():
            select_pe(sts[q])
            flush_out()
        return f

    def pre_slc(q):
        def f():
            if q + 2 < NQ and q + 2 >= 2 and (q + 2) in sts:
                prep_pe(sts[q + 2])
        return f
    for kind, qb in order:
        if kind == "cmp":
            add_branch(qb, 0, cmp_tiles(qb), pre_first=pre_cmp(qb), post_last=(lambda qb=qb: select(sts[qb])))
        elif kind == "win":
            add_branch(qb, 2, win_tiles(qb), pre_first=pre_win(qb))
        else:
            add_branch(qb, 1, slc_tiles(qb), pre_first=pre_slc(qb), post_last=(lambda qb=qb: output_pre(sts[qb])))

    live = {}
    for n in range(len(jobs) + LOOK):
        if n < len(jobs):
            jb = jobs[n]
            for f in jb["pre"]:
                f()
            st = sts[jb["st"]]
            (Kt, Kap, Kp, cnt, Vap, Vt, mfn, Xap) = jb["tl"]
            hh = jb["hh"]; QM = st["QM"]
            bk = k.bank()
            rds = [Kt, QM] + ([st["QMhi"]] if Kp == 128 else [])
            k.op("pe", (lambda bk=bk, Kap=Kap, QM=QM, Kp=Kp, hh=hh, cnt=cnt:
                        (lambda e: e.matmul(bk[0:cnt, 0:512], lhsT=Kap, rhs=QM[0:Kp, hh * 512:(hh + 1) * 512], start=True, stop=True)))(),
                 reads=rds, writes=[bk])
            pt = Pt.get()
            k.do("act", "activation", [bk], [pt], out=pt[0:cnt, :], in_=bk[0:cnt, 0:512], func=AF.Exp, scale=SC)
            if mfn is not None:
                Mt, Map = mfn()
                k.do("dve", "tensor_tensor", [pt, Mt], [pt], out=pt.ap[0:cnt, :].rearrange("p (h q) -> p h q", h=4),
                     in0=pt.ap[0:cnt, :].rearrange("p (h q) -> p h q", h=4),
                     in1=Map.unsqueeze(1).to_broadcast([cnt, 4, 128]), op=ALU.mult)
            live[n] = pt
        m = n - LOOK
        if m >= 0:
            jb = jobs[m]
            (Kt, Kap, Kp, cnt, Vap, Vt, mfn, Xap) = jb["tl"]
            hh = jb["hh"]; ppt = live.pop(m)
            for h in range(4):
                k.mm(O[hh], O[hh][:, h * 65:(h + 1) * 65], ppt, ppt[0:cnt, h * 128:(h + 1) * 128], Vt, Vap,
                     jb["first"] and h == 0, jb["last"], skip=True)
                if Xap is not None:
                    k.mm(IMP, IMP[:, (hh * 4 + h) * 64:(hh * 4 + h + 1) * 64], ppt, ppt[0:cnt, h * 128:(h + 1) * 128], OV, Xap,
                         jb["first"] and h == 0 and hh == 0, jb["last"], skip=True)
            for f in jb["post"]:
                f()
    flush_out()

def build_fused(M, wtab):
    nc = bass.Bass("TRN2", target_bir_lowering=False)
    T = T_SEQ
    NQ = T // 128
    di = lambda n, s, dt=F32: nc.dram_tensor(n, s, dt, kind="ExternalInput").ap()
    dsc = lambda n, s, dt=F32: nc.dram_tensor(n, s, dt).ap()
    D = {}
    wsrc_in = di("wsrc", [128, M])
    D["xT"] = di("xT", [128, 8, T + 2]); D["pT"] = di("pT", [2, 128, 2, T])
    D["g_ab"] = di("g_ab", [128, 8]); D["gains"] = di("gains", [2, 128, 4, 8]); D["cwin"] = di("cwin", [2, 128, NJ, 4])
    D["ctab"] = di("ctab", [64, T]); D["stab"] = di("stab", [64, T])
    D["r_A"] = di("r_A", [4, 128, NCH]); D["r_B"] = di("r_B", [4, 128, NCH]); D["r_c"] = di("r_c", [4, 64, NCH])
    D["m_cw"] = di("m_cw", [4, 2, 64, 5]); D["m_gb"] = di("m_gb", [4, 128, 2]); D["ng"] = di("ng", [8, 128, 128])
    D["mask_in"] = di("mask_in", [128, 128]); D["ident_in"] = di("ident_in", [128, 128], BF16)
    D["gb_in"] = di("gb_in", [2, 128, NQ, 24]); D["qg_in"] = di("qg_in", [128, 64]); D["kg_in"] = di("kg_in", [128, 3, 64])
    D["pos_in"] = di("pos_in", [2, 128, 16]); D["tri_in"] = di("tri_in", [2, 128, 128], BF16)
    D["E_in"] = di("E_in", [64, T], BF16); D["OV_in"] = di("OV_in", [128, 2, 64], BF16)
    D["iota_in"] = di("iota_in", [128, 128]); D["thr_in"] = di("thr_in", [128, 2 * NQ]); D["selc_in"] = di("selc_in", [NQ, 128, 2, 64])
    D["out"] = nc.dram_tensor("out", [128, 8, T // 2], F32, kind="ExternalOutput").ap()
    D["sel"] = di("sel", [128, 2]); D["pT1h"] = di("pT1h", [128, 2, T // 2])
    wbf = dsc("wbf", [128, M], BF16)
    D["zA_fm"] = dsc("zA_fm", [12, 128, 3 + T]); D["zA_tm"] = dsc("zA_tm", [T, 2056])
    D["mixT"] = dsc("mixT", [8, 128, 2 + T], BF16); D["h1T"] = dsc("h1T", [128, 8, 2 + T])
    D["z1tm"] = dsc("z1tm", [T, 1584]); D["c2T"] = dsc("c2T", [2, 128, T + 16]); D["oT"] = dsc("oT", [8, 128, 2 + T], BF16)

    def wfn(name):
        off, X, NB = wtab[name]
        return lambda b: wbf[:, off + b * X:off + (b + 1) * X]
    D["w"] = wfn
    with ExitStack() as es:
        k = K(nc, es)
        k.arena_init(ARENA_KIB)
        k.banks_init(8)
        ctx = TPCtx(k)
        k.persist()
        CB = 4096
        st = Rot(k, 3, [128, CB], F32, "st"); ob = Rot(k, 3, [128, CB], BF16, "ob")
        engs = ["dve", "act", "pool"]
        for bi, c0 in enumerate(range(0, M, CB)):
            a = st.get(); b = ob.get()
            k.dma("sp", a[:, :], wsrc_in[:, c0:c0 + CB], writes=[a])
            e = engs[bi % 3]
            if e == "act":
                k.do("act", "copy", [a], [b], out=b[:, :], in_=a[:, :])
            else:
                k.do(e, "tensor_copy", [a], [b], out=b[:, :], in_=a[:, :])
            k.dma("pool" if e != "pool" else "sp", wbf[:, c0:c0 + CB], b[:, :], reads=[b])
        k.phase_reset()
        phase_A(k, ctx, D)
        k.phase_reset()
        phase_P2(k, ctx, D)
        k.phase_reset()
        phase_TP(k, ctx, D, 0)
        for g in range(2):
            k.phase_reset()
            phase_P5(k, ctx, D, g)
        k.phase_reset()
        phase_TP(k, ctx, D, 1)
        k.emit()
    return nc


ARENA_KIB = 190


def p2_consts():
    T = T_SEQ
    half = 32
    inv = 10000.0 ** (-np.arange(half, dtype=np.float64) / half)
    ang = np.arange(T, dtype=np.float64)[None, :] * inv[:, None]
    cos = np.cos(ang); sin = np.sin(ang)
    ctab = np.concatenate([cos, cos], 0).astype(np.float32)
    stab = np.concatenate([-sin, sin], 0).astype(np.float32)
    l = np.arange(128, dtype=np.float64)
    rA = np.zeros((4, 128, NCH), np.float32); rB = np.zeros((4, 128, NCH), np.float32); rc = np.zeros((4, 64, NCH), np.float32)
    for h in range(4):
        lg = np.log1p(-2.0 ** (-5.0 - h))
        rA[h] = np.exp((l + 1) * lg)[:, None]
        rB[h] = (np.exp(-(l + 1) * lg) * 0.125)[:, None]
        rc[h] = np.exp(128 * lg)
    mask = (l[:, None] <= l[None, :]).astype(np.float32)
    ident = np.eye(128, dtype=np.float32).astype(NPBF)
    return dict(ctab=ctab, stab=stab, r_A=rA, r_B=rB, r_c=rc, mask_in=mask, ident_in=ident)


def p5_consts():
    T = T_SEQ; NQ = T // 128
    l = np.arange(128)
    tri = (l[:, None] <= l[None, :]).astype(np.float32)
    triU = 1.0 - tri
    E = (np.arange(T)[None, :] // 64 == np.arange(64)[:, None]).astype(np.float32)
    n = np.arange(256); j = np.arange(64)
    ov = ((16 * n[:, None] < (j[None, :] + 1) * 64) & (16 * n[:, None] + 32 > j[None, :] * 64)).astype(np.float32)
    ov[255] = 0
    OV = np.ascontiguousarray(ov.reshape(2, 128, 64).transpose(1, 0, 2))
    iota = np.tile(l[None, :].astype(np.float32), (128, 1))
    thr = np.zeros((128, 2 * NQ), np.float32)
    for qb in range(NQ):
        for nt in range(2):
            thr[:, 2 * qb + nt] = 16 * (nt * 128 + l) + 31 - 128 * qb
    selc = np.zeros((NQ, 128, 2, 64), np.float32)
    for qb in range(NQ):
        for q in range(128):
            cur = (128 * qb + q) // 64
            mult = np.ones(64, np.float32); add = np.zeros(64, np.float32)
            mult[0] = 0; add[0] = 1e30
            if cur >= 1:
                mult[cur - 1] = 0; add[cur - 1] = 3e30
            mult[cur] = 0; add[cur] = 2e30
            mult[cur + 1:] = 0; add[cur + 1:] = -1e30
            selc[qb, q, 0] = mult; selc[qb, q, 1] = add
    return dict(tri_in=np.stack([tri, triU]).astype(NPBF), E_in=(E * 30000.0).astype(NPBF), OV_in=OV.astype(NPBF), iota_in=iota, thr_in=thr, selc_in=selc)


def kernel(**inp):
    inp = {k_: np.asarray(v) for k_, v in inp.items()}
    x = inp["x"].astype(np.float32); p = inp["p"].astype(np.float32)
    T = T_SEQ; NQ = T // 128
    wt = wlayout(inp)
    wflat, M = wt.flat()
    nc = build_fused(M, wt.tab)
    shared = dict(p2_consts()); shared.update(p5_consts())
    shared["wsrc"] = wflat
    shared["g_ab"] = gain_fm(inp["ab_norm_g"][0])
    gl = []
    for l in range(2):
        nxt = inp["nsa_norm_g"][0] if l == 0 else inp["ple_gate_norm_g"][1]
        gl.append(np.stack([gain_fm(inp["ffn_norm_g"][l]), gain_fm(inp["ple_norm_g"][l]), gain_fm(inp["ple_gate_norm_g"][l]), gain_fm(nxt)], 1))
    shared["gains"] = np.ascontiguousarray(np.stack(gl))
    cwl = []
    for l in range(2):
        cwm = np.concatenate([inp["ffn_conv_w"][l], inp["ffn_conv_b"][l][None]], 0)
        cwl.append(np.ascontiguousarray(cwm.T.reshape(NJ, 128, 4).transpose(1, 0, 2)))
    shared["cwin"] = np.stack(cwl).astype(np.float32)
    conv_w = inp["ab_conv_w"][0]; conv_b = inp["ab_conv_b"][0]
    mcw = []
    for h in range(4):
        qw = np.concatenate([conv_w[:, h * 64:(h + 1) * 64].T, conv_b[h * 64:(h + 1) * 64, None]], 1)
        kw = np.concatenate([conv_w[:, 256 + h * 64:256 + (h + 1) * 64].T, conv_b[256 + h * 64:256 + (h + 1) * 64, None]], 1)
        mcw.append(np.stack([qw, kw]))
    shared["m_cw"] = np.stack(mcw).astype(np.float32)
    shared["m_gb"] = np.stack([np.tile(np.array([[inp["ab_ig_b"][0][h], inp["ab_fg_b"][0][h]]], np.float32), (128, 1)) for h in range(4)])
    shared["ng"] = np.stack([np.tile(inp["ab_ret_norm_g"][0][h * 128:(h + 1) * 128][None], (128, 1)) for h in range(4)] +
                            [np.tile(inp["ab_m_norm_g"][0][h * 128:(h + 1) * 128][None], (128, 1)) for h in range(4)]).astype(np.float32)
    shared["gb_in"] = np.stack([np.broadcast_to(inp["nsa_gate_b"][0][g * 24:(g + 1) * 24][None, None, :], (128, NQ, 24)) for g in range(2)]).astype(np.float32)
    shared["qg_in"] = np.broadcast_to(inp["nsa_q_norm_g"][0][None, :], (128, 64)).astype(np.float32)
    shared["kg_in"] = np.broadcast_to(inp["nsa_k_norm_g"][0][None], (128, 3, 64)).astype(np.float32)
    def pos2(pos):
        return np.ascontiguousarray(pos.reshape(16, 2, 64).transpose(1, 2, 0).reshape(128, 16)).astype(np.float32)
    shared["pos_in"] = np.stack([pos2(inp["nsa_cmp_pos_k"][0]), pos2(inp["nsa_cmp_pos_v"][0])])
    shared = {k_: np.ascontiguousarray(v) for k_, v in shared.items()}
    maps = []
    for c in range(8):
        b = c // 2
        d = dict(shared)
        xt = np.zeros((128, 8, T + 2), np.float32); xt[:, :, 2:] = fm(x[b])
        d["xT"] = xt
        d["pT"] = np.stack([fm(p[l, b]) for l in range(2)])
        hf = c % 2
        d["pT1h"] = fm(p[1, b, hf * (T // 2):(hf + 1) * (T // 2)])
        sel = np.zeros((128, 2), np.float32); sel[:, hf] = 1.0
        d["sel"] = sel
        maps.append(d)
    res = run_spmd(nc, maps)
    out = np.stack([np.concatenate([unfm(np.asarray(res[2 * b]["out"])), unfm(np.asarray(res[2 * b + 1]["out"]))], 0) for b in range(4)])
    return out.astype(np.float32)
```
